# Optimizing a Trainium2 kernel written in Bass

```python
import jax, jax.numpy as jnp
from jax import lax
import numpy as np

D_MODEL = 1024
BATCH = 4
SEQ = 4096
DEPTH = 2
DEC_BATCH = 128
DEC_SEQ = 8
PAST_LEN = 8192
PAGE_SIZE = 128

N_EVEN = (DEPTH + 1) // 2
N_ODD = DEPTH // 2
HEAD_DIM = 64
A_WIDTH = D_MODEL // 2
B_WIDTH = D_MODEL // 2
C_WIDTH = D_MODEL // 2
D_WIDTH = D_MODEL // 2
CONV_A = 3
CONV_B = 31
POOL_WINDOWS = (2, 4, 8, 16)
POOL_GROUPS = len(POOL_WINDOWS)
POOL_GROUP_W = C_WIDTH // POOL_GROUPS
POOL_PAST = max(POOL_WINDOWS) - 1
WINDOW = 128
D_HEADS = D_WIDTH // HEAD_DIM
D_KV_HEADS = 2
D_REP = D_HEADS // D_KV_HEADS
KV_WIDTH = D_KV_HEADS * HEAD_DIM
EVEN_SPLITS = (A_WIDTH, A_WIDTH, A_WIDTH, A_WIDTH, B_WIDTH, B_WIDTH, B_WIDTH)
ODD_SPLITS = (C_WIDTH, C_WIDTH, D_WIDTH, KV_WIDTH, KV_WIDTH, D_WIDTH)
EVEN_IN = sum(EVEN_SPLITS)
ODD_IN = sum(ODD_SPLITS)
MIX_EVEN = A_WIDTH + B_WIDTH
MIX_ODD = C_WIDTH + D_WIDTH
RMS_EPS = 1e-6
LN_EPS = 1e-5

kernel_name = "hybrid_conv_pool_swa_decoder_step"


def split_cols(z, sizes):
    idx = [int(i) for i in np.cumsum(sizes)[:-1]]
    return jnp.split(z, idx, axis=-1)


def rms_norm(x, g):
    x32 = x.astype(jnp.float32)
    y = x32 * lax.rsqrt(jnp.mean(x32 * x32, axis=-1, keepdims=True) + RMS_EPS)
    return (y * g.astype(jnp.float32)).astype(x.dtype)


def layer_norm(x, g, b):
    x32 = x.astype(jnp.float32)
    mu = jnp.mean(x32, axis=-1, keepdims=True)
    xc = x32 - mu
    var = jnp.mean(xc * xc, axis=-1, keepdims=True)
    y = xc * lax.rsqrt(var + LN_EPS) * g.astype(jnp.float32) + b.astype(jnp.float32)
    return y.astype(x.dtype)


def adaln(c, w_mod, b_mod):
    mod = jax.nn.silu(c) @ w_mod + b_mod
    return jnp.split(mod[:, None, :], 3, axis=-1)


def causal_dwconv(u, past, w):
    full = jnp.concatenate([past.astype(u.dtype), u], axis=1)
    y = lax.conv_general_dilated(full, w[:, None, :].astype(u.dtype), window_strides=(1,), padding="VALID",
                                 dimension_numbers=("NWC", "WIO", "NWC"), feature_group_count=u.shape[-1])
    return y, full[:, -(w.shape[0] - 1):]


def multiscale_pool(u, past, pool_w, pool_scale, pos0):
    n, t, ch = u.shape
    full = jnp.concatenate([past.astype(u.dtype), u], axis=1)
    csum = jnp.cumsum(full.astype(jnp.float32), axis=1)
    csum = jnp.concatenate([jnp.zeros((n, 1, ch), jnp.float32), csum], axis=1)
    end = csum[:, POOL_PAST + 1:]
    pos = pos0 + jnp.arange(t)
    u32 = u.astype(jnp.float32)
    outs = []
    for g, w in enumerate(POOL_WINDOWS):
        sl = slice(g * POOL_GROUP_W, (g + 1) * POOL_GROUP_W)
        start = csum[:, POOL_PAST + 1 - w:POOL_PAST + 1 - w + t, sl]
        cnt = jnp.minimum(w, pos + 1).astype(jnp.float32)[None, :, None]
        outs.append((end[..., sl] - start) / cnt - u32[..., sl])
    pooled = jnp.stack(outs, axis=2)
    mixed = jnp.einsum("ntgc,gcd->ntgd", pooled, pool_w.astype(jnp.float32)).reshape(n, t, ch)
    return (mixed * pool_scale.astype(jnp.float32)).astype(u.dtype), full[:, -POOL_PAST:]


def window_attention(q, k, v, past_k, past_v, sinks, pos0):
    n, t, _, hd = q.shape
    nb = -(-t // WINDOW)
    tp = nb * WINDOW
    kf = jnp.concatenate([past_k.astype(k.dtype), k], axis=1)
    vf = jnp.concatenate([past_v.astype(v.dtype), v], axis=1)
    new_k, new_v = kf[:, -WINDOW:], vf[:, -WINDOW:]
    padk = ((0, 0), (0, tp - t), (0, 0), (0, 0))
    kb = jnp.pad(kf, padk).reshape(n, nb + 1, WINDOW, D_KV_HEADS, hd)
    vb = jnp.pad(vf, padk).reshape(n, nb + 1, WINDOW, D_KV_HEADS, hd)
    kband = jnp.concatenate([kb[:, :-1], kb[:, 1:]], axis=2).astype(jnp.float32)
    vband = jnp.concatenate([vb[:, :-1], vb[:, 1:]], axis=2).astype(jnp.float32)
    qb = jnp.pad(q, padk).reshape(n, nb, WINDOW, D_KV_HEADS, D_REP, hd).astype(jnp.float32)
    s = jnp.einsum("nbqgrd,nbkgd->nbgrqk", qb, kband) * (hd ** -0.5)
    qi = jnp.arange(tp).reshape(nb, WINDOW)
    kj = jnp.arange(nb)[:, None] * WINDOW + jnp.arange(2 * WINDOW)[None, :]
    t_abs = pos0 + qi
    s_abs = pos0 - WINDOW + kj
    dist = t_abs[:, :, None] - s_abs[:, None, :]
    valid = (dist >= 0) & (dist < WINDOW) & (s_abs[:, None, :] >= 0) & (kj[:, None, :] < WINDOW + t)
    slopes = jnp.asarray(2.0 ** (-8.0 * np.arange(1, D_HEADS + 1) / D_HEADS), jnp.float32)
    slopes = slopes.reshape(1, 1, D_KV_HEADS, D_REP, 1, 1)
    s = s - slopes * dist.astype(jnp.float32)[None, :, None, None]
    s = jnp.where(valid[None, :, None, None], s, -jnp.inf)
    sink = sinks.astype(jnp.float32).reshape(1, 1, D_KV_HEADS, D_REP, 1, 1)
    m = jnp.maximum(jnp.max(s, axis=-1, keepdims=True), sink)
    p = jnp.exp(s - m)
    p = p / (jnp.sum(p, axis=-1, keepdims=True) + jnp.exp(sink - m))
    o = jnp.einsum("nbgrqk,nbkgd->nbqgrd", p, vband).reshape(n, tp, D_HEADS * hd)[:, :t]
    return o.astype(q.dtype), new_k, new_v


def even_layer(x, c, st_a, st_b, w_mod, b_mod, g_pre, g_post, w_in, conv_a_w, conv_b_w, conv_b_b,
               ln_g, ln_b, w_out):
    shift, scale, gate = adaln(c, w_mod, b_mod)
    h = rms_norm(x, g_pre) * (1 + scale) + shift
    a_x, a_b, a_c, a_gate, b_val, b_glu, b_gate = split_cols(h @ w_in, EVEN_SPLITS)
    ya, new_a = causal_dwconv(a_c * a_x, st_a, conv_a_w)
    ya = a_b * ya * jax.nn.silu(a_gate)
    ub = b_val * jax.nn.sigmoid(b_glu)
    yb, new_b = causal_dwconv(ub, st_b, conv_b_w)
    yb = jax.nn.silu(layer_norm(yb + conv_b_b, ln_g, ln_b)) * jax.nn.silu(b_gate)
    mix = jnp.concatenate([ya, yb], axis=-1) @ w_out
    return x + gate * rms_norm(mix, g_post), new_a, new_b


def odd_layer(x, c, st_c, st_k, st_v, pos0, w_mod, b_mod, g_pre, g_post, w_in, pool_w, pool_scale,
              sinks, w_out):
    n, t, _ = x.shape
    shift, scale, gate = adaln(c, w_mod, b_mod)
    h = rms_norm(x, g_pre) * (1 + scale) + shift
    c_u, c_gate, q, k, v, d_gate = split_cols(h @ w_in, ODD_SPLITS)
    yc, new_c = multiscale_pool(c_u, st_c, pool_w, pool_scale, pos0)
    yc = yc * jax.nn.silu(c_gate)
    yd, new_k, new_v = window_attention(q.reshape(n, t, D_HEADS, HEAD_DIM),
                                        k.reshape(n, t, D_KV_HEADS, HEAD_DIM),
                                        v.reshape(n, t, D_KV_HEADS, HEAD_DIM), st_k, st_v, sinks, pos0)
    yd = yd * jax.nn.silu(d_gate)
    mix = jnp.concatenate([yc, yd], axis=-1) @ w_out
    return x + gate * rms_norm(mix, g_post), new_c, new_k, new_v


def trunk(x, c, st_a, st_b, st_c, st_k, st_v, pos0, we, wo):
    na, nbs, nc, nk, nv = [], [], [], [], []
    for layer in range(DEPTH):
        i = layer // 2
        if layer % 2 == 0:
            x, sa, sb = even_layer(x, c, st_a[i], st_b[i], *[w[i] for w in we])
            na.append(sa)
            nbs.append(sb)
        else:
            x, sc, sk, sv = odd_layer(x, c, st_c[i], st_k[i], st_v[i], pos0, *[w[i] for w in wo])
            nc.append(sc)
            nk.append(sk)
            nv.append(sv)
    return x, jnp.stack(na), jnp.stack(nbs), jnp.stack(nc), jnp.stack(nk), jnp.stack(nv)


def setup_inputs(seed: int = 0) -> dict:
    key = jax.random.key(seed)
    ks = iter(jax.random.split(key, 40))

    def nrm(shape, s=1.0):
        return s * jax.random.normal(next(ks), shape, jnp.float32)

    d = D_MODEL
    return {
        "x_prompt": nrm((BATCH, SEQ, d)),
        "x_sample": nrm((DEC_BATCH, DEC_SEQ, d)),
        "state_conv_a": nrm((N_EVEN, DEC_BATCH, CONV_A - 1, A_WIDTH)),
        "state_conv_b": nrm((N_EVEN, DEC_BATCH, CONV_B - 1, B_WIDTH)),
        "state_pool_c": nrm((N_ODD, DEC_BATCH, POOL_PAST, C_WIDTH)),
        "cache_win_k": nrm((N_ODD, DEC_BATCH, WINDOW, D_KV_HEADS, HEAD_DIM)),
        "cache_win_v": nrm((N_ODD, DEC_BATCH, WINDOW, D_KV_HEADS, HEAD_DIM)),
        "c_prompt": nrm((BATCH, d)),
        "c_sample": nrm((DEC_BATCH, d)),
        "w_mod_e": nrm((N_EVEN, d, 3 * d), d ** -0.5),
        "b_mod_e": nrm((N_EVEN, 3 * d), 0.02),
        "g_pre_e": 1.0 + nrm((N_EVEN, d), 0.1),
        "g_post_e": 1.0 + nrm((N_EVEN, d), 0.1),
        "w_in_e": nrm((N_EVEN, d, EVEN_IN), d ** -0.5),
        "conv_a_w": nrm((N_EVEN, CONV_A, A_WIDTH), CONV_A ** -0.5),
        "conv_b_w": nrm((N_EVEN, CONV_B, B_WIDTH), CONV_B ** -0.5),
        "conv_b_b": nrm((N_EVEN, B_WIDTH), 0.02),
        "ln_b_g": 1.0 + nrm((N_EVEN, B_WIDTH), 0.1),
        "ln_b_b": nrm((N_EVEN, B_WIDTH), 0.02),
        "w_out_e": nrm((N_EVEN, MIX_EVEN, d), MIX_EVEN ** -0.5),
        "w_mod_o": nrm((N_ODD, d, 3 * d), d ** -0.5),
        "b_mod_o": nrm((N_ODD, 3 * d), 0.02),
        "g_pre_o": 1.0 + nrm((N_ODD, d), 0.1),
        "g_post_o": 1.0 + nrm((N_ODD, d), 0.1),
        "w_in_o": nrm((N_ODD, d, ODD_IN), d ** -0.5),
        "pool_w": nrm((N_ODD, POOL_GROUPS, POOL_GROUP_W, POOL_GROUP_W), POOL_GROUP_W ** -0.5),
        "pool_scale": 1.0 + nrm((N_ODD, C_WIDTH), 0.1),
        "sinks": nrm((N_ODD, D_HEADS)),
        "w_out_o": nrm((N_ODD, MIX_ODD, d), MIX_ODD ** -0.5),
    }


def reference(x_prompt, x_sample, state_conv_a, state_conv_b, state_pool_c, cache_win_k, cache_win_v,
              c_prompt, c_sample, w_mod_e, b_mod_e, g_pre_e, g_post_e, w_in_e, conv_a_w, conv_b_w, conv_b_b,
              ln_b_g, ln_b_b, w_out_e, w_mod_o, b_mod_o, g_pre_o, g_post_o, w_in_o, pool_w, pool_scale,
              sinks, w_out_o):
    we = (w_mod_e, b_mod_e, g_pre_e, g_post_e, w_in_e, conv_a_w, conv_b_w, conv_b_b, ln_b_g, ln_b_b, w_out_e)
    wo = (w_mod_o, b_mod_o, g_pre_o, g_post_o, w_in_o, pool_w, pool_scale, sinks, w_out_o)
    dt = x_prompt.dtype
    z_a = jnp.zeros((N_EVEN, BATCH, CONV_A - 1, A_WIDTH), dt)
    z_b = jnp.zeros((N_EVEN, BATCH, CONV_B - 1, B_WIDTH), dt)
    z_c = jnp.zeros((N_ODD, BATCH, POOL_PAST, C_WIDTH), dt)
    z_kv = jnp.zeros((N_ODD, BATCH, WINDOW, D_KV_HEADS, HEAD_DIM), dt)
    y_prompt, pa, pb, pc, pk, pv = trunk(x_prompt, c_prompt, z_a, z_b, z_c, z_kv, z_kv, 0, we, wo)
    y_sample, sa, sb, sc, sk, sv = trunk(x_sample, c_sample, state_conv_a, state_conv_b, state_pool_c,
                                         cache_win_k, cache_win_v, PAST_LEN, we, wo)
    return (y_prompt, y_sample, pa, sa, pb, sb, pc, sc, pk, sk, pv, sv)
```

```python
import numpy as np
from contextlib import ExitStack
import concourse.bass as bass
import concourse.mybir as mybir
from concourse.bass_utils import run_bass_kernel_spmd

F32 = mybir.dt.float32
BF16 = mybir.dt.bfloat16
AF = mybir.ActivationFunctionType
ALU = mybir.AluOpType
AX = mybir.AxisListType

ENGS = ("pe", "act", "dve", "pool", "sp")


class Op:
    __slots__ = ("eng", "fn", "reads", "writes", "dma_key", "dma_k", "pos",
                 "waits", "signal", "sigval", "name")

    def __init__(self, eng, fn, reads, writes, dma_key, name):
        self.eng = eng
        self.fn = fn
        self.reads = tuple(reads)
        self.writes = tuple(writes)
        self.dma_key = dma_key
        self.dma_k = 0
        self.pos = 0
        self.waits = []
        self.signal = False
        self.sigval = 0
        self.name = name


class Prog:
    def __init__(self):
        self.ops = []
        self.dma_mode = {}
        self.final_keys = []
        self.barriers = set()

    def add(self, eng, fn, reads=(), writes=(), dma_key=None, name=""):
        op = Op(eng, fn, reads, writes, dma_key, name)
        self.ops.append(op)
        return op

    def barrier(self):
        self.barriers.add(len(self.ops))

    def pe(self, fn, reads=(), writes=(), name=""):
        return self.add("pe", fn, reads, writes, name=name)

    def act(self, fn, reads=(), writes=(), name=""):
        return self.add("act", fn, reads, writes, name=name)

    def dve(self, fn, reads=(), writes=(), name=""):
        return self.add("dve", fn, reads, writes, name=name)

    def pool(self, fn, reads=(), writes=(), name=""):
        return self.add("pool", fn, reads, writes, name=name)

    def dma(self, eng, key, fn, reads=(), writes=(), mode="slot", final=False, name=""):
        self.dma_mode.setdefault(key, mode)
        assert self.dma_mode[key] == mode
        if final and key not in self.final_keys:
            self.final_keys.append(key)
        return self.add(eng, fn, reads, writes, dma_key=key, name=name)

    def analyze(self):
        last_writer = {}
        readers = {}
        eng_pos = {e: 0 for e in ENGS}
        dma_cnt = {}
        dma_last = {}
        waited = {e: {} for e in ENGS}
        last_on = {}
        bar_ops = []
        for oi, op in enumerate(self.ops):
            if oi in self.barriers:
                bar_ops = list(last_on.values())
            op.pos = eng_pos[op.eng]
            eng_pos[op.eng] += 1
            raw = set()
            other = set(bar_ops)
            for r in op.reads:
                if r in last_writer:
                    raw.add(last_writer[r])
            for w in op.writes:
                if w in last_writer:
                    other.add(last_writer[w])
                for rd in readers.get(w, ()):
                    other.add(rd)
            if op.dma_key is not None:
                k = dma_cnt.get(op.dma_key, 0) + 1
                dma_cnt[op.dma_key] = k
                op.dma_k = k
                if self.dma_mode[op.dma_key] == "slot" and op.dma_key in dma_last:
                    other.add(dma_last[op.dma_key])
                dma_last[op.dma_key] = op
            need = {}
            for d in raw | other:
                if d is op:
                    continue
                if d.dma_key is not None:
                    sk = ("dma", d.dma_key)
                    v = d.dma_k if self.dma_mode[d.dma_key] == "slot" else -1
                    if sk not in need or (need[sk] != -1 and (v == -1 or v > need[sk])):
                        need[sk] = v
                    continue
                if d.eng == op.eng and op.dma_key is None:
                    if op.eng == "pe":
                        continue
                sk = ("eng", d.eng)
                if sk not in need or d.pos > need[sk].pos:
                    need[sk] = d
            for sk, v in need.items():
                op.waits.append((sk, v))
            for r in op.reads:
                readers.setdefault(r, []).append(op)
            for w in op.writes:
                last_writer[w] = op
                readers[w] = []
            last_on[(op.eng, op.dma_key)] = op
        self.dma_cnt = dma_cnt
        for op in self.ops:
            ws = []
            wd = waited[op.eng]
            for sk, v in op.waits:
                if sk[0] == "dma":
                    val = (self.dma_cnt[sk[1]] if v == -1 else v) * 16
                    if wd.get(sk, 0) >= val:
                        continue
                    wd[sk] = val
                    ws.append((sk, val))
                else:
                    if wd.get(sk, -1) >= v.pos:
                        continue
                    wd[sk] = v.pos
                    v.signal = True
                    ws.append((sk, v))
            op.waits = ws
        cnt = {e: 0 for e in ENGS}
        for op in self.ops:
            if op.signal:
                cnt[op.eng] += 1
                op.sigval = cnt[op.eng]
        self.sig_cnt = cnt

    def emit(self, nc):
        self.analyze()
        with ExitStack() as es:
            sems = {}
            for e in ENGS:
                if self.sig_cnt[e] > 0:
                    sems[("eng", e)] = es.enter_context(nc.semaphore(f"s_{e}"))
            for i, key in enumerate(self.dma_cnt):
                sems[("dma", key)] = es.enter_context(nc.semaphore(f"d_{i}"))
            block = es.enter_context(nc.Block())
            streams = {e: [op for op in self.ops if op.eng == e] for e in ENGS}

            def run(eng, ename):
                for op in streams[ename]:
                    for sk, v in op.waits:
                        if sk[0] == "dma":
                            eng.wait_ge(sems[sk], v)
                        else:
                            eng.wait_ge(sems[sk], v.sigval)
                    ins = op.fn(eng)
                    if op.dma_key is not None:
                        ins.then_inc(sems[("dma", op.dma_key)], 16)
                    elif op.signal:
                        ins.then_inc(sems[("eng", ename)], 1)
                if ename == "sp":
                    for key in self.final_keys:
                        eng.wait_ge(sems[("dma", key)], self.dma_cnt[key] * 16)

            @block.tensor
            def _(eng):
                run(eng, "pe")

            @block.scalar
            def _(eng):
                run(eng, "act")

            @block.vector
            def _(eng):
                run(eng, "dve")

            @block.gpsimd
            def _(eng):
                run(eng, "pool")

            @block.sync
            def _(eng):
                run(eng, "sp")


import ml_dtypes

NCORES = 8
HALO = 256
BT = 256
NBLK_P = (HALO + 2048) // BT
POOL_W = (2, 4, 8, 16)
NEG = -240000.0

SM = {}
_o = 0
for _n, _w in (("bmod_e", 24), ("bmod_o", 24), ("gpre_e", 8), ("gpost_e", 8), ("gpre_o", 8), ("gpost_o", 8),
               ("caw", 12), ("cbw", 124), ("cbb", 4), ("lng", 4), ("lnb", 4), ("psc", 4), ("snk", 4),
               ("hm", 1), ("facm1", 64)):
    SM[_n] = (_o, _w)
    _o += _w
NSM = _o


def build_program():
    nc = bass.Bass("TRN2", target_bir_lowering=False)
    P = Prog()
    es = ExitStack()

    def din(name, shape, dt=F32):
        return nc.dram_tensor(name, list(shape), dt, kind="ExternalInput").ap()

    def dout(name, shape, dt=F32):
        return nc.dram_tensor(name, list(shape), dt, kind="ExternalOutput").ap()

    def sb(name, shape, dt=F32):
        return es.enter_context(nc.sbuf_tensor("sb_" + name, list(shape), dt))

    xp = din("xp", [HALO + 2048, 1024]); xs = din("xs", [128, 1024])
    cT_d = din("cT", [128, 8 * 17]); small_d = din("small", [128, NSM]); poolw_d = din("poolw", [128, 512])
    identf_d = din("identf", [128, 128]); biasp_d = din("biasp", [128, 2048], BF16); biass_d = din("biass", [128, 2048], BF16)
    sa_d = din("sa", [32, 512]); sb_d = din("sb", [480, 512]); sc_d = din("sc", [240, 512])
    ck_d = din("ck", [128, 16 * 128]); cv_d = din("cv", [128, 16 * 128])
    wmod_d = [din("wmod_e", [128, 8 * 3072]), din("wmod_o", [128, 8 * 3072])]
    win_e_d = din("win_e", [128, 8 * 3584]); wout_e_d = din("wout_e", [128, 8 * 1024])
    win_o_d = din("win_o", [128, 8 * 2304]); wout_o_d = din("wout_o", [128, 8 * 1024])
    wout_d = [wout_e_d, wout_o_d]
    wout_bf = [nc.dram_tensor(f"wout_bf{l}", [128, 8 * 1024], BF16, kind="Internal").ap() for l in range(2)]
    yp_o = dout("yp", [2048, 1024]); ys_o = dout("ys", [128, 1024])
    pa_o = dout("pa", [2, 512]); sa_o = dout("sa_o", [32, 512])
    pb_o = dout("pb", [30, 512]); sb_o = dout("sb_o", [480, 512])
    pc_o = dout("pc", [15, 512]); sc_o = dout("sc_o", [240, 512])
    pk_o = dout("pk", [128, 128]); sk_o = dout("sk_o", [2048, 128])
    pv_o = dout("pv", [128, 128]); sv_o = dout("sv_o", [2048, 128])

    WE = sb("WE", [128, 8 * 3584], BF16); WOi = sb("WOi", [128, 8 * 2304], BF16); WOUT = sb("WOUT", [128, 8 * 1024], BF16)
    win_e = WE[:].rearrange("p (k n) -> p k n", k=8)
    win_o = WOi[:].rearrange("p (k n) -> p k n", k=8)
    wout3 = WOUT[:].rearrange("p (k n) -> p k n", k=8)
    identf = sb("identf", [128, 128]); identb = sb("identb", [128, 128], BF16); onesb = sb("onesb", [128, 128], BF16)
    biasT = sb("biasT", [128, 2048], BF16)
    small = sb("small", [128, NSM])
    Wp = sb("Wp", [128, 8 * 128], BF16)
    diag3 = sb("diag3", [128, 12 * 128], BF16)
    wb2 = sb("wb2", [128, 124], BF16); b1sc = sb("b1sc", [128, 16]); esk = sb("esk", [128, 4]); epsc = sb("epsc", [128, 2])
    modv = sb("modv", [128, 2 * 3 * 8 * 17])
    xst = [sb(f"xst{i}", [128, 512]) for i in range(2)]
    NXT = 3
    xTs = [sb(f"xT{i}", [128, 8 * BT]) for i in range(NXT)]
    tmp = [None, None] + [sb(f"tmp{i}", [128, BT]) for i in range(2, 7)]
    PT = sb("PT", [128, 2 * 1024], BF16)
    UB = sb("UB", [128, 4928], BF16)
    NDG = 3
    dg = sb("dg", [128, NDG * 1024], BF16)
    ps = [es.enter_context(nc.psum_tensor(f"ps{i}", [128, 512], F32)) for i in range(8)]

    class Ctx:
        pass

    def mkctx(n, nslots, tmps):
        cx = Ctx()
        cx.n = n
        cx.hT = sb(n + "hT", [128, 8 * BT], BF16)
        cx.hT3 = cx.hT[:].rearrange("p (k t) -> p k t", k=8)
        cx.sqmix = sb(n + "sqmix", [128, 8 * BT], BF16)
        cx.sq3 = cx.sqmix[:].rearrange("p (k t) -> p k t", k=8)
        cx.rstd = sb(n + "rstd", [128, BT])
        cx.scr = sb(n + "scr", [128, nslots * BT])
        cx.sqB3 = cx.scr[:, 0:4 * BT].bitcast(BF16).rearrange("p (k t) -> p k t", k=8)
        cx.tmp = tmps
        cx.kr = n + "rstd"
        cx.khc = lambda c: f"{n}hT{c}"
        cx.ksc = lambda c: f"{n}sq{c}"
        cx.kh_all = [f"{n}hT{c}" for c in range(8)]
        cx.ks_all = [f"{n}sq{c}" for c in range(8)]
        cx.sk = lambda i: f"{n}scr{i}"
        return cx

    C0 = mkctx("a", 8, [(tmp[3], "tmp3"), (tmp[4], "tmp4"), (tmp[2], "tmp2"), (tmp[3], "tmp3"), (tmp[4], "tmp4")])
    C1 = mkctx("b", 6, [(tmp[5], "tmp5"), (tmp[6], "tmp6")])
    acc3 = C0.scr[:, 0:4 * BT].rearrange("p (k t) -> p k t", k=4)
    ybs = C0.scr[:, 4 * BT:8 * BT].bitcast(BF16)
    ybb3 = ybs[:, 0:4 * BT].rearrange("p (k t) -> p k t", k=4)
    ysq3 = ybs[:, 4 * BT:8 * BT].rearrange("p (k t) -> p k t", k=4)
    dn = C1.scr[:, 0:512]
    sgd3 = C1.scr[:, 2 * BT:4 * BT].bitcast(BF16).rearrange("p (k t) -> p k t", k=4)
    qT3 = C1.scr[:, 4 * BT:6 * BT].bitcast(BF16).rearrange("p (k t) -> p k t", k=4)

    def sm(name, a=0, b=None):
        o, w = SM[name]
        return small[:, o + a:o + (w if b is None else b)]

    def mv(l, kind, kc, b0, b1):
        o = ((l * 3 + kind) * 8 + kc) * 17
        return modv[:, o + b0:o + b1]

    def xT3(par):
        return xTs[par][:].rearrange("p (k t) -> p k t", k=8)

    def xkc(par, c):
        return f"xT{par}_{c}"

    def xk_all(par):
        return [f"xT{par}_{c}" for c in range(8)]

    class Bufs:
        pass

    def carve(kind):
        b = Bufs()
        if kind == "p":
            b.ts = 1
            b.axc = UB[:, 0:1032].rearrange("p (c t) -> p c t", c=4)
            b.ub = UB[:, 1032:2176].rearrange("p (c t) -> p c t", c=4)
            b.cu = UB[:, 2432:3516].rearrange("p (c t) -> p c t", c=4)
            b.kT = UB[:, 3516:3900]
            b.vt = UB[:, 3900:4284].rearrange("p (t d) -> p t d", t=3)
            b.key = "ubp"
        else:
            b.ts = 16
            b.ub = UB[:, 0:2432].rearrange("p (c t) -> p c t", c=4)
            b.axc = UB[:, 4284:4924].rearrange("p (c t) -> p c t", c=4)
            b.cu = UB[:, 2432:3904].rearrange("p (c t) -> p c t", c=4)
            b.kT = UB[:, 3904:4032]
            b.vt = UB[:, 4032:4160].rearrange("p (t d) -> p t d", t=1)
            b.key = "ubs"
        return b

    bank_i = [0]
    dg_cnt = [0]
    xin_cnt = [0]

    CONV_BANK = 7

    def nb():
        i = bank_i[0] % 7
        bank_i[0] += 1
        return i

    def stage_slot():
        i = xin_cnt[0] % 2
        xin_cnt[0] += 1
        return i

    def ld(eng, key, out, in_, w, mode="group"):
        P.dma(eng, key, lambda e: e.dma_start(out=out, in_=in_), writes=w, mode=mode)

    cT = tmp[4][:, 0:136]
    scT = tmp[3][:, 0:68].bitcast(BF16)
    poolw = xst[0][:, 0:512]
    ld("sp", "c0", small[:], small_d, ["small"]); ld("sp", "c0", cT, cT_d, ["tmp4"])
    ld("sp", "c0", identf[:], identf_d, ["identf"]); ld("sp", "c0", poolw, poolw_d, ["xst0"])
    ld("sp", "c0", biasT[:], biasp_d, ["biasT"])

    def load_wpiece(src_d, ncols_total, c0, c1, slot, key):
        P.dma("pool", key, lambda e: e.dma_start(out=wout3[:, :, slot * 256:slot * 256 + (c1 - c0)],
                                                 in_=src_d.rearrange("p (k n) -> p k n", k=8)[:, :, c0:c1]), writes=[f"WOUT{slot}"], mode="slot")

    P.act(lambda e: e.copy(out=identb[:], in_=identf[:]), reads=["identf"], writes=["identb"])
    P.dve(lambda e: e.memset(onesb[:], 1.0), writes=["onesb"])
    P.dve(lambda e: e.memset(epsc[:, 0:1], 1e-6), writes=["epsc"])
    P.dve(lambda e: e.memset(epsc[:, 1:2], 1e-5), writes=["epsc"])
    P.act(lambda e: e.activation(out=scT, in_=cT, func=AF.Silu), reads=["tmp4"], writes=["tmp3"])
    P.act(lambda e: e.activation(out=esk[:], in_=sm("snk"), func=AF.Exp), reads=["small"], writes=["esk"])
    P.dve(lambda e: e.tensor_scalar(out=wb2[:], in0=sm("cbw"), scalar1=0.5, scalar2=None, op0=ALU.mult), reads=["small"], writes=["wb2"])
    for c in range(4):
        for j in range(3):
            P.dve(lambda e, c=c, j=j: e.tensor_scalar(out=diag3[:, (c * 3 + j) * 128:(c * 3 + j + 1) * 128], in0=identb[:],
                                                      scalar1=sm("caw", c * 3 + j, c * 3 + j + 1), scalar2=None, op0=ALU.mult),
                  reads=["identb", "small"], writes=["diag3"])
    for g, w in enumerate(POOL_W):
        P.dve(lambda e, g=g, w=w: e.tensor_scalar(out=Wp[:, (2 * g) * 128:(2 * g + 1) * 128], in0=xst[0][:, g * 128:(g + 1) * 128],
                                                  scalar1=(1.0 / w - 1.0), scalar2=None, op0=ALU.mult), reads=["xst0"], writes=["Wp"])
        P.dve(lambda e, g=g, w=w: e.tensor_scalar(out=Wp[:, (2 * g + 1) * 128:(2 * g + 2) * 128], in0=xst[0][:, g * 128:(g + 1) * 128],
                                                  scalar1=(1.0 / w), scalar2=None, op0=ALU.mult), reads=["xst0"], writes=["Wp"])
    scT3 = scT.rearrange("p (k b) -> p k b", k=8)

    def do_mod(l):
        bm_, gpre, gpost = (("bmod_e", "gpre_e", "gpost_e"), ("bmod_o", "gpre_o", "gpost_o"))[l]
        P.dve(lambda e: e.tensor_scalar(out=b1sc[:, l * 8:(l + 1) * 8], in0=sm(bm_, 8, 16), scalar1=1.0, scalar2=None, op0=ALU.add),
              reads=["small"], writes=["b1sc"])
        for pc_ in range(12):
            slot = pc_ % 4
            load_wpiece(wmod_d[l], 3072, pc_ * 256, (pc_ + 1) * 256, slot, f"wmod{slot}")
            for q in range(2):
                fj = pc_ * 2 + q
                bi_ = nb()
                for kc in range(8):
                    P.pe(lambda e, bi_=bi_, q=q, kc=kc, slot=slot: e.matmul(ps[bi_][:, 0:17], lhsT=wout3[:, kc, slot * 256 + q * 128:slot * 256 + (q + 1) * 128],
                                                                         rhs=scT3[:, kc, :], start=(kc == 0), stop=(kc == 7)),
                         reads=[f"WOUT{slot}", "tmp3"], writes=[f"ps{bi_}"])
                j = fj % 8
                if fj < 8:
                    P.act(lambda e, bi_=bi_, fj=fj, j=j: e.activation(out=mv(l, 0, j, 0, 17), in_=ps[bi_][:, 0:17], func=AF.Identity,
                                                                    bias=sm(bm_, fj, fj + 1), scale=1.0), reads=[f"ps{bi_}", "small"], writes=["modv"])
                elif fj < 16:
                    P.dve(lambda e, bi_=bi_, j=j: e.tensor_scalar(out=mv(l, 1, j, 0, 17), in0=ps[bi_][:, 0:17], scalar1=b1sc[:, l * 8 + j:l * 8 + j + 1],
                                                                scalar2=sm(gpre, j, j + 1), op0=ALU.add, op1=ALU.mult),
                          reads=[f"ps{bi_}", "small", "b1sc"], writes=["modv"])
                else:
                    P.dve(lambda e, bi_=bi_, fj=fj, j=j: e.tensor_scalar(out=mv(l, 2, j, 0, 17), in0=ps[bi_][:, 0:17], scalar1=sm(bm_, fj, fj + 1),
                                                                       scalar2=sm(gpost, j, j + 1), op0=ALU.add, op1=ALU.mult),
                          reads=[f"ps{bi_}", "small"], writes=["modv"])

    do_mod(0)
    do_mod(1)
    for i in range(7):
        ld("pool", f"we{i}", win_e[:, :, i * 512:(i + 1) * 512],
           win_e_d.rearrange("p (k n) -> p k n", k=8)[:, :, i * 512:(i + 1) * 512], [f"WE{i}"])
    P.dma("pool", "wcast0", lambda e: e.dma_start(out=wout_bf[0], in_=wout_d[0]), writes=["woutbf0"], mode="slot")
    ld("pool", "wo_i", win_o, win_o_d.rearrange("p (k n) -> p k n", k=8), ["WOi"])
    P.dma("pool", "wcast1", lambda e: e.dma_start(out=wout_bf[1], in_=wout_d[1]), writes=["woutbf1"], mode="slot")
    P.dma("sp", "d2d", lambda e: e.dma_start(out=sb_o[0:22 * 16, :], in_=sb_d[8 * 16:30 * 16, :]), final=True)
    P.dma("sp", "d2d", lambda e: e.dma_start(out=sc_o[0:7 * 16, :], in_=sc_d[8 * 16:15 * 16, :]), final=True)
    P.dma("sp", "d2d", lambda e: e.dma_start(out=sk_o[0:120 * 16, :], in_=ck_d.rearrange("p (s d) -> (p s) d", s=16)[8 * 16:128 * 16, :]), final=True)
    P.dma("sp", "d2d", lambda e: e.dma_start(out=sv_o[0:120 * 16, :], in_=cv_d.rearrange("p (s d) -> (p s) d", s=16)[8 * 16:128 * 16, :]), final=True)

    def load_x(src_rows, par, tcol):
        x3 = xT3(par)
        for h in range(2):
            s = stage_slot()
            P.dma("sp", f"xst{s}", lambda e, s=s, h=h: e.dma_start(out=xst[s][:], in_=src_rows[:, h * 512:(h + 1) * 512]), writes=[f"xst{s}"])
            bk = nb()
            for q in range(4):
                P.pe(lambda e, bk=bk, q=q, s=s: e.transpose(out=ps[bk][:, q * 128:(q + 1) * 128], in_=xst[s][:, q * 128:(q + 1) * 128], identity=identf[:]),
                     reads=[f"xst{s}", "identf"], writes=[f"ps{bk}"])
            P.act(lambda e, bk=bk, h=h: e.copy(out=x3[:, h * 4:(h + 1) * 4, tcol:tcol + 128], in_=ps[bk][:].rearrange("p (q t) -> p q t", q=4)),
                  reads=[f"ps{bk}"], writes=[xkc(par, h * 4 + q_) for q_ in range(4)])

    def store_y(dst_rows, par, tcol):
        x3 = xT3(par)
        for h in range(2):
            s = stage_slot()
            bk = nb()
            for q in range(4):
                kc = h * 4 + q
                P.pe(lambda e, bk=bk, q=q, kc=kc: e.transpose(out=ps[bk][:, q * 128:(q + 1) * 128], in_=x3[:, kc, tcol:tcol + 128], identity=identf[:]),
                     reads=[xkc(par, kc), "identf"], writes=[f"ps{bk}"])
            P.act(lambda e, bk=bk, s=s: e.copy(out=xst[s][:], in_=ps[bk][:]), reads=[f"ps{bk}"], writes=[f"xst{s}"])
            P.dma("sp", f"xst{s}", lambda e, s=s, h=h: e.dma_start(out=dst_rows[:, h * 512:(h + 1) * 512], in_=xst[s][:]), reads=[f"xst{s}"], final=True)

    def stats_tail(cx, sqv3, sqkeys, nt):
        bk = nb()
        for kc in range(8):
            P.pe(lambda e, kc=kc: e.matmul(ps[bk][:, 0:nt], lhsT=onesb[:], rhs=sqv3[:, kc, 0:nt], start=(kc == 0), stop=(kc == 7)),
                 reads=[sqkeys[kc] if len(sqkeys) == 8 else sqkeys[kc // 2], "onesb"], writes=[f"ps{bk}"])
        P.act(lambda e: e.activation(out=cx.rstd[:, 0:nt], in_=ps[bk][:, 0:nt], func=AF.Sqrt, bias=epsc[:, 0:1], scale=1.0 / 1024),
              reads=[f"ps{bk}", "epsc"], writes=[cx.kr])
        P.dve(lambda e: e.reciprocal(out=cx.rstd[:, 0:nt], in_=cx.rstd[:, 0:nt]), reads=[cx.kr], writes=[cx.kr])

    def bc_mod(l, kind, kc):
        return mv(l, kind, kc, 1, 17).unsqueeze(1).broadcast_to([128, 8, 16])

    def tok3(ap2):
        return ap2.rearrange("p (i s) -> p i s", s=16)

    def prenorm(cx, l, nt, sample, par):
        x3 = xT3(par)
        P.act(lambda e: e.activation(out=cx.sq3[:, :, 0:nt], in_=x3[:, :, 0:nt], func=AF.Square), reads=xk_all(par), writes=cx.ks_all)
        yield
        stats_tail(cx, cx.sq3, cx.ks_all, nt)
        yield
        for kc in range(8):
            t, tk = cx.tmp[kc % 2]
            if not sample:
                P.dve(lambda e, kc=kc, t=t: e.scalar_tensor_tensor(out=t[:, 0:nt], in0=x3[:, kc, 0:nt], scalar=mv(l, 1, kc, 0, 1), in1=cx.rstd[:, 0:nt],
                                                                   op0=ALU.mult, op1=ALU.mult), reads=[xkc(par, kc), cx.kr, "modv"], writes=[tk])
                P.act(lambda e, kc=kc, t=t: e.activation(out=cx.hT3[:, kc, 0:nt], in_=t[:, 0:nt], func=AF.Identity, bias=mv(l, 0, kc, 0, 1), scale=1.0),
                      reads=[tk, "modv"], writes=[cx.khc(kc)])
            else:
                P.dve(lambda e, kc=kc, t=t: e.tensor_tensor(out=t[:, 0:nt], in0=x3[:, kc, 0:nt], in1=cx.rstd[:, 0:nt], op=ALU.mult), reads=[xkc(par, kc), cx.kr], writes=[tk])
                P.dve(lambda e, kc=kc, t=t: e.tensor_tensor(out=tok3(t[:, 0:nt]), in0=tok3(t[:, 0:nt]), in1=bc_mod(l, 1, kc), op=ALU.mult),
                      reads=[tk, "modv"], writes=[tk])
                P.dve(lambda e, kc=kc, t=t: e.tensor_tensor(out=tok3(cx.hT3[:, kc, 0:nt]), in0=tok3(t[:, 0:nt]), in1=bc_mod(l, 0, kc), op=ALU.add),
                      reads=[tk, "modv"], writes=[cx.khc(kc)])
        yield

    def group(cx, W3, wkey, col0, nt, ncols=128):
        bk = nb()
        for kc in range(8):
            P.pe(lambda e, kc=kc: e.matmul(ps[bk][0:ncols, 0:nt], lhsT=W3[:, kc, col0:col0 + ncols], rhs=cx.hT3[:, kc, 0:nt], start=(kc == 0), stop=(kc == 7)),
                 reads=[wkey, cx.khc(kc)], writes=[f"ps{bk}"])
        return bk

    def load_wout_slot(l, sl):
        P.dma("sp", f"wout{sl}", lambda e: e.dma_start(out=wout3[:, :, sl * 256:(sl + 1) * 256],
                                                      in_=wout_bf[l].rearrange("p (k n) -> p k n", k=8)[:, :, sl * 256:(sl + 1) * 256]),
              reads=[f"woutbf{l}"], writes=[f"WOUT{sl}"], mode="slot")

    wout_lock = [None] * 4

    def try_wout(cx, l, st):
        for sl in range(4):
            if not st["held"][sl] and wout_lock[sl] is None:
                wout_lock[sl] = cx.n
                load_wout_slot(l, sl)
                st["held"][sl] = True

    def out_proj(cx, l, nt, sample, par, st):
        x3 = xT3(par)
        while not all(st["held"]):
            try_wout(cx, l, st)
            if not all(st["held"]):
                yield
        for dc in range(8):
            bk = nb()
            for kc in range(8):
                P.pe(lambda e, kc=kc, dc=dc, bk=bk: e.matmul(ps[bk][:, 0:nt], lhsT=wout3[:, kc, dc * 128:(dc + 1) * 128], rhs=cx.sq3[:, kc, 0:nt], start=(kc == 0), stop=(kc == 7)),
                     reads=[f"WOUT{dc // 2}", cx.ksc(kc)], writes=[f"ps{bk}"])
            P.act(lambda e, dc=dc, bk=bk: e.copy(out=cx.hT3[:, dc, 0:nt], in_=ps[bk][:, 0:nt]), reads=[f"ps{bk}"], writes=[cx.khc(dc)])
            P.act(lambda e, dc=dc, bk=bk: e.activation(out=cx.sqB3[:, dc, 0:nt], in_=ps[bk][:, 0:nt], func=AF.Square), reads=[f"ps{bk}"], writes=[cx.sk(dc // 2)])
            if dc % 2 == 1:
                wout_lock[dc // 2] = None
            yield
        stats_tail(cx, cx.sqB3, [cx.sk(i) for i in range(4)], nt)
        yield
        for dc in range(8):
            t, tk = cx.tmp[dc % 2]
            if not sample:
                P.dve(lambda e, dc=dc, t=t: e.scalar_tensor_tensor(out=t[:, 0:nt], in0=cx.hT3[:, dc, 0:nt], scalar=mv(l, 2, dc, 0, 1), in1=cx.rstd[:, 0:nt],
                                                                   op0=ALU.mult, op1=ALU.mult), reads=[cx.khc(dc), cx.kr, "modv"], writes=[tk])
            else:
                P.dve(lambda e, dc=dc, t=t: e.tensor_tensor(out=t[:, 0:nt], in0=cx.hT3[:, dc, 0:nt], in1=cx.rstd[:, 0:nt], op=ALU.mult), reads=[cx.khc(dc), cx.kr], writes=[tk])
                P.dve(lambda e, dc=dc, t=t: e.tensor_tensor(out=tok3(t[:, 0:nt]), in0=tok3(t[:, 0:nt]), in1=bc_mod(l, 2, dc), op=ALU.mult),
                      reads=[tk, "modv"], writes=[tk])
            P.pool(lambda e, dc=dc, t=t: e.tensor_tensor(out=x3[:, dc, 0:nt], in0=x3[:, dc, 0:nt], in1=t[:, 0:nt], op=ALU.add), reads=[xkc(par, dc), tk], writes=[xkc(par, dc)])
        yield

    def ld_(ap, a, b):
        return ap[:, :, a:b] if len(ap.shape) == 3 else ap[:, a:b]

    def carry(buf, S, L, first, use_hm, keys):
        if first:
            P.pool(lambda e: e.memset(ld_(buf, 0, S), 0.0), writes=keys)
        elif use_hm:
            P.act(lambda e: e.activation(out=ld_(buf, 0, S), in_=ld_(buf, L, L + S), func=AF.Copy, scale=sm("hm")),
                  reads=keys + ["small"], writes=keys)
        else:
            P.pool(lambda e: e.tensor_copy(out=ld_(buf, 0, S), in_=ld_(buf, L, L + S)), reads=keys, writes=keys)

    def state_out(src3, nch, tcol0, srckeys, dmas, scale=1.0):
        s = stage_slot()
        bk = nb()
        pb = ps[bk][:].bitcast(BF16)
        for c in range(nch):
            P.pe(lambda e, c=c: e.transpose(out=pb[:, c * 128:(c + 1) * 128], in_=src3[:, c, tcol0:tcol0 + 128], identity=identb[:]),
                 reads=list(srckeys) + ["identb"], writes=[f"ps{bk}"])
        P.act(lambda e: e.activation(out=xst[s][:, 0:nch * 128], in_=pb[:, 0:nch * 128], func=AF.Copy, scale=scale), reads=[f"ps{bk}"], writes=[f"xst{s}"])
        for (dst, r0, r1) in dmas:
            P.dma("sp", f"xst{s}", lambda e, dst=dst, r0=r0, r1=r1: e.dma_start(out=dst, in_=xst[s][r0:r1, 0:nch * 128]), reads=[f"xst{s}"], final=True)

    def blk(bi):
        b = Bufs()
        b.sample = (bi == NBLK_P)
        b.nt = 128 if b.sample else BT
        b.ntile = b.nt // 128
        b.par = bi % NXT
        b.B = carve("s" if b.sample else "p")
        b.first, b.use_hm, b.last_p, b.halo = (bi == 0), (bi == 1), (bi == NBLK_P - 1), (bi == 0)
        k = b.B.key
        b.kA, b.kB, b.kC, b.kK, b.kV = k + "A", k + "B", k + "C", k + "K", k + "V"
        b.kBc = [k + "B" + str(c) for c in range(4)]
        pL0 = ["ubpA", "ubpB"] + ["ubpB" + str(c) for c in range(4)]
        pL1 = ["ubpC", "ubpK", "ubpV"]
        b.wA = [b.kA]
        b.wB = [[b.kBc[c]] + (pL0 if b.sample else []) for c in range(4)]
        b.wC = [b.kC] + (pL1 if b.sample else [])
        b.wK = [b.kK] + (pL1 if b.sample else [])
        b.wV = [b.kV] + (pL1 if b.sample else [])
        return b

    def load_state(src_d, nrows, dst3, col0, dkeys):
        r = 0
        while r < nrows:
            n = min(128, nrows - r)
            s = stage_slot()
            P.dma("sp", f"xst{s}", lambda e, r=r, n=n, s=s: e.dma_start(out=xst[s][0:n, 0:512], in_=src_d[r:r + n, :]), writes=[f"xst{s}"])
            bk = nb()
            for c in range(4):
                P.pe(lambda e, c=c, n=n, s=s, bk=bk: e.transpose(out=ps[bk][:, c * 128:c * 128 + n], in_=xst[s][0:n, c * 128:(c + 1) * 128], identity=identf[0:n, 0:n]),
                     reads=[f"xst{s}", "identf"], writes=[f"ps{bk}"])
            P.act(lambda e, r=r, n=n, bk=bk: e.copy(out=dst3[:, :, col0 + r:col0 + r + n], in_=ps[bk][:].rearrange("p (c t) -> p c t", c=4)[:, :, 0:n]),
                  reads=[f"ps{bk}"], writes=dkeys)
            r += n

    xoi = (NBLK_P + 1) % NXT
    xo = xTs[xoi][:].bitcast(BF16)
    kTc = xo[:, 0:2048].rearrange("p (s t) -> p s t", s=16)
    Vc = xo[:, 2048:4096].rearrange("p (s d) -> p s d", s=16)
    xok_all = xk_all(xoi)

    def sample_loads_L0(b):
        pL0 = ["ubpA", "ubpB"] + ["ubpB" + str(c) for c in range(4)]
        load_state(sa_d, 32, b.B.axc, 0, ["ubsA"])
        load_state(sb_d, 480, b.B.ub, 0, ["ubsB"] + pL0)
        P.act(lambda e: e.activation(out=b.B.ub[:, :, 0:480], in_=b.B.ub[:, :, 0:480], func=AF.Copy, scale=2.0), reads=["ubsB"], writes=["ubsB"])

    def sample_loads_L1(b):
        pL1 = ["ubpC", "ubpK", "ubpV"]
        ld("sp", "c1", biasT[:], biass_d, ["biasT"], mode="slot")
        load_state(sc_d, 240, b.B.cu, 0, ["ubsC"] + pL1)
        for s_ in range(0, 16, 4):
            sl = stage_slot()
            P.dma("sp", f"xst{sl}", lambda e, s_=s_, sl=sl: e.dma_start(out=xst[sl][:, :], in_=ck_d[:, s_ * 128:(s_ + 4) * 128]), writes=[f"xst{sl}"])
            bk = nb()
            for q in range(4):
                P.pe(lambda e, q=q, sl=sl, bk=bk: e.transpose(out=ps[bk][:, q * 128:(q + 1) * 128], in_=xst[sl][:, q * 128:(q + 1) * 128], identity=identf[:]),
                     reads=[f"xst{sl}", "identf"], writes=[f"ps{bk}"])
            P.act(lambda e, s_=s_, bk=bk: e.copy(out=kTc[:, s_:s_ + 4, :], in_=ps[bk][:].rearrange("p (q t) -> p q t", q=4)), reads=[f"ps{bk}"], writes=["kTc"] + xok_all)
            sl = stage_slot()
            P.dma("sp", f"xst{sl}", lambda e, s_=s_, sl=sl: e.dma_start(out=xst[sl][:, :], in_=cv_d[:, s_ * 128:(s_ + 4) * 128]), writes=[f"xst{sl}"])
            P.act(lambda e, s_=s_, sl=sl: e.copy(out=Vc[:, s_:s_ + 4, :], in_=xst[sl][:].rearrange("p (s d) -> p s d", s=4)), reads=[f"xst{sl}"], writes=["Vc"] + xok_all)

    def gen_L0(bi):
        b = blk(bi)
        cx, B, nt, sample, par, ts = C0, b.B, b.nt, b.sample, b.par, b.B.ts
        st = {"held": [False] * 4}
        if sample:
            sample_loads_L0(b)
            yield
        for t in range(b.ntile):
            load_x(xs if sample else xp[bi * BT + t * 128: bi * BT + (t + 1) * 128, :], par, t * 128)
            yield
        if not sample:
            carry(B.axc, 2, BT, b.first, b.use_hm, [b.kA])
            carry(B.ub, 30, BT, b.first, b.use_hm, [b.kB] + b.kBc)
        yield from prenorm(cx, 0, nt, sample, par)
        t2, k2 = cx.tmp[2]; t3, k3 = cx.tmp[3]; t4, k4 = cx.tmp[4]
        for c in range(4):
            b1 = group(cx, win_e, "WE0", 0 * 512 + c * 128, nt)
            P.act(lambda e, b1=b1: e.copy(out=t2[:, 0:nt], in_=ps[b1][:, 0:nt]), reads=[f"ps{b1}"], writes=[k2])
            b2 = group(cx, win_e, "WE2", 2 * 512 + c * 128, nt)
            P.dve(lambda e, b2=b2, c=c: e.tensor_tensor(out=B.axc[:, c, 2 * ts:2 * ts + nt], in0=ps[b2][:, 0:nt], in1=t2[:, 0:nt], op=ALU.mult),
                  reads=[f"ps{b2}", k2], writes=[b.kA])
            yield
            b3 = group(cx, win_e, "WE1", 1 * 512 + c * 128, nt)
            b4 = group(cx, win_e, "WE3", 3 * 512 + c * 128, nt)
            P.act(lambda e, b4=b4: e.activation(out=t3[:, 0:nt], in_=ps[b4][:, 0:nt], func=AF.Silu), reads=[f"ps{b4}"], writes=[k3])
            P.dve(lambda e, b3=b3: e.tensor_tensor(out=t4[:, 0:nt], in0=ps[b3][:, 0:nt], in1=t3[:, 0:nt], op=ALU.mult), reads=[f"ps{b3}", k3], writes=[k4])
            yield
            b5 = nb()
            for j in range(3):
                P.pe(lambda e, j=j, c=c, b5=b5: e.matmul(ps[b5][:, 0:nt], lhsT=diag3[:, (c * 3 + j) * 128:(c * 3 + j + 1) * 128],
                                                        rhs=B.axc[:, c, j * ts:j * ts + nt], start=(j == 0), stop=(j == 2)), reads=["diag3", b.kA], writes=[f"ps{b5}"])
            P.dve(lambda e, b5=b5, c=c: e.tensor_tensor(out=cx.sq3[:, c, 0:nt], in0=ps[b5][:, 0:nt], in1=t4[:, 0:nt], op=ALU.mult),
                  reads=[f"ps{b5}", k4], writes=[cx.ksc(c)])
            yield
        if b.last_p or sample:
            state_out(B.axc, 4, 2 * ts + nt - 128, [b.kA], [(sa_o[0:32, :], 96, 128)] if sample else [(pa_o[:, :], 126, 128)])
        for c in range(4):
            bv = group(cx, win_e, "WE4", 4 * 512 + c * 128, nt)
            bg = group(cx, win_e, "WE5", 5 * 512 + c * 128, nt)
            P.act(lambda e, bg=bg: e.activation(out=t2[:, 0:nt], in_=ps[bg][:, 0:nt], func=AF.Tanh, scale=0.5), reads=[f"ps{bg}"], writes=[k2])
            P.dve(lambda e, bv=bv, c=c: e.scalar_tensor_tensor(out=B.ub[:, c, 30 * ts:30 * ts + nt], in0=t2[:, 0:nt], scalar=1.0, in1=ps[bv][:, 0:nt],
                                                              op0=ALU.add, op1=ALU.mult), reads=[f"ps{bv}", k2], writes=b.wB[c])
            yield
        if b.last_p or sample:
            state_out(B.ub, 4, 30 * ts + nt - 128, b.kBc, [(sb_o[22 * 16:30 * 16, :], 0, 128)] if sample else [(pb_o[:, :], 98, 128)], scale=0.5)
        pieces = []
        for c in range(4):
            j = 0
            while j < 31:
                n = min(8, 31 - j)
                pieces.append((c, j, n))
                j += n

        def gen_piece(pidx):
            c, j, n = pieces[pidx]
            pi = (dg_cnt[0] + pidx) % NDG
            dgp = dg[:, pi * 1024:pi * 1024 + n * 128]
            P.dve(lambda e, dgp=dgp, n=n, c=c, j=j: e.tensor_tensor(
                out=dgp.rearrange("p (j m) -> p j m", j=n), in0=identb[:].unsqueeze(1).broadcast_to([128, n, 128]),
                in1=wb2[:, c * 31 + j:c * 31 + j + n].unsqueeze(2).broadcast_to([128, n, 128]), op=ALU.mult),
                reads=["identb", "wb2"], writes=[f"dg{pi}"])

        gen_piece(0)
        gen_piece(1)
        yield
        bc_ = None
        for pidx, (c, j, n) in enumerate(pieces):
            if j == 0:
                bc_ = CONV_BANK
            pi = (dg_cnt[0] + pidx) % NDG
            dgp = dg[:, pi * 1024:pi * 1024 + n * 128]
            for jj in range(n):
                P.pe(lambda e, dgp=dgp, jj=jj, j=j, c=c, bc_=bc_: e.matmul(ps[bc_][:, 0:nt], lhsT=dgp[:, jj * 128:(jj + 1) * 128],
                                                                        rhs=B.ub[:, c, (j + jj) * ts:(j + jj) * ts + nt],
                                                                        start=(j + jj == 0), stop=(j + jj == 30)),
                     reads=[f"dg{pi}", b.kBc[c], b.kB], writes=[f"ps{bc_}"])
            if pidx + 2 < len(pieces):
                gen_piece(pidx + 2)
            if j + n == 31:
                P.act(lambda e, c=c, bc_=bc_: e.activation(out=acc3[:, c, 0:nt], in_=ps[bc_][:, 0:nt], func=AF.Identity, bias=sm("cbb", c, c + 1), scale=1.0),
                      reads=[f"ps{bc_}", "small"], writes=[cx.sk(c)])
                P.act(lambda e, c=c, bc_=bc_: e.activation(out=ybb3[:, c, 0:nt], in_=ps[bc_][:, 0:nt], func=AF.Identity, bias=sm("cbb", c, c + 1), scale=1.0),
                      reads=[f"ps{bc_}", "small"], writes=[cx.sk(4), cx.sk(5)])
                P.act(lambda e, c=c, bc_=bc_: e.activation(out=ysq3[:, c, 0:nt], in_=ps[bc_][:, 0:nt], func=AF.Square, bias=sm("cbb", c, c + 1), scale=1.0),
                      reads=[f"ps{bc_}", "small"], writes=[cx.sk(6), cx.sk(7)])
                bgt = group(cx, win_e, "WE6", 6 * 512 + c * 128, nt)
                P.act(lambda e, bgt=bgt, c=c: e.activation(out=cx.sq3[:, 4 + c, 0:nt], in_=ps[bgt][:, 0:nt], func=AF.Silu), reads=[f"ps{bgt}"], writes=[cx.ksc(4 + c)])
            yield
        dg_cnt[0] += len(pieces)
        yield "tail"
        bm = nb()
        for c in range(4):
            P.pe(lambda e, c=c: e.matmul(ps[bm][:, 0:nt], lhsT=onesb[:], rhs=ybb3[:, c, 0:nt], start=(c == 0), stop=(c == 3)),
                 reads=[cx.sk(4), cx.sk(5), "onesb"], writes=[f"ps{bm}"])
        be = nb()
        for c in range(4):
            P.pe(lambda e, c=c: e.matmul(ps[be][:, 0:nt], lhsT=onesb[:], rhs=ysq3[:, c, 0:nt], start=(c == 0), stop=(c == 3)),
                 reads=[cx.sk(6), cx.sk(7), "onesb"], writes=[f"ps{be}"])
        yield
        mean, var = t2, t3
        P.dve(lambda e: e.tensor_scalar(out=mean[:, 0:nt], in0=ps[bm][:, 0:nt], scalar1=1.0 / 512, scalar2=None, op0=ALU.mult), reads=[f"ps{bm}"], writes=[k2])
        P.dve(lambda e: e.tensor_tensor(out=var[:, 0:nt], in0=mean[:, 0:nt], in1=mean[:, 0:nt], op=ALU.mult), reads=[k2], writes=[k3])
        P.dve(lambda e: e.scalar_tensor_tensor(out=var[:, 0:nt], in0=ps[be][:, 0:nt], scalar=1.0 / 512, in1=var[:, 0:nt], op0=ALU.mult, op1=ALU.subtract),
              reads=[f"ps{be}", k3], writes=[k3])
        P.act(lambda e: e.activation(out=var[:, 0:nt], in_=var[:, 0:nt], func=AF.Sqrt, bias=epsc[:, 1:2], scale=1.0), reads=[k3, "epsc"], writes=[k3])
        P.dve(lambda e: e.reciprocal(out=var[:, 0:nt], in_=var[:, 0:nt]), reads=[k3], writes=[k3])
        try_wout(cx, 0, st)
        yield
        for c in range(4):
            P.dve(lambda e, c=c: e.tensor_tensor(out=acc3[:, c, 0:nt], in0=acc3[:, c, 0:nt], in1=mean[:, 0:nt], op=ALU.subtract), reads=[cx.sk(c), k2], writes=[cx.sk(c)])
            P.dve(lambda e, c=c: e.tensor_tensor(out=acc3[:, c, 0:nt], in0=acc3[:, c, 0:nt], in1=var[:, 0:nt], op=ALU.mult), reads=[cx.sk(c), k3], writes=[cx.sk(c)])
        for c in range(4):
            P.act(lambda e, c=c: e.activation(out=acc3[:, c, 0:nt], in_=acc3[:, c, 0:nt], func=AF.Silu, bias=sm("lnb", c, c + 1), scale=sm("lng", c, c + 1)),
                  reads=[cx.sk(c), "small"], writes=[cx.sk(c)])
        for c in range(4):
            P.dve(lambda e, c=c: e.tensor_tensor(out=cx.sq3[:, 4 + c, 0:nt], in0=acc3[:, c, 0:nt], in1=cx.sq3[:, 4 + c, 0:nt], op=ALU.mult), reads=[cx.sk(c), cx.ksc(4 + c)], writes=[cx.ksc(4 + c)])
        yield
        yield from out_proj(cx, 0, nt, sample, par, st)

    def gen_L1(bi):
        b = blk(bi)
        cx, B, nt, sample, par, ts = C1, b.B, b.nt, b.sample, b.par, b.B.ts
        st = {"held": [False] * 4}
        ntile = b.ntile
        if sample:
            sample_loads_L1(b)
            yield
        if not sample:
            carry(B.cu, 15, BT, b.first, b.use_hm, [b.kC])
            carry(B.kT, 128, BT, b.first, False, [b.kK])
            if b.first:
                P.pool(lambda e: e.memset(B.vt[:, 0, :], 0.0), writes=[b.kV])
            else:
                P.pool(lambda e: e.tensor_copy(out=B.vt[:, 0, :], in_=B.vt[:, 2, :]), reads=[b.kV], writes=[b.kV])
        yield from prenorm(cx, 1, nt, sample, par)
        t3, k3 = cx.tmp[0]; t4, k4 = cx.tmp[1]
        for g, w in enumerate(POOL_W):
            bu = group(cx, win_o, "WOi", 0 + g * 128, nt)
            P.act(lambda e, bu=bu, g=g: e.copy(out=B.cu[:, g, 15 * ts:15 * ts + nt], in_=ps[bu][:, 0:nt]), reads=[f"ps{bu}"], writes=b.wC)
            if b.halo:
                continue
            bgc = group(cx, win_o, "WOi", 512 + g * 128, nt)
            P.act(lambda e, bgc=bgc, g=g: e.activation(out=cx.sq3[:, g, 0:nt], in_=ps[bgc][:, 0:nt], func=AF.Silu), reads=[f"ps{bgc}"], writes=[cx.ksc(g)])
            yield
            bp = nb()
            for j in range(w):
                P.pe(lambda e, j=j, g=g, bp=bp, w=w: e.matmul(ps[bp][:, 0:nt], lhsT=Wp[:, (2 * g + (1 if j else 0)) * 128:(2 * g + (1 if j else 0) + 1) * 128],
                                                        rhs=B.cu[:, g, (15 - j) * ts:(15 - j) * ts + nt], start=(j == 0), stop=(j == w - 1)),
                     reads=["Wp", b.kC], writes=[f"ps{bp}"])
            if b.use_hm:
                bq = nb()
                for j in range(w):
                    P.pe(lambda e, j=j, g=g, bq=bq, w=w: e.matmul(ps[bq][:, 0:16], lhsT=Wp[:, (2 * g + 1) * 128:(2 * g + 2) * 128],
                                                            rhs=B.cu[:, g, (15 - j):(15 - j) + 16], start=(j == 0), stop=(j == w - 1)),
                         reads=["Wp", b.kC], writes=[f"ps{bq}"])
                P.dve(lambda e, bq=bq, g=g: e.tensor_tensor(out=t3[:, 0:16], in0=ps[bq][:, 0:16], in1=sm("facm1", g * 16, g * 16 + 16), op=ALU.mult),
                      reads=[f"ps{bq}", "small"], writes=[k3])
                P.act(lambda e, bp=bp: e.copy(out=t4[:, 0:nt], in_=ps[bp][:, 0:nt]), reads=[f"ps{bp}"], writes=[k4])
                P.dve(lambda e: e.tensor_tensor(out=t4[:, 0:16], in0=t4[:, 0:16], in1=t3[:, 0:16], op=ALU.add), reads=[k3, k4], writes=[k4])
                P.dve(lambda e, g=g: e.scalar_tensor_tensor(out=cx.sq3[:, g, 0:nt], in0=t4[:, 0:nt], scalar=sm("psc", g, g + 1), in1=cx.sq3[:, g, 0:nt], op0=ALU.mult, op1=ALU.mult),
                      reads=[k4, cx.ksc(g), "small"], writes=[cx.ksc(g)])
            else:
                P.dve(lambda e, bp=bp, g=g: e.scalar_tensor_tensor(out=cx.sq3[:, g, 0:nt], in0=ps[bp][:, 0:nt], scalar=sm("psc", g, g + 1), in1=cx.sq3[:, g, 0:nt], op0=ALU.mult, op1=ALU.mult),
                      reads=[f"ps{bp}", cx.ksc(g), "small"], writes=[cx.ksc(g)])
            yield
        if b.last_p or sample:
            state_out(B.cu, 4, 15 * ts + nt - 128, [b.kC], [(sc_o[7 * 16:15 * 16, :], 0, 128)] if sample else [(pc_o[:, :], 113, 128)])
        kcol0 = 0 if sample else 128
        bk_ = group(cx, win_o, "WOi", 1536, nt)
        P.act(lambda e: e.copy(out=B.kT[:, kcol0:kcol0 + nt], in_=ps[bk_][:, 0:nt]), reads=[f"ps{bk_}"], writes=b.wK)
        for t in range(ntile):
            bkv = nb()
            for kc in range(8):
                P.pe(lambda e, kc=kc, t=t, bkv=bkv: e.matmul(ps[bkv][:, 0:256], lhsT=cx.hT3[:, kc, t * 128:(t + 1) * 128], rhs=win_o[:, kc, 1536:1792],
                                                            start=(kc == 0), stop=(kc == 7)), reads=["WOi", cx.khc(kc)], writes=[f"ps{bkv}"])
            vslot = t if sample else 1 + t
            P.act(lambda e, bkv=bkv, vslot=vslot: e.copy(out=B.vt[:, vslot, :], in_=ps[bkv][:, 128:256]), reads=[f"ps{bkv}"], writes=b.wV)
            if sample or (b.last_p and t == ntile - 1):
                s = stage_slot()
                P.act(lambda e, bkv=bkv, s=s: e.copy(out=xst[s][:, 0:256], in_=ps[bkv][:, 0:256]), reads=[f"ps{bkv}"], writes=[f"xst{s}"])
                dk, dv = (sk_o[120 * 16:128 * 16, :], sv_o[120 * 16:128 * 16, :]) if sample else (pk_o[:, :], pv_o[:, :])
                P.dma("sp", f"xst{s}", lambda e, s=s, dk=dk: e.dma_start(out=dk, in_=xst[s][:, 0:128]), reads=[f"xst{s}"], final=True)
                P.dma("sp", f"xst{s}", lambda e, s=s, dv=dv: e.dma_start(out=dv, in_=xst[s][:, 128:256]), reads=[f"xst{s}"], final=True)
        yield
        if b.halo:
            return
        kq = [cx.sk(4), cx.sk(5)]
        kg = [cx.sk(2), cx.sk(3)]
        for r in range(4):
            bq_ = group(cx, win_o, "WOi", 1024 + r * 128, nt)
            P.act(lambda e, bq_=bq_, r=r: e.copy(out=qT3[:, r, 0:nt], in_=ps[bq_][:, 0:nt]), reads=[f"ps{bq_}"], writes=kq)
            bd_ = group(cx, win_o, "WOi", 1792 + r * 128, nt)
            P.act(lambda e, bd_=bd_, r=r: e.activation(out=sgd3[:, r, 0:nt], in_=ps[bd_][:, 0:nt], func=AF.Silu), reads=[f"ps{bd_}"], writes=kg)
            try_wout(cx, 1, st)
            yield
        bias4 = biasT[:].rearrange("p (b h q) -> p b h q", b=2, h=8)
        PT4 = PT[:].rearrange("p (b h q) -> p b h q", b=2, h=8)
        for t in range(ntile):
            q0 = t * 128
            for kb in range(2):
                for g in range(2):
                    bs = nb()
                    P.pe(lambda e, kb=kb, g=g, bs=bs: e.matmul(ps[bs][:, :], lhsT=identb[:], rhs=bias4[:, kb, 4 * g:4 * g + 4, :], start=True, stop=False),
                         reads=["identb", "biasT"], writes=[f"ps{bs}"])
                    if sample and kb == 1:
                        for s_ in range(16):
                            P.pe(lambda e, g=g, bs=bs, s_=s_: e.matmul(ps[bs][:].rearrange("p (r i s) -> p r i s", r=4, s=16)[:, :, :, s_],
                                                                      lhsT=kTc[g * 64:(g + 1) * 64, s_, :],
                                                                      rhs=qT3[g * 64:(g + 1) * 64, :, 0:128].rearrange("p r (i s) -> p r i s", s=16)[:, :, :, s_],
                                                                      start=False, stop=(s_ == 15)), reads=["kTc"] + kq, writes=[f"ps{bs}"])
                    else:
                        kc0 = (kcol0 + q0) if kb == 0 else (kcol0 + q0 - 128)
                        for r in range(4):
                            P.pe(lambda e, g=g, r=r, bs=bs, kc0=kc0, q0=q0: e.matmul(ps[bs][:, r * 128:(r + 1) * 128], lhsT=B.kT[g * 64:(g + 1) * 64, kc0:kc0 + 128],
                                                                                    rhs=qT3[g * 64:(g + 1) * 64, r, q0:q0 + 128], start=False, stop=(r == 3)),
                                 reads=[b.kK] + kq, writes=[f"ps{bs}"])
                    P.act(lambda e, kb=kb, g=g, bs=bs: e.activation(out=PT4[:, kb, 4 * g:4 * g + 4, :], in_=ps[bs][:].rearrange("p (h q) -> p h q", h=4),
                                                                    func=AF.Exp, scale=0.125), reads=[f"ps{bs}"], writes=[f"PT{kb}"])
                if kb == 1 and b.use_hm and t == 0:
                    P.act(lambda e: e.activation(out=PT[:, 1024:2048], in_=PT[:, 1024:2048], func=AF.Copy, scale=sm("hm")),
                          reads=["PT1", "small"], writes=["PT1"])
                try_wout(cx, 1, st)
                yield
            bnum, bden = nb(), nb()
            for (bo, isden) in ((bnum, False), (bden, True)):
                for g in range(2):
                    vcur = B.vt[:, (t if sample else 1 + t), g * 64:(g + 1) * 64]
                    P.pe(lambda e, g=g, bo=bo, isden=isden, vcur=vcur: e.matmul(ps[bo][g * 64:(g + 1) * 64, :], lhsT=(onesb[:, 0:64] if isden else vcur),
                                                                              rhs=PT4[:, 0, 4 * g:4 * g + 4, :], start=True, stop=False),
                         reads=[b.kV, "PT0", "onesb"], writes=[f"ps{bo}"])
                    if sample:
                        for s_ in range(16):
                            P.pe(lambda e, g=g, bo=bo, isden=isden, s_=s_: e.matmul(
                                ps[bo][g * 64:(g + 1) * 64, :].rearrange("p (r i s) -> p r i s", r=4, s=16)[:, :, :, s_],
                                lhsT=(onesb[:, 0:64] if isden else Vc[:, s_, g * 64:(g + 1) * 64]),
                                rhs=PT4[:, 1, 4 * g:4 * g + 4, :].rearrange("p r (i s) -> p r i s", s=16)[:, :, :, s_],
                                start=False, stop=(s_ == 15)), reads=["Vc", "PT1", "onesb"], writes=[f"ps{bo}"])
                    else:
                        vprev = B.vt[:, t, g * 64:(g + 1) * 64]
                        P.pe(lambda e, g=g, bo=bo, isden=isden, vprev=vprev: e.matmul(ps[bo][g * 64:(g + 1) * 64, :], lhsT=(onesb[:, 0:64] if isden else vprev),
                                                                                    rhs=PT4[:, 1, 4 * g:4 * g + 4, :], start=False, stop=True),
                             reads=[b.kV, "PT1", "onesb"], writes=[f"ps{bo}"])
            yield
            kd = [cx.sk(0), cx.sk(1)]
            P.dve(lambda e, bden=bden: e.tensor_tensor(out=dn.rearrange("p (r q) -> p r q", r=4), in0=ps[bden][:].rearrange("p (r q) -> p r q", r=4),
                                                       in1=esk[:].unsqueeze(2).broadcast_to([128, 4, 128]), op=ALU.add), reads=[f"ps{bden}", "esk"], writes=kd)
            P.dve(lambda e: e.reciprocal(out=dn, in_=dn), reads=kd, writes=kd)
            P.dve(lambda e, bnum=bnum: e.tensor_tensor(out=dn, in0=ps[bnum][:], in1=dn, op=ALU.mult), reads=[f"ps{bnum}"] + kd, writes=kd)
            P.dve(lambda e, q0=q0: e.tensor_tensor(out=cx.sq3[:, 4:8, q0:q0 + 128], in0=dn.rearrange("p (r q) -> p r q", r=4), in1=sgd3[:, :, q0:q0 + 128], op=ALU.mult),
                  reads=kd + kg, writes=[cx.ksc(4 + r_) for r_ in range(4)])
            yield
        yield "hold"
        yield from out_proj(cx, 1, nt, sample, par, st)
        for t in range(ntile):
            if sample:
                store_y(ys_o[:, :], par, 0)
            else:
                r0 = (bi - 1) * BT + t * 128
                store_y(yp_o[r0:r0 + 128, :], par, t * 128)
            yield

    def run(g):
        for _ in g:
            pass

    def interleave(ga, gb):
        da = db = False
        while not (da and db):
            if not da:
                try:
                    next(ga)
                except StopIteration:
                    da = True
            if not db:
                try:
                    next(gb)
                except StopIteration:
                    db = True

    run(gen_L0(0))
    run(gen_L0(1))
    doneL0, doneL1 = {0, 1}, set()
    a, bq = 2, 0
    gA = gB = None
    a_tail = False
    b_hold = False
    while len(doneL1) < NBLK_P + 1:
        if gA is None and a < NBLK_P + 1 and ((a - NXT) < 0 or (a - NXT) in doneL1):
            gA = gen_L0(a)
            a_tail = False
        if gB is None and bq < NBLK_P + 1 and bq in doneL0:
            gB = gen_L1(bq)
            b_hold = False
        if b_hold and (a_tail or gA is None):
            b_hold = False
        if gB is not None and not b_hold:
            try:
                if next(gB) == "hold" and gA is not None and not a_tail:
                    b_hold = True
            except StopIteration:
                doneL1.add(bq); bq += 1; gB = None
        if gA is not None:
            try:
                if next(gA) == "tail":
                    a_tail = True
            except StopIteration:
                doneL0.add(a); a += 1; gA = None
    P.emit(nc)
    es.close()
    return nc


def _wl(w):
    n = w.shape[1]
    return np.ascontiguousarray(w.reshape(8, 128, n).transpose(1, 0, 2).reshape(128, 8 * n))


def _vl(v, nch):
    return np.ascontiguousarray(v.reshape(nch, 128).T)


def _bias_tables():
    k = np.arange(128)[:, None]
    q = np.arange(128)[None, :]
    slopes = 2.0 ** (-(np.arange(8) + 1.0))
    bp = np.full((128, 2, 8, 128), NEG, np.float32)
    bs = np.full((128, 2, 8, 128), NEG, np.float32)
    qi, qs = q // 16, q % 16
    ki, ks = k // 16, k % 16
    for h in range(8):
        sl = 8.0 * slopes[h]
        bp[:, 0, h, :] = np.where(q >= k, -sl * (q - k), NEG)
        bp[:, 1, h, :] = np.where(k > q, -sl * (q + 128 - k), NEG)
        bs[:, 0, h, :] = np.where((qs == ks) & (ki <= qi), -sl * (qi - ki), NEG)
        bs[:, 1, h, :] = np.where(k > qi, -sl * (128 + qi - k), NEG)
    return (bp.reshape(128, 2048).astype(ml_dtypes.bfloat16), bs.reshape(128, 2048).astype(ml_dtypes.bfloat16))


_NC_CACHE = {}


def kernel(x_prompt, x_sample, state_conv_a, state_conv_b, state_pool_c, cache_win_k, cache_win_v,
           c_prompt, c_sample, w_mod_e, b_mod_e, g_pre_e, g_post_e, w_in_e, conv_a_w, conv_b_w, conv_b_b,
           ln_b_g, ln_b_b, w_out_e, w_mod_o, b_mod_o, g_pre_o, g_post_o, w_in_o, pool_w, pool_scale,
           sinks, w_out_o):
    f32 = np.float32
    A = lambda a: np.asarray(a, dtype=f32)
    x_prompt, x_sample = A(x_prompt), A(x_sample)
    hp = np.array([(g * 4 + r) * 64 + d for r in range(4) for g in range(2) for d in range(64)])
    cols = np.concatenate([np.arange(0, 1024), 1024 + hp, np.arange(1536, 1792), 1792 + hp])
    wino = A(w_in_o)[0][:, cols]
    rows = np.concatenate([np.arange(0, 512), 512 + hp])
    wouto = A(w_out_o)[0][rows, :]
    shared = {
        "wmod_e": _wl(A(w_mod_e)[0]), "wmod_o": _wl(A(w_mod_o)[0]),
        "win_e": _wl(A(w_in_e)[0]), "wout_e": _wl(A(w_out_e)[0]),
        "win_o": _wl(wino), "wout_o": _wl(wouto),
        "identf": np.eye(128, dtype=f32),
        "poolw": np.ascontiguousarray(A(pool_w)[0].transpose(1, 0, 2).reshape(128, 512)),
    }
    shared["biasp"], shared["biass"] = _bias_tables()
    sm_base = np.zeros((128, NSM), f32)

    def put(name, arr):
        o, w = SM[name]
        sm_base[:, o:o + w] = arr

    put("bmod_e", _vl(A(b_mod_e)[0], 24)); put("bmod_o", _vl(A(b_mod_o)[0], 24))
    put("gpre_e", _vl(A(g_pre_e)[0], 8)); put("gpost_e", _vl(A(g_post_e)[0], 8))
    put("gpre_o", _vl(A(g_pre_o)[0], 8)); put("gpost_o", _vl(A(g_post_o)[0], 8))
    put("caw", A(conv_a_w)[0].reshape(3, 4, 128).transpose(2, 1, 0).reshape(128, 12))
    put("cbw", A(conv_b_w)[0].reshape(31, 4, 128).transpose(2, 1, 0).reshape(128, 124))
    put("cbb", _vl(A(conv_b_b)[0], 4)); put("lng", _vl(A(ln_b_g)[0], 4)); put("lnb", _vl(A(ln_b_b)[0], 4))
    put("psc", _vl(A(pool_scale)[0], 4))
    put("snk", np.repeat(A(sinks)[0].reshape(2, 4), 64, axis=0))
    fac = np.zeros((4, 16), f32)
    for g, w in enumerate(POOL_W):
        for t in range(16):
            fac[g, t] = w / min(w, t + 1) - 1.0
    in_maps = []
    for c in range(NCORES):
        b, hf = c // 2, c % 2
        xp = np.zeros((HALO + 2048, 1024), f32)
        if hf == 1:
            xp[:] = x_prompt[b, 2048 - HALO:4096]
        else:
            xp[HALO:] = x_prompt[b, 0:2048]
        sl = slice(16 * c, 16 * c + 16)
        xs = np.ascontiguousarray(x_sample[sl].transpose(1, 0, 2).reshape(128, 1024))
        call = np.concatenate([A(c_prompt)[b:b + 1], A(c_sample)[sl]], axis=0)
        cT = np.ascontiguousarray(call.reshape(17, 8, 128).transpose(2, 1, 0).reshape(128, 136))
        smc = sm_base.copy()
        o, w = SM["hm"]; smc[:, o] = float(hf)
        o, w = SM["facm1"]; smc[:, o:o + w] = (fac.reshape(1, 64) if hf == 0 else 0.0)
        m = dict(shared)
        m.update({
            "xp": xp, "xs": xs, "cT": cT, "small": smc,
            "sa": np.ascontiguousarray(A(state_conv_a)[0, sl].transpose(1, 0, 2).reshape(32, 512)),
            "sb": np.ascontiguousarray(A(state_conv_b)[0, sl].transpose(1, 0, 2).reshape(480, 512)),
            "sc": np.ascontiguousarray(A(state_pool_c)[0, sl].transpose(1, 0, 2).reshape(240, 512)),
            "ck": np.ascontiguousarray(A(cache_win_k)[0, sl].reshape(16, 128, 128).transpose(1, 0, 2).reshape(128, 2048)),
            "cv": np.ascontiguousarray(A(cache_win_v)[0, sl].reshape(16, 128, 128).transpose(1, 0, 2).reshape(128, 2048)),
        })
        in_maps.append(m)
    if "nc" not in _NC_CACHE:
        _NC_CACHE["nc"] = build_program()
    res = run_bass_kernel_spmd(_NC_CACHE["nc"], in_maps, core_ids=list(range(NCORES)))
    R = res.results
    y_prompt = np.zeros((4, 4096, 1024), f32); y_sample = np.zeros((128, 8, 1024), f32)
    pa = np.zeros((1, 4, 2, 512), f32); sa = np.zeros((1, 128, 2, 512), f32)
    pb = np.zeros((1, 4, 30, 512), f32); sbo = np.zeros((1, 128, 30, 512), f32)
    pc = np.zeros((1, 4, 15, 512), f32); sco = np.zeros((1, 128, 15, 512), f32)
    pk = np.zeros((1, 4, 128, 2, 64), f32); sk = np.zeros((1, 128, 128, 2, 64), f32)
    pv = np.zeros((1, 4, 128, 2, 64), f32); sv = np.zeros((1, 128, 128, 2, 64), f32)
    for c in range(NCORES):
        b, hf = c // 2, c % 2
        r = R[c]
        sl = slice(16 * c, 16 * c + 16)
        y_prompt[b, hf * 2048:(hf + 1) * 2048] = r["yp"]
        y_sample[sl] = r["ys"].reshape(8, 16, 1024).transpose(1, 0, 2)
        sa[0, sl] = r["sa_o"].reshape(2, 16, 512).transpose(1, 0, 2)
        sbo[0, sl] = r["sb_o"].reshape(30, 16, 512).transpose(1, 0, 2)
        sco[0, sl] = r["sc_o"].reshape(15, 16, 512).transpose(1, 0, 2)
        sk[0, sl] = r["sk_o"].reshape(128, 16, 2, 64).transpose(1, 0, 2, 3)
        sv[0, sl] = r["sv_o"].reshape(128, 16, 2, 64).transpose(1, 0, 2, 3)
        if hf == 1:
            pa[0, b] = r["pa"]; pb[0, b] = r["pb"]; pc[0, b] = r["pc"]
            pk[0, b] = r["pk"].reshape(128, 2, 64); pv[0, b] = r["pv"].reshape(128, 2, 64)
    return (y_prompt, y_sample, pa, sa, pb, sbo, pc, sco, pk, sk, pv, sv)
```

```python
import numpy as np
from contextlib import ExitStack
import concourse.bass as bass
import concourse.mybir as mybir
from concourse.bass_utils import run_bass_kernel_spmd

F32 = mybir.dt.float32
BF16 = mybir.dt.bfloat16
AF = mybir.ActivationFunctionType
ALU = mybir.AluOpType
AX = mybir.AxisListType

ENGS = ("pe", "act", "dve", "pool", "sp")


class Op:
    __slots__ = ("eng", "fn", "reads", "writes", "dma_key", "dma_k", "pos",
                 "waits", "signal", "sigval", "name")

    def __init__(self, eng, fn, reads, writes, dma_key, name):
        self.eng = eng
        self.fn = fn
        self.reads = tuple(reads)
        self.writes = tuple(writes)
        self.dma_key = dma_key
        self.dma_k = 0
        self.pos = 0
        self.waits = []
        self.signal = False
        self.sigval = 0
        self.name = name


class Prog:
    def __init__(self):
        self.ops = []
        self.dma_mode = {}
        self.final_keys = []
        self.barriers = set()

    def add(self, eng, fn, reads=(), writes=(), dma_key=None, name=""):
        op = Op(eng, fn, reads, writes, dma_key, name)
        self.ops.append(op)
        return op

    def barrier(self):
        self.barriers.add(len(self.ops))

    def pe(self, fn, reads=(), writes=(), name=""):
        return self.add("pe", fn, reads, writes, name=name)

    def act(self, fn, reads=(), writes=(), name=""):
        return self.add("act", fn, reads, writes, name=name)

    def dve(self, fn, reads=(), writes=(), name=""):
        return self.add("dve", fn, reads, writes, name=name)

    def pool(self, fn, reads=(), writes=(), name=""):
        return self.add("pool", fn, reads, writes, name=name)

    def dma(self, eng, key, fn, reads=(), writes=(), mode="slot", final=False, name=""):
        self.dma_mode.setdefault(key, mode)
        assert self.dma_mode[key] == mode
        if final and key not in self.final_keys:
            self.final_keys.append(key)
        return self.add(eng, fn, reads, writes, dma_key=key, name=name)

    def analyze(self):
        last_writer = {}
        readers = {}
        eng_pos = {e: 0 for e in ENGS}
        dma_cnt = {}
        dma_last = {}
        waited = {e: {} for e in ENGS}
        last_on = {}
        bar_ops = []
        for oi, op in enumerate(self.ops):
            if oi in self.barriers:
                bar_ops = list(last_on.values())
            op.pos = eng_pos[op.eng]
            eng_pos[op.eng] += 1
            raw = set()
            other = set(bar_ops)
            for r in op.reads:
                if r in last_writer:
                    raw.add(last_writer[r])
            for w in op.writes:
                if w in last_writer:
                    other.add(last_writer[w])
                for rd in readers.get(w, ()):
                    other.add(rd)
            if op.dma_key is not None:
                k = dma_cnt.get(op.dma_key, 0) + 1
                dma_cnt[op.dma_key] = k
                op.dma_k = k
                if self.dma_mode[op.dma_key] == "slot" and op.dma_key in dma_last:
                    other.add(dma_last[op.dma_key])
                dma_last[op.dma_key] = op
            need = {}
            for d in raw | other:
                if d is op:
                    continue
                if d.dma_key is not None:
                    sk = ("dma", d.dma_key)
                    v = d.dma_k if self.dma_mode[d.dma_key] == "slot" else -1
                    if sk not in need or (need[sk] != -1 and (v == -1 or v > need[sk])):
                        need[sk] = v
                    continue
                if d.eng == op.eng and op.dma_key is None:
                    if op.eng == "pe":
                        continue
                sk = ("eng", d.eng)
                if sk not in need or d.pos > need[sk].pos:
                    need[sk] = d
            for sk, v in need.items():
                op.waits.append((sk, v))
            for r in op.reads:
                readers.setdefault(r, []).append(op)
            for w in op.writes:
                last_writer[w] = op
                readers[w] = []
            last_on[(op.eng, op.dma_key)] = op
        self.dma_cnt = dma_cnt
        for op in self.ops:
            ws = []
            wd = waited[op.eng]
            for sk, v in op.waits:
                if sk[0] == "dma":
                    val = (self.dma_cnt[sk[1]] if v == -1 else v) * 16
                    if wd.get(sk, 0) >= val:
                        continue
                    wd[sk] = val
                    ws.append((sk, val))
                else:
                    if wd.get(sk, -1) >= v.pos:
                        continue
                    wd[sk] = v.pos
                    v.signal = True
                    ws.append((sk, v))
            op.waits = ws
        cnt = {e: 0 for e in ENGS}
        for op in self.ops:
            if op.signal:
                cnt[op.eng] += 1
                op.sigval = cnt[op.eng]
        self.sig_cnt = cnt

    def emit(self, nc):
        self.analyze()
        with ExitStack() as es:
            sems = {}
            for e in ENGS:
                if self.sig_cnt[e] > 0:
                    sems[("eng", e)] = es.enter_context(nc.semaphore(f"s_{e}"))
            for i, key in enumerate(self.dma_cnt):
                sems[("dma", key)] = es.enter_context(nc.semaphore(f"d_{i}"))
            block = es.enter_context(nc.Block())
            streams = {e: [op for op in self.ops if op.eng == e] for e in ENGS}

            def run(eng, ename):
                for op in streams[ename]:
                    for sk, v in op.waits:
                        if sk[0] == "dma":
                            eng.wait_ge(sems[sk], v)
                        else:
                            eng.wait_ge(sems[sk], v.sigval)
                    ins = op.fn(eng)
                    if op.dma_key is not None:
                        ins.then_inc(sems[("dma", op.dma_key)], 16)
                    elif op.signal:
                        ins.then_inc(sems[("eng", ename)], 1)
                if ename == "sp":
                    for key in self.final_keys:
                        eng.wait_ge(sems[("dma", key)], self.dma_cnt[key] * 16)

            @block.tensor
            def _(eng):
                run(eng, "pe")

            @block.scalar
            def _(eng):
                run(eng, "act")

            @block.vector
            def _(eng):
                run(eng, "dve")

            @block.gpsimd
            def _(eng):
                run(eng, "pool")

            @block.sync
            def _(eng):
                run(eng, "sp")


import ml_dtypes

NCORES = 8
HALO = 256
BT = 256
NBLK_P = (HALO + 2048) // BT
POOL_W = (2, 4, 8, 16)
NEG = -240000.0

SM = {}
_o = 0
for _n, _w in (("bmod_e", 24), ("bmod_o", 24), ("gpre_e", 8), ("gpost_e", 8), ("gpre_o", 8), ("gpost_o", 8),
               ("caw", 12), ("cbw", 124), ("cbb", 4), ("lng", 4), ("lnb", 4), ("psc", 4), ("snk", 4),
               ("hm", 1), ("facm1", 64)):
    SM[_n] = (_o, _w)
    _o += _w
NSM = _o


def build_program():
    nc = bass.Bass("TRN2", target_bir_lowering=False)
    P = Prog()
    es = ExitStack()

    def din(name, shape, dt=F32):
        return nc.dram_tensor(name, list(shape), dt, kind="ExternalInput").ap()

    def dout(name, shape, dt=F32):
        return nc.dram_tensor(name, list(shape), dt, kind="ExternalOutput").ap()

    def sb(name, shape, dt=F32):
        return es.enter_context(nc.sbuf_tensor("sb_" + name, list(shape), dt))

    xp = din("xp", [HALO + 2048, 1024]); xs = din("xs", [128, 1024])
    cT_d = din("cT", [128, 8 * 17]); small_d = din("small", [128, NSM]); poolw_d = din("poolw", [128, 512])
    identf_d = din("identf", [128, 128]); biasp_d = din("biasp", [128, 2048], BF16); biass_d = din("biass", [128, 2048], BF16)
    sa_d = din("sa", [32, 512]); sb_d = din("sb", [480, 512]); sc_d = din("sc", [240, 512])
    ck_d = din("ck", [128, 16 * 128]); cv_d = din("cv", [128, 16 * 128])
    wmod_d = [din("wmod_e", [128, 8 * 3072]), din("wmod_o", [128, 8 * 3072])]
    win_e_d = din("win_e", [128, 8 * 3584]); wout_e_d = din("wout_e", [128, 8 * 1024])
    win_o_d = din("win_o", [128, 8 * 2304]); wout_o_d = din("wout_o", [128, 8 * 1024])
    wout_d = [wout_e_d, wout_o_d]
    wout_bf = [nc.dram_tensor(f"wout_bf{l}", [128, 8 * 1024], BF16, kind="Internal").ap() for l in range(2)]
    yp_o = dout("yp", [2048, 1024]); ys_o = dout("ys", [128, 1024])
    pa_o = dout("pa", [2, 512]); sa_o = dout("sa_o", [32, 512])
    pb_o = dout("pb", [30, 512]); sb_o = dout("sb_o", [480, 512])
    pc_o = dout("pc", [15, 512]); sc_o = dout("sc_o", [240, 512])
    pk_o = dout("pk", [128, 128]); sk_o = dout("sk_o", [2048, 128])
    pv_o = dout("pv", [128, 128]); sv_o = dout("sv_o", [2048, 128])

    WE = sb("WE", [128, 8 * 3584], BF16); WOi = sb("WOi", [128, 8 * 2304], BF16); WOUT = sb("WOUT", [128, 8 * 1024], BF16)
    win_e = WE[:].rearrange("p (k n) -> p k n", k=8)
    win_o = WOi[:].rearrange("p (k n) -> p k n", k=8)
    wout3 = WOUT[:].rearrange("p (k n) -> p k n", k=8)
    identf = sb("identf", [128, 128]); identb = sb("identb", [128, 128], BF16); onesb = sb("onesb", [128, 128], BF16)
    biasT = sb("biasT", [128, 2048], BF16)
    small = sb("small", [128, NSM])
    Wp = sb("Wp", [128, 8 * 128], BF16)
    diag3 = sb("diag3", [128, 12 * 128], BF16)
    wb2 = sb("wb2", [128, 124], BF16); b1sc = sb("b1sc", [128, 16]); esk = sb("esk", [128, 4]); epsc = sb("epsc", [128, 2])
    modv = sb("modv", [128, 2 * 3 * 8 * 17])
    xst = [sb(f"xst{i}", [128, 512]) for i in range(2)]
    NXT = 3
    xTs = [sb(f"xT{i}", [128, 8 * BT]) for i in range(NXT)]
    tmp = [None, None] + [sb(f"tmp{i}", [128, BT]) for i in range(2, 7)]
    PT = sb("PT", [128, 2 * 1024], BF16)
    UB = sb("UB", [128, 4928], BF16)
    NDG = 3
    dg = sb("dg", [128, NDG * 1024], BF16)
    ps = [es.enter_context(nc.psum_tensor(f"ps{i}", [128, 512], F32)) for i in range(8)]

    class Ctx:
        pass

    def mkctx(n, nslots, tmps):
        cx = Ctx()
        cx.n = n
        cx.hT = sb(n + "hT", [128, 8 * BT], BF16)
        cx.hT3 = cx.hT[:].rearrange("p (k t) -> p k t", k=8)
        cx.sqmix = sb(n + "sqmix", [128, 8 * BT], BF16)
        cx.sq3 = cx.sqmix[:].rearrange("p (k t) -> p k t", k=8)
        cx.rstd = sb(n + "rstd", [128, BT])
        cx.scr = sb(n + "scr", [128, nslots * BT])
        cx.sqB3 = cx.scr[:, 0:4 * BT].bitcast(BF16).rearrange("p (k t) -> p k t", k=8)
        cx.tmp = tmps
        cx.kr = n + "rstd"
        cx.khc = lambda c: f"{n}hT{c}"
        cx.ksc = lambda c: f"{n}sq{c}"
        cx.kh_all = [f"{n}hT{c}" for c in range(8)]
        cx.ks_all = [f"{n}sq{c}" for c in range(8)]
        cx.sk = lambda i: f"{n}scr{i}"
        return cx

    C0 = mkctx("a", 8, [(tmp[3], "tmp3"), (tmp[4], "tmp4"), (tmp[2], "tmp2"), (tmp[3], "tmp3"), (tmp[4], "tmp4")])
    C1 = mkctx("b", 6, [(tmp[5], "tmp5"), (tmp[6], "tmp6")])
    acc3 = C0.scr[:, 0:4 * BT].rearrange("p (k t) -> p k t", k=4)
    ybs = C0.scr[:, 4 * BT:8 * BT].bitcast(BF16)
    ybb3 = ybs[:, 0:4 * BT].rearrange("p (k t) -> p k t", k=4)
    ysq3 = ybs[:, 4 * BT:8 * BT].rearrange("p (k t) -> p k t", k=4)
    dn = C1.scr[:, 0:512]
    sgd3 = C1.scr[:, 2 * BT:4 * BT].bitcast(BF16).rearrange("p (k t) -> p k t", k=4)
    qT3 = C1.scr[:, 4 * BT:6 * BT].bitcast(BF16).rearrange("p (k t) -> p k t", k=4)

    def sm(name, a=0, b=None):
        o, w = SM[name]
        return small[:, o + a:o + (w if b is None else b)]

    def mv(l, kind, kc, b0, b1):
        o = ((l * 3 + kind) * 8 + kc) * 17
        return modv[:, o + b0:o + b1]

    def xT3(par):
        return xTs[par][:].rearrange("p (k t) -> p k t", k=8)

    def xkc(par, c):
        return f"xT{par}_{c}"

    def xk_all(par):
        return [f"xT{par}_{c}" for c in range(8)]

    class Bufs:
        pass

    def carve(kind):
        b = Bufs()
        if kind == "p":
            b.ts = 1
            b.axc = UB[:, 0:1032].rearrange("p (c t) -> p c t", c=4)
            b.ub = UB[:, 1032:2176].rearrange("p (c t) -> p c t", c=4)
            b.cu = UB[:, 2432:3516].rearrange("p (c t) -> p c t", c=4)
            b.kT = UB[:, 3516:3900]
            b.vt = UB[:, 3900:4284].rearrange("p (t d) -> p t d", t=3)
            b.key = "ubp"
        else:
            b.ts = 16
            b.ub = UB[:, 0:2432].rearrange("p (c t) -> p c t", c=4)
            b.axc = UB[:, 4284:4924].rearrange("p (c t) -> p c t", c=4)
            b.cu = UB[:, 2432:3904].rearrange("p (c t) -> p c t", c=4)
            b.kT = UB[:, 3904:4032]
            b.vt = UB[:, 4032:4160].rearrange("p (t d) -> p t d", t=1)
            b.key = "ubs"
        return b

    bank_i = [0]
    dg_cnt = [0]
    xin_cnt = [0]

    CONV_BANK = 7

    def nb():
        i = bank_i[0] % 7
        bank_i[0] += 1
        return i

    def stage_slot():
        i = xin_cnt[0] % 2
        xin_cnt[0] += 1
        return i

    def ld(eng, key, out, in_, w, mode="group"):
        P.dma(eng, key, lambda e: e.dma_start(out=out, in_=in_), writes=w, mode=mode)

    cT = tmp[4][:, 0:136]
    scT = tmp[3][:, 0:68].bitcast(BF16)
    poolw = xst[0][:, 0:512]
    ld("sp", "c0", small[:], small_d, ["small"]); ld("sp", "c0", cT, cT_d, ["tmp4"])
    ld("sp", "c0", identf[:], identf_d, ["identf"]); ld("sp", "c0", poolw, poolw_d, ["xst0"])
    ld("sp", "c0", biasT[:], biasp_d, ["biasT"])

    def load_wpiece(src_d, ncols_total, c0, c1, slot, key):
        P.dma("pool", key, lambda e: e.dma_start(out=wout3[:, :, slot * 256:slot * 256 + (c1 - c0)],
                                                 in_=src_d.rearrange("p (k n) -> p k n", k=8)[:, :, c0:c1]), writes=[f"WOUT{slot}"], mode="slot")

    P.act(lambda e: e.copy(out=identb[:], in_=identf[:]), reads=["identf"], writes=["identb"])
    P.dve(lambda e: e.memset(onesb[:], 1.0), writes=["onesb"])
    P.dve(lambda e: e.memset(epsc[:, 0:1], 1e-6), writes=["epsc"])
    P.dve(lambda e: e.memset(epsc[:, 1:2], 1e-5), writes=["epsc"])
    P.act(lambda e: e.activation(out=scT, in_=cT, func=AF.Silu), reads=["tmp4"], writes=["tmp3"])
    P.act(lambda e: e.activation(out=esk[:], in_=sm("snk"), func=AF.Exp), reads=["small"], writes=["esk"])
    P.dve(lambda e: e.tensor_scalar(out=wb2[:], in0=sm("cbw"), scalar1=0.5, scalar2=None, op0=ALU.mult), reads=["small"], writes=["wb2"])
    for c in range(4):
        for j in range(3):
            P.dve(lambda e, c=c, j=j: e.tensor_scalar(out=diag3[:, (c * 3 + j) * 128:(c * 3 + j + 1) * 128], in0=identb[:],
                                                      scalar1=sm("caw", c * 3 + j, c * 3 + j + 1), scalar2=None, op0=ALU.mult),
                  reads=["identb", "small"], writes=["diag3"])
    for g, w in enumerate(POOL_W):
        P.dve(lambda e, g=g, w=w: e.tensor_scalar(out=Wp[:, (2 * g) * 128:(2 * g + 1) * 128], in0=xst[0][:, g * 128:(g + 1) * 128],
                                                  scalar1=(1.0 / w - 1.0), scalar2=None, op0=ALU.mult), reads=["xst0"], writes=["Wp"])
        P.dve(lambda e, g=g, w=w: e.tensor_scalar(out=Wp[:, (2 * g + 1) * 128:(2 * g + 2) * 128], in0=xst[0][:, g * 128:(g + 1) * 128],
                                                  scalar1=(1.0 / w), scalar2=None, op0=ALU.mult), reads=["xst0"], writes=["Wp"])
    scT3 = scT.rearrange("p (k b) -> p k b", k=8)

    def do_mod(l):
        bm_, gpre, gpost = (("bmod_e", "gpre_e", "gpost_e"), ("bmod_o", "gpre_o", "gpost_o"))[l]
        P.dve(lambda e: e.tensor_scalar(out=b1sc[:, l * 8:(l + 1) * 8], in0=sm(bm_, 8, 16), scalar1=1.0, scalar2=None, op0=ALU.add),
              reads=["small"], writes=["b1sc"])
        for pc_ in range(12):
            slot = pc_ % 4
            load_wpiece(wmod_d[l], 3072, pc_ * 256, (pc_ + 1) * 256, slot, f"wmod{slot}")
            for q in range(2):
                fj = pc_ * 2 + q
                bi_ = nb()
                for kc in range(8):
                    P.pe(lambda e, bi_=bi_, q=q, kc=kc, slot=slot: e.matmul(ps[bi_][:, 0:17], lhsT=wout3[:, kc, slot * 256 + q * 128:slot * 256 + (q + 1) * 128],
                                                                         rhs=scT3[:, kc, :], start=(kc == 0), stop=(kc == 7)),
                         reads=[f"WOUT{slot}", "tmp3"], writes=[f"ps{bi_}"])
                j = fj % 8
                if fj < 8:
                    P.act(lambda e, bi_=bi_, fj=fj, j=j: e.activation(out=mv(l, 0, j, 0, 17), in_=ps[bi_][:, 0:17], func=AF.Identity,
                                                                    bias=sm(bm_, fj, fj + 1), scale=1.0), reads=[f"ps{bi_}", "small"], writes=["modv"])
                elif fj < 16:
                    P.dve(lambda e, bi_=bi_, j=j: e.tensor_scalar(out=mv(l, 1, j, 0, 17), in0=ps[bi_][:, 0:17], scalar1=b1sc[:, l * 8 + j:l * 8 + j + 1],
                                                                scalar2=sm(gpre, j, j + 1), op0=ALU.add, op1=ALU.mult),
                          reads=[f"ps{bi_}", "small", "b1sc"], writes=["modv"])
                else:
                    P.dve(lambda e, bi_=bi_, fj=fj, j=j: e.tensor_scalar(out=mv(l, 2, j, 0, 17), in0=ps[bi_][:, 0:17], scalar1=sm(bm_, fj, fj + 1),
                                                                       scalar2=sm(gpost, j, j + 1), op0=ALU.add, op1=ALU.mult),
                          reads=[f"ps{bi_}", "small"], writes=["modv"])

    do_mod(0)
    do_mod(1)
    for i in range(7):
        ld("pool", f"we{i}", win_e[:, :, i * 512:(i + 1) * 512],
           win_e_d.rearrange("p (k n) -> p k n", k=8)[:, :, i * 512:(i + 1) * 512], [f"WE{i}"])
    P.dma("pool", "wcast0", lambda e: e.dma_start(out=wout_bf[0], in_=wout_d[0]), writes=["woutbf0"], mode="slot")
    ld("pool", "wo_i", win_o, win_o_d.rearrange("p (k n) -> p k n", k=8), ["WOi"])
    P.dma("pool", "wcast1", lambda e: e.dma_start(out=wout_bf[1], in_=wout_d[1]), writes=["woutbf1"], mode="slot")
    P.dma("sp", "d2d", lambda e: e.dma_start(out=sb_o[0:22 * 16, :], in_=sb_d[8 * 16:30 * 16, :]), final=True)
    P.dma("sp", "d2d", lambda e: e.dma_start(out=sc_o[0:7 * 16, :], in_=sc_d[8 * 16:15 * 16, :]), final=True)
    P.dma("sp", "d2d", lambda e: e.dma_start(out=sk_o[0:120 * 16, :], in_=ck_d.rearrange("p (s d) -> (p s) d", s=16)[8 * 16:128 * 16, :]), final=True)
    P.dma("sp", "d2d", lambda e: e.dma_start(out=sv_o[0:120 * 16, :], in_=cv_d.rearrange("p (s d) -> (p s) d", s=16)[8 * 16:128 * 16, :]), final=True)

    def load_x(src_rows, par, tcol):
        x3 = xT3(par)
        for h in range(2):
            s = stage_slot()
            P.dma("sp", f"xst{s}", lambda e, s=s, h=h: e.dma_start(out=xst[s][:], in_=src_rows[:, h * 512:(h + 1) * 512]), writes=[f"xst{s}"])
            bk = nb()
            for q in range(4):
                P.pe(lambda e, bk=bk, q=q, s=s: e.transpose(out=ps[bk][:, q * 128:(q + 1) * 128], in_=xst[s][:, q * 128:(q + 1) * 128], identity=identf[:]),
                     reads=[f"xst{s}", "identf"], writes=[f"ps{bk}"])
            P.act(lambda e, bk=bk, h=h: e.copy(out=x3[:, h * 4:(h + 1) * 4, tcol:tcol + 128], in_=ps[bk][:].rearrange("p (q t) -> p q t", q=4)),
                  reads=[f"ps{bk}"], writes=[xkc(par, h * 4 + q_) for q_ in range(4)])

    def store_y(dst_rows, par, tcol):
        x3 = xT3(par)
        for h in range(2):
            s = stage_slot()
            bk = nb()
            for q in range(4):
                kc = h * 4 + q
                P.pe(lambda e, bk=bk, q=q, kc=kc: e.transpose(out=ps[bk][:, q * 128:(q + 1) * 128], in_=x3[:, kc, tcol:tcol + 128], identity=identf[:]),
                     reads=[xkc(par, kc), "identf"], writes=[f"ps{bk}"])
            P.act(lambda e, bk=bk, s=s: e.copy(out=xst[s][:], in_=ps[bk][:]), reads=[f"ps{bk}"], writes=[f"xst{s}"])
            P.dma("sp", f"xst{s}", lambda e, s=s, h=h: e.dma_start(out=dst_rows[:, h * 512:(h + 1) * 512], in_=xst[s][:]), reads=[f"xst{s}"], final=True)

    def stats_tail(cx, sqv3, sqkeys, nt):
        bk = nb()
        for kc in range(8):
            P.pe(lambda e, kc=kc: e.matmul(ps[bk][:, 0:nt], lhsT=onesb[:], rhs=sqv3[:, kc, 0:nt], start=(kc == 0), stop=(kc == 7)),
                 reads=[sqkeys[kc] if len(sqkeys) == 8 else sqkeys[kc // 2], "onesb"], writes=[f"ps{bk}"])
        P.act(lambda e: e.activation(out=cx.rstd[:, 0:nt], in_=ps[bk][:, 0:nt], func=AF.Sqrt, bias=epsc[:, 0:1], scale=1.0 / 1024),
              reads=[f"ps{bk}", "epsc"], writes=[cx.kr])
        P.dve(lambda e: e.reciprocal(out=cx.rstd[:, 0:nt], in_=cx.rstd[:, 0:nt]), reads=[cx.kr], writes=[cx.kr])

    def bc_mod(l, kind, kc):
        return mv(l, kind, kc, 1, 17).unsqueeze(1).broadcast_to([128, 8, 16])

    def tok3(ap2):
        return ap2.rearrange("p (i s) -> p i s", s=16)

    def prenorm(cx, l, nt, sample, par):
        x3 = xT3(par)
        for hh in range(2):
            P.act(lambda e, hh=hh: e.activation(out=cx.sq3[:, hh * 4:(hh + 1) * 4, 0:nt], in_=x3[:, hh * 4:(hh + 1) * 4, 0:nt], func=AF.Square),
                  reads=xk_all(par)[hh * 4:(hh + 1) * 4], writes=cx.ks_all[hh * 4:(hh + 1) * 4])
        yield
        stats_tail(cx, cx.sq3, cx.ks_all, nt)
        yield
        for kc in range(8):
            t, tk = cx.tmp[kc % 2]
            if not sample:
                P.dve(lambda e, kc=kc, t=t: e.scalar_tensor_tensor(out=t[:, 0:nt], in0=x3[:, kc, 0:nt], scalar=mv(l, 1, kc, 0, 1), in1=cx.rstd[:, 0:nt],
                                                                   op0=ALU.mult, op1=ALU.mult), reads=[xkc(par, kc), cx.kr, "modv"], writes=[tk])
                P.act(lambda e, kc=kc, t=t: e.activation(out=cx.hT3[:, kc, 0:nt], in_=t[:, 0:nt], func=AF.Identity, bias=mv(l, 0, kc, 0, 1), scale=1.0),
                      reads=[tk, "modv"], writes=[cx.khc(kc)])
            else:
                P.dve(lambda e, kc=kc, t=t: e.tensor_tensor(out=t[:, 0:nt], in0=x3[:, kc, 0:nt], in1=cx.rstd[:, 0:nt], op=ALU.mult), reads=[xkc(par, kc), cx.kr], writes=[tk])
                P.dve(lambda e, kc=kc, t=t: e.tensor_tensor(out=tok3(t[:, 0:nt]), in0=tok3(t[:, 0:nt]), in1=bc_mod(l, 1, kc), op=ALU.mult),
                      reads=[tk, "modv"], writes=[tk])
                P.dve(lambda e, kc=kc, t=t: e.tensor_tensor(out=tok3(cx.hT3[:, kc, 0:nt]), in0=tok3(t[:, 0:nt]), in1=bc_mod(l, 0, kc), op=ALU.add),
                      reads=[tk, "modv"], writes=[cx.khc(kc)])
        yield

    def group(cx, W3, wkey, col0, nt, ncols=128):
        bk = nb()
        for kc in range(8):
            P.pe(lambda e, kc=kc: e.matmul(ps[bk][0:ncols, 0:nt], lhsT=W3[:, kc, col0:col0 + ncols], rhs=cx.hT3[:, kc, 0:nt], start=(kc == 0), stop=(kc == 7)),
                 reads=[wkey, cx.khc(kc)], writes=[f"ps{bk}"])
        return bk

    def load_wout_slot(l, sl):
        P.dma("sp", f"wout{sl}", lambda e: e.dma_start(out=wout3[:, :, sl * 256:(sl + 1) * 256],
                                                      in_=wout_bf[l].rearrange("p (k n) -> p k n", k=8)[:, :, sl * 256:(sl + 1) * 256]),
              reads=[f"woutbf{l}"], writes=[f"WOUT{sl}"], mode="slot")

    wout_lock = [None] * 4

    def try_wout(cx, l, st):
        for sl in range(4):
            if not st["held"][sl] and wout_lock[sl] is None:
                wout_lock[sl] = cx.n
                load_wout_slot(l, sl)
                st["held"][sl] = True

    def out_proj(cx, l, nt, sample, par, st):
        x3 = xT3(par)
        while not all(st["held"]):
            try_wout(cx, l, st)
            if not all(st["held"]):
                yield
        for dc in range(8):
            bk = nb()
            for kc in range(8):
                P.pe(lambda e, kc=kc, dc=dc, bk=bk: e.matmul(ps[bk][:, 0:nt], lhsT=wout3[:, kc, dc * 128:(dc + 1) * 128], rhs=cx.sq3[:, kc, 0:nt], start=(kc == 0), stop=(kc == 7)),
                     reads=[f"WOUT{dc // 2}", cx.ksc(kc)], writes=[f"ps{bk}"])
            P.act(lambda e, dc=dc, bk=bk: e.copy(out=cx.hT3[:, dc, 0:nt], in_=ps[bk][:, 0:nt]), reads=[f"ps{bk}"], writes=[cx.khc(dc)])
            P.act(lambda e, dc=dc, bk=bk: e.activation(out=cx.sqB3[:, dc, 0:nt], in_=ps[bk][:, 0:nt], func=AF.Square), reads=[f"ps{bk}"], writes=[cx.sk(dc // 2)])
            if dc % 2 == 1:
                wout_lock[dc // 2] = None
            yield
        stats_tail(cx, cx.sqB3, [cx.sk(i) for i in range(4)], nt)
        yield
        for dc in range(8):
            t, tk = cx.tmp[dc % 2]
            if not sample:
                P.dve(lambda e, dc=dc, t=t: e.scalar_tensor_tensor(out=t[:, 0:nt], in0=cx.hT3[:, dc, 0:nt], scalar=mv(l, 2, dc, 0, 1), in1=cx.rstd[:, 0:nt],
                                                                   op0=ALU.mult, op1=ALU.mult), reads=[cx.khc(dc), cx.kr, "modv"], writes=[tk])
            else:
                P.dve(lambda e, dc=dc, t=t: e.tensor_tensor(out=t[:, 0:nt], in0=cx.hT3[:, dc, 0:nt], in1=cx.rstd[:, 0:nt], op=ALU.mult), reads=[cx.khc(dc), cx.kr], writes=[tk])
                P.dve(lambda e, dc=dc, t=t: e.tensor_tensor(out=tok3(t[:, 0:nt]), in0=tok3(t[:, 0:nt]), in1=bc_mod(l, 2, dc), op=ALU.mult),
                      reads=[tk, "modv"], writes=[tk])
            P.pool(lambda e, dc=dc, t=t: e.tensor_tensor(out=x3[:, dc, 0:nt], in0=x3[:, dc, 0:nt], in1=t[:, 0:nt], op=ALU.add), reads=[xkc(par, dc), tk], writes=[xkc(par, dc)])
        yield

    def ld_(ap, a, b):
        return ap[:, :, a:b] if len(ap.shape) == 3 else ap[:, a:b]

    def carry(buf, S, L, first, use_hm, keys):
        if first:
            P.pool(lambda e: e.memset(ld_(buf, 0, S), 0.0), writes=keys)
        elif use_hm:
            P.act(lambda e: e.activation(out=ld_(buf, 0, S), in_=ld_(buf, L, L + S), func=AF.Copy, scale=sm("hm")),
                  reads=keys + ["small"], writes=keys)
        else:
            P.pool(lambda e: e.tensor_copy(out=ld_(buf, 0, S), in_=ld_(buf, L, L + S)), reads=keys, writes=keys)

    def state_out(src3, nch, tcol0, srckeys, dmas, scale=1.0):
        s = stage_slot()
        bk = nb()
        pb = ps[bk][:].bitcast(BF16)
        for c in range(nch):
            P.pe(lambda e, c=c: e.transpose(out=pb[:, c * 128:(c + 1) * 128], in_=src3[:, c, tcol0:tcol0 + 128], identity=identb[:]),
                 reads=list(srckeys) + ["identb"], writes=[f"ps{bk}"])
        P.act(lambda e: e.activation(out=xst[s][:, 0:nch * 128], in_=pb[:, 0:nch * 128], func=AF.Copy, scale=scale), reads=[f"ps{bk}"], writes=[f"xst{s}"])
        for (dst, r0, r1) in dmas:
            P.dma("sp", f"xst{s}", lambda e, dst=dst, r0=r0, r1=r1: e.dma_start(out=dst, in_=xst[s][r0:r1, 0:nch * 128]), reads=[f"xst{s}"], final=True)

    def blk(bi):
        b = Bufs()
        b.sample = (bi == NBLK_P)
        b.nt = 128 if b.sample else BT
        b.ntile = b.nt // 128
        b.par = bi % NXT
        b.B = carve("s" if b.sample else "p")
        b.first, b.use_hm, b.last_p, b.halo = (bi == 0), (bi == 1), (bi == NBLK_P - 1), (bi == 0)
        k = b.B.key
        b.kA, b.kB, b.kC, b.kK, b.kV = k + "A", k + "B", k + "C", k + "K", k + "V"
        b.kBc = [k + "B" + str(c) for c in range(4)]
        pL0 = ["ubpA", "ubpB"] + ["ubpB" + str(c) for c in range(4)]
        pL1 = ["ubpC", "ubpK", "ubpV"]
        b.wA = [b.kA]
        b.wB = [[b.kBc[c]] + (pL0 if b.sample else []) for c in range(4)]
        b.wC = [b.kC] + (pL1 if b.sample else [])
        b.wK = [b.kK] + (pL1 if b.sample else [])
        b.wV = [b.kV] + (pL1 if b.sample else [])
        return b

    def load_state(src_d, nrows, dst3, col0, dkeys):
        r = 0
        while r < nrows:
            n = min(128, nrows - r)
            s = stage_slot()
            P.dma("sp", f"xst{s}", lambda e, r=r, n=n, s=s: e.dma_start(out=xst[s][0:n, 0:512], in_=src_d[r:r + n, :]), writes=[f"xst{s}"])
            bk = nb()
            for c in range(4):
                P.pe(lambda e, c=c, n=n, s=s, bk=bk: e.transpose(out=ps[bk][:, c * 128:c * 128 + n], in_=xst[s][0:n, c * 128:(c + 1) * 128], identity=identf[0:n, 0:n]),
                     reads=[f"xst{s}", "identf"], writes=[f"ps{bk}"])
            P.act(lambda e, r=r, n=n, bk=bk: e.copy(out=dst3[:, :, col0 + r:col0 + r + n], in_=ps[bk][:].rearrange("p (c t) -> p c t", c=4)[:, :, 0:n]),
                  reads=[f"ps{bk}"], writes=dkeys)
            r += n

    xoi = (NBLK_P + 1) % NXT
    xo = xTs[xoi][:].bitcast(BF16)
    kTc = xo[:, 0:2048].rearrange("p (s t) -> p s t", s=16)
    Vc = xo[:, 2048:4096].rearrange("p (s d) -> p s d", s=16)
    xok_all = xk_all(xoi)

    def sample_loads_L0(b):
        pL0 = ["ubpA", "ubpB"] + ["ubpB" + str(c) for c in range(4)]
        load_state(sa_d, 32, b.B.axc, 0, ["ubsA"])
        load_state(sb_d, 480, b.B.ub, 0, ["ubsB"] + pL0)
        P.act(lambda e: e.activation(out=b.B.ub[:, :, 0:480], in_=b.B.ub[:, :, 0:480], func=AF.Copy, scale=2.0), reads=["ubsB"], writes=["ubsB"])

    def sample_loads_L1(b):
        pL1 = ["ubpC", "ubpK", "ubpV"]
        ld("sp", "c1", biasT[:], biass_d, ["biasT"], mode="slot")
        load_state(sc_d, 240, b.B.cu, 0, ["ubsC"] + pL1)
        for s_ in range(0, 16, 4):
            sl = stage_slot()
            P.dma("sp", f"xst{sl}", lambda e, s_=s_, sl=sl: e.dma_start(out=xst[sl][:, :], in_=ck_d[:, s_ * 128:(s_ + 4) * 128]), writes=[f"xst{sl}"])
            bk = nb()
            for q in range(4):
                P.pe(lambda e, q=q, sl=sl, bk=bk: e.transpose(out=ps[bk][:, q * 128:(q + 1) * 128], in_=xst[sl][:, q * 128:(q + 1) * 128], identity=identf[:]),
                     reads=[f"xst{sl}", "identf"], writes=[f"ps{bk}"])
            P.act(lambda e, s_=s_, bk=bk: e.copy(out=kTc[:, s_:s_ + 4, :], in_=ps[bk][:].rearrange("p (q t) -> p q t", q=4)), reads=[f"ps{bk}"], writes=["kTc"] + xok_all)
            sl = stage_slot()
            P.dma("sp", f"xst{sl}", lambda e, s_=s_, sl=sl: e.dma_start(out=xst[sl][:, :], in_=cv_d[:, s_ * 128:(s_ + 4) * 128]), writes=[f"xst{sl}"])
            P.act(lambda e, s_=s_, sl=sl: e.copy(out=Vc[:, s_:s_ + 4, :], in_=xst[sl][:].rearrange("p (s d) -> p s d", s=4)), reads=[f"xst{sl}"], writes=["Vc"] + xok_all)

    def gen_L0(bi):
        b = blk(bi)
        cx, B, nt, sample, par, ts = C0, b.B, b.nt, b.sample, b.par, b.B.ts
        st = {"held": [False] * 4}
        if sample:
            sample_loads_L0(b)
            yield
        for t in range(b.ntile):
            load_x(xs if sample else xp[bi * BT + t * 128: bi * BT + (t + 1) * 128, :], par, t * 128)
            yield
        if not sample:
            carry(B.axc, 2, BT, b.first, b.use_hm, [b.kA])
            carry(B.ub, 30, BT, b.first, b.use_hm, [b.kB] + b.kBc)
        yield from prenorm(cx, 0, nt, sample, par)
        t2, k2 = cx.tmp[2]; t3, k3 = cx.tmp[3]; t4, k4 = cx.tmp[4]
        for c in range(4):
            b1 = group(cx, win_e, "WE0", 0 * 512 + c * 128, nt)
            P.act(lambda e, b1=b1: e.copy(out=t2[:, 0:nt], in_=ps[b1][:, 0:nt]), reads=[f"ps{b1}"], writes=[k2])
            b2 = group(cx, win_e, "WE2", 2 * 512 + c * 128, nt)
            P.dve(lambda e, b2=b2, c=c: e.tensor_tensor(out=B.axc[:, c, 2 * ts:2 * ts + nt], in0=ps[b2][:, 0:nt], in1=t2[:, 0:nt], op=ALU.mult),
                  reads=[f"ps{b2}", k2], writes=[b.kA])
            yield
            b3 = group(cx, win_e, "WE1", 1 * 512 + c * 128, nt)
            b4 = group(cx, win_e, "WE3", 3 * 512 + c * 128, nt)
            P.act(lambda e, b4=b4: e.activation(out=t3[:, 0:nt], in_=ps[b4][:, 0:nt], func=AF.Silu), reads=[f"ps{b4}"], writes=[k3])
            P.dve(lambda e, b3=b3: e.tensor_tensor(out=t4[:, 0:nt], in0=ps[b3][:, 0:nt], in1=t3[:, 0:nt], op=ALU.mult), reads=[f"ps{b3}", k3], writes=[k4])
            yield
            b5 = nb()
            for j in range(3):
                P.pe(lambda e, j=j, c=c, b5=b5: e.matmul(ps[b5][:, 0:nt], lhsT=diag3[:, (c * 3 + j) * 128:(c * 3 + j + 1) * 128],
                                                        rhs=B.axc[:, c, j * ts:j * ts + nt], start=(j == 0), stop=(j == 2)), reads=["diag3", b.kA], writes=[f"ps{b5}"])
            P.dve(lambda e, b5=b5, c=c: e.tensor_tensor(out=cx.sq3[:, c, 0:nt], in0=ps[b5][:, 0:nt], in1=t4[:, 0:nt], op=ALU.mult),
                  reads=[f"ps{b5}", k4], writes=[cx.ksc(c)])
            yield
        if b.last_p or sample:
            state_out(B.axc, 4, 2 * ts + nt - 128, [b.kA], [(sa_o[0:32, :], 96, 128)] if sample else [(pa_o[:, :], 126, 128)])
        for c in range(4):
            bv = group(cx, win_e, "WE4", 4 * 512 + c * 128, nt)
            bg = group(cx, win_e, "WE5", 5 * 512 + c * 128, nt)
            P.act(lambda e, bg=bg: e.activation(out=t2[:, 0:nt], in_=ps[bg][:, 0:nt], func=AF.Tanh, scale=0.5), reads=[f"ps{bg}"], writes=[k2])
            P.dve(lambda e, bv=bv, c=c: e.scalar_tensor_tensor(out=B.ub[:, c, 30 * ts:30 * ts + nt], in0=t2[:, 0:nt], scalar=1.0, in1=ps[bv][:, 0:nt],
                                                              op0=ALU.add, op1=ALU.mult), reads=[f"ps{bv}", k2], writes=b.wB[c])
            yield
        if b.last_p or sample:
            state_out(B.ub, 4, 30 * ts + nt - 128, b.kBc, [(sb_o[22 * 16:30 * 16, :], 0, 128)] if sample else [(pb_o[:, :], 98, 128)], scale=0.5)
        pieces = []
        for c in range(4):
            j = 0
            while j < 31:
                n = min(8, 31 - j)
                pieces.append((c, j, n))
                j += n

        def gen_piece(pidx):
            c, j, n = pieces[pidx]
            pi = (dg_cnt[0] + pidx) % NDG
            dgp = dg[:, pi * 1024:pi * 1024 + n * 128]
            P.dve(lambda e, dgp=dgp, n=n, c=c, j=j: e.tensor_tensor(
                out=dgp.rearrange("p (j m) -> p j m", j=n), in0=identb[:].unsqueeze(1).broadcast_to([128, n, 128]),
                in1=wb2[:, c * 31 + j:c * 31 + j + n].unsqueeze(2).broadcast_to([128, n, 128]), op=ALU.mult),
                reads=["identb", "wb2"], writes=[f"dg{pi}"])

        gen_piece(0)
        gen_piece(1)
        yield
        bc_ = None
        for pidx, (c, j, n) in enumerate(pieces):
            if j == 0:
                bc_ = CONV_BANK
            pi = (dg_cnt[0] + pidx) % NDG
            dgp = dg[:, pi * 1024:pi * 1024 + n * 128]
            for jj in range(n):
                P.pe(lambda e, dgp=dgp, jj=jj, j=j, c=c, bc_=bc_: e.matmul(ps[bc_][:, 0:nt], lhsT=dgp[:, jj * 128:(jj + 1) * 128],
                                                                        rhs=B.ub[:, c, (j + jj) * ts:(j + jj) * ts + nt],
                                                                        start=(j + jj == 0), stop=(j + jj == 30)),
                     reads=[f"dg{pi}", b.kBc[c], b.kB], writes=[f"ps{bc_}"])
            if pidx + 2 < len(pieces):
                gen_piece(pidx + 2)
            if j + n == 31:
                P.act(lambda e, c=c, bc_=bc_: e.activation(out=acc3[:, c, 0:nt], in_=ps[bc_][:, 0:nt], func=AF.Identity, bias=sm("cbb", c, c + 1), scale=1.0),
                      reads=[f"ps{bc_}", "small"], writes=[cx.sk(c)])
                P.act(lambda e, c=c, bc_=bc_: e.activation(out=ybb3[:, c, 0:nt], in_=ps[bc_][:, 0:nt], func=AF.Identity, bias=sm("cbb", c, c + 1), scale=1.0),
                      reads=[f"ps{bc_}", "small"], writes=[cx.sk(4), cx.sk(5)])
                P.act(lambda e, c=c, bc_=bc_: e.activation(out=ysq3[:, c, 0:nt], in_=ps[bc_][:, 0:nt], func=AF.Square, bias=sm("cbb", c, c + 1), scale=1.0),
                      reads=[f"ps{bc_}", "small"], writes=[cx.sk(6), cx.sk(7)])
                bgt = group(cx, win_e, "WE6", 6 * 512 + c * 128, nt)
                P.act(lambda e, bgt=bgt, c=c: e.activation(out=cx.sq3[:, 4 + c, 0:nt], in_=ps[bgt][:, 0:nt], func=AF.Silu), reads=[f"ps{bgt}"], writes=[cx.ksc(4 + c)])
            yield
        dg_cnt[0] += len(pieces)
        yield "tail"
        bm = nb()
        for c in range(4):
            P.pe(lambda e, c=c: e.matmul(ps[bm][:, 0:nt], lhsT=onesb[:], rhs=ybb3[:, c, 0:nt], start=(c == 0), stop=(c == 3)),
                 reads=[cx.sk(4), cx.sk(5), "onesb"], writes=[f"ps{bm}"])
        be = nb()
        for c in range(4):
            P.pe(lambda e, c=c: e.matmul(ps[be][:, 0:nt], lhsT=onesb[:], rhs=ysq3[:, c, 0:nt], start=(c == 0), stop=(c == 3)),
                 reads=[cx.sk(6), cx.sk(7), "onesb"], writes=[f"ps{be}"])
        yield
        mean, var = t2, t3
        P.dve(lambda e: e.tensor_scalar(out=mean[:, 0:nt], in0=ps[bm][:, 0:nt], scalar1=1.0 / 512, scalar2=None, op0=ALU.mult), reads=[f"ps{bm}"], writes=[k2])
        P.dve(lambda e: e.tensor_tensor(out=var[:, 0:nt], in0=mean[:, 0:nt], in1=mean[:, 0:nt], op=ALU.mult), reads=[k2], writes=[k3])
        P.dve(lambda e: e.scalar_tensor_tensor(out=var[:, 0:nt], in0=ps[be][:, 0:nt], scalar=1.0 / 512, in1=var[:, 0:nt], op0=ALU.mult, op1=ALU.subtract),
              reads=[f"ps{be}", k3], writes=[k3])
        P.act(lambda e: e.activation(out=var[:, 0:nt], in_=var[:, 0:nt], func=AF.Sqrt, bias=epsc[:, 1:2], scale=1.0), reads=[k3, "epsc"], writes=[k3])
        P.dve(lambda e: e.reciprocal(out=var[:, 0:nt], in_=var[:, 0:nt]), reads=[k3], writes=[k3])
        try_wout(cx, 0, st)
        yield
        for c in range(4):
            P.dve(lambda e, c=c: e.tensor_tensor(out=acc3[:, c, 0:nt], in0=acc3[:, c, 0:nt], in1=mean[:, 0:nt], op=ALU.subtract), reads=[cx.sk(c), k2], writes=[cx.sk(c)])
            P.dve(lambda e, c=c: e.tensor_tensor(out=acc3[:, c, 0:nt], in0=acc3[:, c, 0:nt], in1=var[:, 0:nt], op=ALU.mult), reads=[cx.sk(c), k3], writes=[cx.sk(c)])
        for c in range(4):
            P.act(lambda e, c=c: e.activation(out=acc3[:, c, 0:nt], in_=acc3[:, c, 0:nt], func=AF.Silu, bias=sm("lnb", c, c + 1), scale=sm("lng", c, c + 1)),
                  reads=[cx.sk(c), "small"], writes=[cx.sk(c)])
        for c in range(4):
            P.dve(lambda e, c=c: e.tensor_tensor(out=cx.sq3[:, 4 + c, 0:nt], in0=acc3[:, c, 0:nt], in1=cx.sq3[:, 4 + c, 0:nt], op=ALU.mult), reads=[cx.sk(c), cx.ksc(4 + c)], writes=[cx.ksc(4 + c)])
        yield
        yield from out_proj(cx, 0, nt, sample, par, st)

    def gen_L1(bi):
        b = blk(bi)
        cx, B, nt, sample, par, ts = C1, b.B, b.nt, b.sample, b.par, b.B.ts
        st = {"held": [False] * 4}
        ntile = b.ntile
        if sample:
            sample_loads_L1(b)
            yield
        if not sample:
            carry(B.cu, 15, BT, b.first, b.use_hm, [b.kC])
            carry(B.kT, 128, BT, b.first, False, [b.kK])
            if b.first:
                P.pool(lambda e: e.memset(B.vt[:, 0, :], 0.0), writes=[b.kV])
            else:
                P.pool(lambda e: e.tensor_copy(out=B.vt[:, 0, :], in_=B.vt[:, 2, :]), reads=[b.kV], writes=[b.kV])
        yield from prenorm(cx, 1, nt, sample, par)
        t3, k3 = cx.tmp[0]; t4, k4 = cx.tmp[1]
        for g, w in enumerate(POOL_W):
            bu = group(cx, win_o, "WOi", 0 + g * 128, nt)
            P.act(lambda e, bu=bu, g=g: e.copy(out=B.cu[:, g, 15 * ts:15 * ts + nt], in_=ps[bu][:, 0:nt]), reads=[f"ps{bu}"], writes=b.wC)
            if b.halo:
                continue
            bgc = group(cx, win_o, "WOi", 512 + g * 128, nt)
            P.act(lambda e, bgc=bgc, g=g: e.activation(out=cx.sq3[:, g, 0:nt], in_=ps[bgc][:, 0:nt], func=AF.Silu), reads=[f"ps{bgc}"], writes=[cx.ksc(g)])
            yield
            bp = nb()
            for j in range(w):
                P.pe(lambda e, j=j, g=g, bp=bp, w=w: e.matmul(ps[bp][:, 0:nt], lhsT=Wp[:, (2 * g + (1 if j else 0)) * 128:(2 * g + (1 if j else 0) + 1) * 128],
                                                        rhs=B.cu[:, g, (15 - j) * ts:(15 - j) * ts + nt], start=(j == 0), stop=(j == w - 1)),
                     reads=["Wp", b.kC], writes=[f"ps{bp}"])
            if b.use_hm:
                bq = nb()
                for j in range(w):
                    P.pe(lambda e, j=j, g=g, bq=bq, w=w: e.matmul(ps[bq][:, 0:16], lhsT=Wp[:, (2 * g + 1) * 128:(2 * g + 2) * 128],
                                                            rhs=B.cu[:, g, (15 - j):(15 - j) + 16], start=(j == 0), stop=(j == w - 1)),
                         reads=["Wp", b.kC], writes=[f"ps{bq}"])
                P.dve(lambda e, bq=bq, g=g: e.tensor_tensor(out=t3[:, 0:16], in0=ps[bq][:, 0:16], in1=sm("facm1", g * 16, g * 16 + 16), op=ALU.mult),
                      reads=[f"ps{bq}", "small"], writes=[k3])
                P.act(lambda e, bp=bp: e.copy(out=t4[:, 0:nt], in_=ps[bp][:, 0:nt]), reads=[f"ps{bp}"], writes=[k4])
                P.dve(lambda e: e.tensor_tensor(out=t4[:, 0:16], in0=t4[:, 0:16], in1=t3[:, 0:16], op=ALU.add), reads=[k3, k4], writes=[k4])
                P.dve(lambda e, g=g: e.scalar_tensor_tensor(out=cx.sq3[:, g, 0:nt], in0=t4[:, 0:nt], scalar=sm("psc", g, g + 1), in1=cx.sq3[:, g, 0:nt], op0=ALU.mult, op1=ALU.mult),
                      reads=[k4, cx.ksc(g), "small"], writes=[cx.ksc(g)])
            else:
                P.dve(lambda e, bp=bp, g=g: e.scalar_tensor_tensor(out=cx.sq3[:, g, 0:nt], in0=ps[bp][:, 0:nt], scalar=sm("psc", g, g + 1), in1=cx.sq3[:, g, 0:nt], op0=ALU.mult, op1=ALU.mult),
                      reads=[f"ps{bp}", cx.ksc(g), "small"], writes=[cx.ksc(g)])
            yield
        if b.last_p or sample:
            state_out(B.cu, 4, 15 * ts + nt - 128, [b.kC], [(sc_o[7 * 16:15 * 16, :], 0, 128)] if sample else [(pc_o[:, :], 113, 128)])
        kcol0 = 0 if sample else 128
        bk_ = group(cx, win_o, "WOi", 1536, nt)
        P.act(lambda e: e.copy(out=B.kT[:, kcol0:kcol0 + nt], in_=ps[bk_][:, 0:nt]), reads=[f"ps{bk_}"], writes=b.wK)
        for t in range(ntile):
            bkv = nb()
            for kc in range(8):
                P.pe(lambda e, kc=kc, t=t, bkv=bkv: e.matmul(ps[bkv][:, 0:256], lhsT=cx.hT3[:, kc, t * 128:(t + 1) * 128], rhs=win_o[:, kc, 1536:1792],
                                                            start=(kc == 0), stop=(kc == 7)), reads=["WOi", cx.khc(kc)], writes=[f"ps{bkv}"])
            vslot = t if sample else 1 + t
            P.act(lambda e, bkv=bkv, vslot=vslot: e.copy(out=B.vt[:, vslot, :], in_=ps[bkv][:, 128:256]), reads=[f"ps{bkv}"], writes=b.wV)
            if sample or (b.last_p and t == ntile - 1):
                s = stage_slot()
                P.act(lambda e, bkv=bkv, s=s: e.copy(out=xst[s][:, 0:256], in_=ps[bkv][:, 0:256]), reads=[f"ps{bkv}"], writes=[f"xst{s}"])
                dk, dv = (sk_o[120 * 16:128 * 16, :], sv_o[120 * 16:128 * 16, :]) if sample else (pk_o[:, :], pv_o[:, :])
                P.dma("sp", f"xst{s}", lambda e, s=s, dk=dk: e.dma_start(out=dk, in_=xst[s][:, 0:128]), reads=[f"xst{s}"], final=True)
                P.dma("sp", f"xst{s}", lambda e, s=s, dv=dv: e.dma_start(out=dv, in_=xst[s][:, 128:256]), reads=[f"xst{s}"], final=True)
        yield
        if b.halo:
            return
        kq = [cx.sk(4), cx.sk(5)]
        kg = [cx.sk(2), cx.sk(3)]
        for r in range(4):
            bq_ = group(cx, win_o, "WOi", 1024 + r * 128, nt)
            P.act(lambda e, bq_=bq_, r=r: e.copy(out=qT3[:, r, 0:nt], in_=ps[bq_][:, 0:nt]), reads=[f"ps{bq_}"], writes=kq)
            bd_ = group(cx, win_o, "WOi", 1792 + r * 128, nt)
            P.act(lambda e, bd_=bd_, r=r: e.activation(out=sgd3[:, r, 0:nt], in_=ps[bd_][:, 0:nt], func=AF.Silu), reads=[f"ps{bd_}"], writes=kg)
            try_wout(cx, 1, st)
            yield
        bias4 = biasT[:].rearrange("p (b h q) -> p b h q", b=2, h=8)
        PT4 = PT[:].rearrange("p (b h q) -> p b h q", b=2, h=8)
        for t in range(ntile):
            q0 = t * 128
            for kb in range(2):
                for g in range(2):
                    bs = nb()
                    P.pe(lambda e, kb=kb, g=g, bs=bs: e.matmul(ps[bs][:, :], lhsT=identb[:], rhs=bias4[:, kb, 4 * g:4 * g + 4, :], start=True, stop=False),
                         reads=["identb", "biasT"], writes=[f"ps{bs}"])
                    if sample and kb == 1:
                        for s_ in range(16):
                            P.pe(lambda e, g=g, bs=bs, s_=s_: e.matmul(ps[bs][:].rearrange("p (r i s) -> p r i s", r=4, s=16)[:, :, :, s_],
                                                                      lhsT=kTc[g * 64:(g + 1) * 64, s_, :],
                                                                      rhs=qT3[g * 64:(g + 1) * 64, :, 0:128].rearrange("p r (i s) -> p r i s", s=16)[:, :, :, s_],
                                                                      start=False, stop=(s_ == 15)), reads=["kTc"] + kq, writes=[f"ps{bs}"])
                    else:
                        kc0 = (kcol0 + q0) if kb == 0 else (kcol0 + q0 - 128)
                        for r in range(4):
                            P.pe(lambda e, g=g, r=r, bs=bs, kc0=kc0, q0=q0: e.matmul(ps[bs][:, r * 128:(r + 1) * 128], lhsT=B.kT[g * 64:(g + 1) * 64, kc0:kc0 + 128],
                                                                                    rhs=qT3[g * 64:(g + 1) * 64, r, q0:q0 + 128], start=False, stop=(r == 3)),
                                 reads=[b.kK] + kq, writes=[f"ps{bs}"])
                    P.act(lambda e, kb=kb, g=g, bs=bs: e.activation(out=PT4[:, kb, 4 * g:4 * g + 4, :], in_=ps[bs][:].rearrange("p (h q) -> p h q", h=4),
                                                                    func=AF.Exp, scale=0.125), reads=[f"ps{bs}"], writes=[f"PT{kb}"])
                if kb == 1 and b.use_hm and t == 0:
                    P.act(lambda e: e.activation(out=PT[:, 1024:2048], in_=PT[:, 1024:2048], func=AF.Copy, scale=sm("hm")),
                          reads=["PT1", "small"], writes=["PT1"])
                try_wout(cx, 1, st)
                yield
            bnum, bden = nb(), nb()
            for (bo, isden) in ((bnum, False), (bden, True)):
                for g in range(2):
                    vcur = B.vt[:, (t if sample else 1 + t), g * 64:(g + 1) * 64]
                    P.pe(lambda e, g=g, bo=bo, isden=isden, vcur=vcur: e.matmul(ps[bo][g * 64:(g + 1) * 64, :], lhsT=(onesb[:, 0:64] if isden else vcur),
                                                                              rhs=PT4[:, 0, 4 * g:4 * g + 4, :], start=True, stop=False),
                         reads=[b.kV, "PT0", "onesb"], writes=[f"ps{bo}"])
                    if sample:
                        for s_ in range(16):
                            P.pe(lambda e, g=g, bo=bo, isden=isden, s_=s_: e.matmul(
                                ps[bo][g * 64:(g + 1) * 64, :].rearrange("p (r i s) -> p r i s", r=4, s=16)[:, :, :, s_],
                                lhsT=(onesb[:, 0:64] if isden else Vc[:, s_, g * 64:(g + 1) * 64]),
                                rhs=PT4[:, 1, 4 * g:4 * g + 4, :].rearrange("p r (i s) -> p r i s", s=16)[:, :, :, s_],
                                start=False, stop=(s_ == 15)), reads=["Vc", "PT1", "onesb"], writes=[f"ps{bo}"])
                    else:
                        vprev = B.vt[:, t, g * 64:(g + 1) * 64]
                        P.pe(lambda e, g=g, bo=bo, isden=isden, vprev=vprev: e.matmul(ps[bo][g * 64:(g + 1) * 64, :], lhsT=(onesb[:, 0:64] if isden else vprev),
                                                                                    rhs=PT4[:, 1, 4 * g:4 * g + 4, :], start=False, stop=True),
                             reads=[b.kV, "PT1", "onesb"], writes=[f"ps{bo}"])
            yield
            kd = [cx.sk(0), cx.sk(1)]
            P.dve(lambda e, bden=bden: e.tensor_tensor(out=dn.rearrange("p (r q) -> p r q", r=4), in0=ps[bden][:].rearrange("p (r q) -> p r q", r=4),
                                                       in1=esk[:].unsqueeze(2).broadcast_to([128, 4, 128]), op=ALU.add), reads=[f"ps{bden}", "esk"], writes=kd)
            P.dve(lambda e: e.reciprocal(out=dn, in_=dn), reads=kd, writes=kd)
            P.dve(lambda e, bnum=bnum: e.tensor_tensor(out=dn, in0=ps[bnum][:], in1=dn, op=ALU.mult), reads=[f"ps{bnum}"] + kd, writes=kd)
            P.dve(lambda e, q0=q0: e.tensor_tensor(out=cx.sq3[:, 4:8, q0:q0 + 128], in0=dn.rearrange("p (r q) -> p r q", r=4), in1=sgd3[:, :, q0:q0 + 128], op=ALU.mult),
                  reads=kd + kg, writes=[cx.ksc(4 + r_) for r_ in range(4)])
            yield
        yield "hold"
        yield from out_proj(cx, 1, nt, sample, par, st)
        for t in range(ntile):
            if sample:
                store_y(ys_o[:, :], par, 0)
            else:
                r0 = (bi - 1) * BT + t * 128
                store_y(yp_o[r0:r0 + 128, :], par, t * 128)
            yield

    def run(g):
        for _ in g:
            pass

    def interleave(ga, gb):
        da = db = False
        while not (da and db):
            if not da:
                try:
                    next(ga)
                except StopIteration:
                    da = True
            if not db:
                try:
                    next(gb)
                except StopIteration:
                    db = True

    run(gen_L0(0))
    run(gen_L0(1))
    doneL0, doneL1 = {0, 1}, set()
    a, bq = 2, 0
    gA = gB = None
    a_tail = False
    b_hold = False
    while len(doneL1) < NBLK_P + 1:
        if gA is None and a < NBLK_P + 1 and ((a - NXT) < 0 or (a - NXT) in doneL1):
            gA = gen_L0(a)
            a_tail = False
        if gB is None and bq < NBLK_P + 1 and bq in doneL0:
            gB = gen_L1(bq)
            b_hold = False
        if gA is not None:
            try:
                if next(gA) == "tail":
                    a_tail = True
            except StopIteration:
                doneL0.add(a); a += 1; gA = None
        if b_hold and (a_tail or gA is None):
            b_hold = False
        if gB is not None and not b_hold:
            try:
                if next(gB) == "hold" and gA is not None and not a_tail:
                    b_hold = True
            except StopIteration:
                doneL1.add(bq); bq += 1; gB = None
    P.emit(nc)
    es.close()
    return nc


def _wl(w):
    n = w.shape[1]
    return np.ascontiguousarray(w.reshape(8, 128, n).transpose(1, 0, 2).reshape(128, 8 * n))


def _vl(v, nch):
    return np.ascontiguousarray(v.reshape(nch, 128).T)


def _bias_tables():
    k = np.arange(128)[:, None]
    q = np.arange(128)[None, :]
    slopes = 2.0 ** (-(np.arange(8) + 1.0))
    bp = np.full((128, 2, 8, 128), NEG, np.float32)
    bs = np.full((128, 2, 8, 128), NEG, np.float32)
    qi, qs = q // 16, q % 16
    ki, ks = k // 16, k % 16
    for h in range(8):
        sl = 8.0 * slopes[h]
        bp[:, 0, h, :] = np.where(q >= k, -sl * (q - k), NEG)
        bp[:, 1, h, :] = np.where(k > q, -sl * (q + 128 - k), NEG)
        bs[:, 0, h, :] = np.where((qs == ks) & (ki <= qi), -sl * (qi - ki), NEG)
        bs[:, 1, h, :] = np.where(k > qi, -sl * (128 + qi - k), NEG)
    return (bp.reshape(128, 2048).astype(ml_dtypes.bfloat16), bs.reshape(128, 2048).astype(ml_dtypes.bfloat16))


_NC_CACHE = {}


def kernel(x_prompt, x_sample, state_conv_a, state_conv_b, state_pool_c, cache_win_k, cache_win_v,
           c_prompt, c_sample, w_mod_e, b_mod_e, g_pre_e, g_post_e, w_in_e, conv_a_w, conv_b_w, conv_b_b,
           ln_b_g, ln_b_b, w_out_e, w_mod_o, b_mod_o, g_pre_o, g_post_o, w_in_o, pool_w, pool_scale,
           sinks, w_out_o):
    f32 = np.float32
    A = lambda a: np.asarray(a, dtype=f32)
    x_prompt, x_sample = A(x_prompt), A(x_sample)
    hp = np.array([(g * 4 + r) * 64 + d for r in range(4) for g in range(2) for d in range(64)])
    cols = np.concatenate([np.arange(0, 1024), 1024 + hp, np.arange(1536, 1792), 1792 + hp])
    wino = A(w_in_o)[0][:, cols]
    rows = np.concatenate([np.arange(0, 512), 512 + hp])
    wouto = A(w_out_o)[0][rows, :]
    shared = {
        "wmod_e": _wl(A(w_mod_e)[0]), "wmod_o": _wl(A(w_mod_o)[0]),
        "win_e": _wl(A(w_in_e)[0]), "wout_e": _wl(A(w_out_e)[0]),
        "win_o": _wl(wino), "wout_o": _wl(wouto),
        "identf": np.eye(128, dtype=f32),
        "poolw": np.ascontiguousarray(A(pool_w)[0].transpose(1, 0, 2).reshape(128, 512)),
    }
    shared["biasp"], shared["biass"] = _bias_tables()
    sm_base = np.zeros((128, NSM), f32)

    def put(name, arr):
        o, w = SM[name]
        sm_base[:, o:o + w] = arr

    put("bmod_e", _vl(A(b_mod_e)[0], 24)); put("bmod_o", _vl(A(b_mod_o)[0], 24))
    put("gpre_e", _vl(A(g_pre_e)[0], 8)); put("gpost_e", _vl(A(g_post_e)[0], 8))
    put("gpre_o", _vl(A(g_pre_o)[0], 8)); put("gpost_o", _vl(A(g_post_o)[0], 8))
    put("caw", A(conv_a_w)[0].reshape(3, 4, 128).transpose(2, 1, 0).reshape(128, 12))
    put("cbw", A(conv_b_w)[0].reshape(31, 4, 128).transpose(2, 1, 0).reshape(128, 124))
    put("cbb", _vl(A(conv_b_b)[0], 4)); put("lng", _vl(A(ln_b_g)[0], 4)); put("lnb", _vl(A(ln_b_b)[0], 4))
    put("psc", _vl(A(pool_scale)[0], 4))
    put("snk", np.repeat(A(sinks)[0].reshape(2, 4), 64, axis=0))
    fac = np.zeros((4, 16), f32)
    for g, w in enumerate(POOL_W):
        for t in range(16):
            fac[g, t] = w / min(w, t + 1) - 1.0
    in_maps = []
    for c in range(NCORES):
        b, hf = c // 2, c % 2
        xp = np.zeros((HALO + 2048, 1024), f32)
        if hf == 1:
            xp[:] = x_prompt[b, 2048 - HALO:4096]
        else:
            xp[HALO:] = x_prompt[b, 0:2048]
        sl = slice(16 * c, 16 * c + 16)
        xs = np.ascontiguousarray(x_sample[sl].transpose(1, 0, 2).reshape(128, 1024))
        call = np.concatenate([A(c_prompt)[b:b + 1], A(c_sample)[sl]], axis=0)
        cT = np.ascontiguousarray(call.reshape(17, 8, 128).transpose(2, 1, 0).reshape(128, 136))
        smc = sm_base.copy()
        o, w = SM["hm"]; smc[:, o] = float(hf)
        o, w = SM["facm1"]; smc[:, o:o + w] = (fac.reshape(1, 64) if hf == 0 else 0.0)
        m = dict(shared)
        m.update({
            "xp": xp, "xs": xs, "cT": cT, "small": smc,
            "sa": np.ascontiguousarray(A(state_conv_a)[0, sl].transpose(1, 0, 2).reshape(32, 512)),
            "sb": np.ascontiguousarray(A(state_conv_b)[0, sl].transpose(1, 0, 2).reshape(480, 512)),
            "sc": np.ascontiguousarray(A(state_pool_c)[0, sl].transpose(1, 0, 2).reshape(240, 512)),
            "ck": np.ascontiguousarray(A(cache_win_k)[0, sl].reshape(16, 128, 128).transpose(1, 0, 2).reshape(128, 2048)),
            "cv": np.ascontiguousarray(A(cache_win_v)[0, sl].reshape(16, 128, 128).transpose(1, 0, 2).reshape(128, 2048)),
        })
        in_maps.append(m)
    if "nc" not in _NC_CACHE:
        _NC_CACHE["nc"] = build_program()
    res = run_bass_kernel_spmd(_NC_CACHE["nc"], in_maps, core_ids=list(range(NCORES)))
    R = res.results
    y_prompt = np.zeros((4, 4096, 1024), f32); y_sample = np.zeros((128, 8, 1024), f32)
    pa = np.zeros((1, 4, 2, 512), f32); sa = np.zeros((1, 128, 2, 512), f32)
    pb = np.zeros((1, 4, 30, 512), f32); sbo = np.zeros((1, 128, 30, 512), f32)
    pc = np.zeros((1, 4, 15, 512), f32); sco = np.zeros((1, 128, 15, 512), f32)
    pk = np.zeros((1, 4, 128, 2, 64), f32); sk = np.zeros((1, 128, 128, 2, 64), f32)
    pv = np.zeros((1, 4, 128, 2, 64), f32); sv = np.zeros((1, 128, 128, 2, 64), f32)
    for c in range(NCORES):
        b, hf = c // 2, c % 2
        r = R[c]
        sl = slice(16 * c, 16 * c + 16)
        y_prompt[b, hf * 2048:(hf + 1) * 2048] = r["yp"]
        y_sample[sl] = r["ys"].reshape(8, 16, 1024).transpose(1, 0, 2)
        sa[0, sl] = r["sa_o"].reshape(2, 16, 512).transpose(1, 0, 2)
        sbo[0, sl] = r["sb_o"].reshape(30, 16, 512).transpose(1, 0, 2)
        sco[0, sl] = r["sc_o"].reshape(15, 16, 512).transpose(1, 0, 2)
        sk[0, sl] = r["sk_o"].reshape(128, 16, 2, 64).transpose(1, 0, 2, 3)
        sv[0, sl] = r["sv_o"].reshape(128, 16, 2, 64).transpose(1, 0, 2, 3)
        if hf == 1:
            pa[0, b] = r["pa"]; pb[0, b] = r["pb"]; pc[0, b] = r["pc"]
            pk[0, b] = r["pk"].reshape(128, 2, 64); pv[0, b] = r["pv"].reshape(128, 2, 64)
    return (y_prompt, y_sample, pa, sa, pb, sbo, pc, sco, pk, sk, pv, sv)
```

```python
import numpy as np
from contextlib import ExitStack
import concourse.bass as bass
import concourse.mybir as mybir
from concourse.bass_utils import run_bass_kernel_spmd

F32 = mybir.dt.float32
BF16 = mybir.dt.bfloat16
AF = mybir.ActivationFunctionType
ALU = mybir.AluOpType
AX = mybir.AxisListType

ENGS = ("pe", "act", "dve", "pool", "sp")


class Op:
    __slots__ = ("eng", "fn", "reads", "writes", "dma_key", "dma_k", "pos",
                 "waits", "signal", "sigval", "name")

    def __init__(self, eng, fn, reads, writes, dma_key, name):
        self.eng = eng
        self.fn = fn
        self.reads = tuple(reads)
        self.writes = tuple(writes)
        self.dma_key = dma_key
        self.dma_k = 0
        self.pos = 0
        self.waits = []
        self.signal = False
        self.sigval = 0
        self.name = name


class Prog:
    def __init__(self):
        self.ops = []
        self.dma_mode = {}
        self.final_keys = []
        self.barriers = set()

    def add(self, eng, fn, reads=(), writes=(), dma_key=None, name=""):
        op = Op(eng, fn, reads, writes, dma_key, name)
        self.ops.append(op)
        return op

    def barrier(self):
        self.barriers.add(len(self.ops))

    def pe(self, fn, reads=(), writes=(), name=""):
        return self.add("pe", fn, reads, writes, name=name)

    def act(self, fn, reads=(), writes=(), name=""):
        return self.add("act", fn, reads, writes, name=name)

    def dve(self, fn, reads=(), writes=(), name=""):
        return self.add("dve", fn, reads, writes, name=name)

    def pool(self, fn, reads=(), writes=(), name=""):
        return self.add("pool", fn, reads, writes, name=name)

    def dma(self, eng, key, fn, reads=(), writes=(), mode="slot", final=False, name=""):
        self.dma_mode.setdefault(key, mode)
        assert self.dma_mode[key] == mode
        if final and key not in self.final_keys:
            self.final_keys.append(key)
        return self.add(eng, fn, reads, writes, dma_key=key, name=name)

    def analyze(self):
        last_writer = {}
        readers = {}
        eng_pos = {e: 0 for e in ENGS}
        dma_cnt = {}
        dma_last = {}
        waited = {e: {} for e in ENGS}
        last_on = {}
        bar_ops = []
        for oi, op in enumerate(self.ops):
            if oi in self.barriers:
                bar_ops = list(last_on.values())
            op.pos = eng_pos[op.eng]
            eng_pos[op.eng] += 1
            raw = set()
            other = set(bar_ops)
            for r in op.reads:
                if r in last_writer:
                    raw.add(last_writer[r])
            for w in op.writes:
                if w in last_writer:
                    other.add(last_writer[w])
                for rd in readers.get(w, ()):
                    other.add(rd)
            if op.dma_key is not None:
                k = dma_cnt.get(op.dma_key, 0) + 1
                dma_cnt[op.dma_key] = k
                op.dma_k = k
                if self.dma_mode[op.dma_key] == "slot" and op.dma_key in dma_last:
                    other.add(dma_last[op.dma_key])
                dma_last[op.dma_key] = op
            need = {}
            for d in raw | other:
                if d is op:
                    continue
                if d.dma_key is not None:
                    sk = ("dma", d.dma_key)
                    v = d.dma_k if self.dma_mode[d.dma_key] == "slot" else -1
                    if sk not in need or (need[sk] != -1 and (v == -1 or v > need[sk])):
                        need[sk] = v
                    continue
                if d.eng == op.eng and op.dma_key is None:
                    if op.eng == "pe":
                        continue
                sk = ("eng", d.eng)
                if sk not in need or d.pos > need[sk].pos:
                    need[sk] = d
            for sk, v in need.items():
                op.waits.append((sk, v))
            for r in op.reads:
                readers.setdefault(r, []).append(op)
            for w in op.writes:
                last_writer[w] = op
                readers[w] = []
            last_on[(op.eng, op.dma_key)] = op
        self.dma_cnt = dma_cnt
        for op in self.ops:
            ws = []
            wd = waited[op.eng]
            for sk, v in op.waits:
                if sk[0] == "dma":
                    val = (self.dma_cnt[sk[1]] if v == -1 else v) * 16
                    if wd.get(sk, 0) >= val:
                        continue
                    wd[sk] = val
                    ws.append((sk, val))
                else:
                    if wd.get(sk, -1) >= v.pos:
                        continue
                    wd[sk] = v.pos
                    v.signal = True
                    ws.append((sk, v))
            op.waits = ws
        cnt = {e: 0 for e in ENGS}
        for op in self.ops:
            if op.signal:
                cnt[op.eng] += 1
                op.sigval = cnt[op.eng]
        self.sig_cnt = cnt

    def emit(self, nc):
        self.analyze()
        with ExitStack() as es:
            sems = {}
            for e in ENGS:
                if self.sig_cnt[e] > 0:
                    sems[("eng", e)] = es.enter_context(nc.semaphore(f"s_{e}"))
            for i, key in enumerate(self.dma_cnt):
                sems[("dma", key)] = es.enter_context(nc.semaphore(f"d_{i}"))
            block = es.enter_context(nc.Block())
            streams = {e: [op for op in self.ops if op.eng == e] for e in ENGS}

            def run(eng, ename):
                for op in streams[ename]:
                    for sk, v in op.waits:
                        if sk[0] == "dma":
                            eng.wait_ge(sems[sk], v)
                        else:
                            eng.wait_ge(sems[sk], v.sigval)
                    ins = op.fn(eng)
                    if op.dma_key is not None:
                        ins.then_inc(sems[("dma", op.dma_key)], 16)
                    elif op.signal:
                        ins.then_inc(sems[("eng", ename)], 1)
                if ename == "sp":
                    for key in self.final_keys:
                        eng.wait_ge(sems[("dma", key)], self.dma_cnt[key] * 16)

            @block.tensor
            def _(eng):
                run(eng, "pe")

            @block.scalar
            def _(eng):
                run(eng, "act")

            @block.vector
            def _(eng):
                run(eng, "dve")

            @block.gpsimd
            def _(eng):
                run(eng, "pool")

            @block.sync
            def _(eng):
                run(eng, "sp")


import ml_dtypes

NCORES = 8
HALO = 256
BT = 256
NBLK_P = (HALO + 2048) // BT
POOL_W = (2, 4, 8, 16)
NEG = -240000.0

SM = {}
_o = 0
for _n, _w in (("bmod_e", 24), ("bmod_o", 24), ("gpre_e", 8), ("gpost_e", 8), ("gpre_o", 8), ("gpost_o", 8),
               ("caw", 12), ("cbw", 124), ("cbb", 4), ("lng", 4), ("lnb", 4), ("psc", 4), ("snk", 4),
               ("hm", 1), ("facm1", 64)):
    SM[_n] = (_o, _w)
    _o += _w
NSM = _o


def build_program():
    nc = bass.Bass("TRN2", target_bir_lowering=False)
    P = Prog()
    es = ExitStack()

    def din(name, shape, dt=F32):
        return nc.dram_tensor(name, list(shape), dt, kind="ExternalInput").ap()

    def dout(name, shape, dt=F32):
        return nc.dram_tensor(name, list(shape), dt, kind="ExternalOutput").ap()

    def sb(name, shape, dt=F32):
        return es.enter_context(nc.sbuf_tensor("sb_" + name, list(shape), dt))

    xp = din("xp", [HALO + 2048, 1024]); xs = din("xs", [128, 1024])
    cT_d = din("cT", [128, 8 * 17]); small_d = din("small", [128, NSM]); poolw_d = din("poolw", [128, 512])
    identf_d = din("identf", [128, 128]); biasp_d = din("biasp", [128, 2048], BF16); biass_d = din("biass", [128, 2048], BF16)
    sa_d = din("sa", [32, 512]); sb_d = din("sb", [480, 512]); sc_d = din("sc", [240, 512])
    ck_d = din("ck", [128, 16 * 128]); cv_d = din("cv", [128, 16 * 128])
    wmod_d = [din("wmod_e", [128, 8 * 3072]), din("wmod_o", [128, 8 * 3072])]
    win_e_d = din("win_e", [128, 8 * 3584]); wout_e_d = din("wout_e", [128, 8 * 1024])
    win_o_d = din("win_o", [128, 8 * 2304]); wout_o_d = din("wout_o", [128, 8 * 1024])
    wout_d = [wout_e_d, wout_o_d]
    wout_bf = [nc.dram_tensor(f"wout_bf{l}", [128, 8 * 1024], BF16, kind="Internal").ap() for l in range(2)]
    yp_o = dout("yp", [2048, 1024]); ys_o = dout("ys", [128, 1024])
    pa_o = dout("pa", [2, 512]); sa_o = dout("sa_o", [32, 512])
    pb_o = dout("pb", [30, 512]); sb_o = dout("sb_o", [480, 512])
    pc_o = dout("pc", [15, 512]); sc_o = dout("sc_o", [240, 512])
    pk_o = dout("pk", [128, 128]); sk_o = dout("sk_o", [2048, 128])
    pv_o = dout("pv", [128, 128]); sv_o = dout("sv_o", [2048, 128])

    WE = sb("WE", [128, 8 * 3584], BF16); WOi = sb("WOi", [128, 8 * 2304], BF16); WOUT = sb("WOUT", [128, 8 * 1024], BF16)
    win_e = WE[:].rearrange("p (k n) -> p k n", k=8)
    win_o = WOi[:].rearrange("p (k n) -> p k n", k=8)
    wout3 = WOUT[:].rearrange("p (k n) -> p k n", k=8)
    identf = sb("identf", [128, 128]); identb = sb("identb", [128, 128], BF16); onesb = sb("onesb", [128, 128], BF16)
    biasT = sb("biasT", [128, 2048], BF16)
    small = sb("small", [128, NSM])
    Wp = sb("Wp", [128, 8 * 128], BF16)
    diag3 = sb("diag3", [128, 12 * 128], BF16)
    wb2 = sb("wb2", [128, 124], BF16); b1sc = sb("b1sc", [128, 16]); esk = sb("esk", [128, 4]); epsc = sb("epsc", [128, 2])
    modv = sb("modv", [128, 2 * 3 * 8 * 17])
    xst = [sb(f"xst{i}", [128, 512]) for i in range(2)]
    NXT = 3
    xTs = [sb(f"xT{i}", [128, 8 * BT]) for i in range(NXT)]
    tmp = [None, None] + [sb(f"tmp{i}", [128, BT]) for i in range(2, 7)]
    PT = sb("PT", [128, 2 * 1024], BF16)
    UB = sb("UB", [128, 4928], BF16)
    NDG = 3
    dg = sb("dg", [128, NDG * 1024], BF16)
    ps = [es.enter_context(nc.psum_tensor(f"ps{i}", [128, 512], F32)) for i in range(8)]

    class Ctx:
        pass

    def mkctx(n, nslots, tmps):
        cx = Ctx()
        cx.n = n
        cx.hT = sb(n + "hT", [128, 8 * BT], BF16)
        cx.hT3 = cx.hT[:].rearrange("p (k t) -> p k t", k=8)
        cx.sqmix = sb(n + "sqmix", [128, 8 * BT], BF16)
        cx.sq3 = cx.sqmix[:].rearrange("p (k t) -> p k t", k=8)
        cx.rstd = sb(n + "rstd", [128, BT])
        cx.scr = sb(n + "scr", [128, nslots * BT])
        cx.sqB3 = cx.scr[:, 0:4 * BT].bitcast(BF16).rearrange("p (k t) -> p k t", k=8)
        cx.tmp = tmps
        cx.kr = n + "rstd"
        cx.khc = lambda c: f"{n}hT{c}"
        cx.ksc = lambda c: f"{n}sq{c}"
        cx.kh_all = [f"{n}hT{c}" for c in range(8)]
        cx.ks_all = [f"{n}sq{c}" for c in range(8)]
        cx.sk = lambda i: f"{n}scr{i}"
        return cx

    C0 = mkctx("a", 8, [(tmp[3], "tmp3"), (tmp[4], "tmp4"), (tmp[2], "tmp2"), (tmp[3], "tmp3"), (tmp[4], "tmp4")])
    C1 = mkctx("b", 6, [(tmp[5], "tmp5"), (tmp[6], "tmp6")])
    acc3 = C0.scr[:, 0:4 * BT].rearrange("p (k t) -> p k t", k=4)
    ybs = C0.scr[:, 4 * BT:8 * BT].bitcast(BF16)
    ybb3 = ybs[:, 0:4 * BT].rearrange("p (k t) -> p k t", k=4)
    ysq3 = ybs[:, 4 * BT:8 * BT].rearrange("p (k t) -> p k t", k=4)
    dn = C1.scr[:, 0:512]
    sgd3 = C1.scr[:, 2 * BT:4 * BT].bitcast(BF16).rearrange("p (k t) -> p k t", k=4)
    qT3 = C1.scr[:, 4 * BT:6 * BT].bitcast(BF16).rearrange("p (k t) -> p k t", k=4)

    def sm(name, a=0, b=None):
        o, w = SM[name]
        return small[:, o + a:o + (w if b is None else b)]

    def mv(l, kind, kc, b0, b1):
        o = ((l * 3 + kind) * 8 + kc) * 17
        return modv[:, o + b0:o + b1]

    def xT3(par):
        return xTs[par][:].rearrange("p (k t) -> p k t", k=8)

    def xkc(par, c):
        return f"xT{par}_{c}"

    def xk_all(par):
        return [f"xT{par}_{c}" for c in range(8)]

    class Bufs:
        pass

    def carve(kind):
        b = Bufs()
        if kind == "p":
            b.ts = 1
            b.axc = UB[:, 0:1032].rearrange("p (c t) -> p c t", c=4)
            b.ub = UB[:, 1032:2176].rearrange("p (c t) -> p c t", c=4)
            b.cu = UB[:, 2432:3516].rearrange("p (c t) -> p c t", c=4)
            b.kT = UB[:, 3516:3900]
            b.vt = UB[:, 3900:4284].rearrange("p (t d) -> p t d", t=3)
            b.key = "ubp"
        else:
            b.ts = 16
            b.ub = UB[:, 0:2432].rearrange("p (c t) -> p c t", c=4)
            b.axc = UB[:, 4284:4924].rearrange("p (c t) -> p c t", c=4)
            b.cu = UB[:, 2432:3904].rearrange("p (c t) -> p c t", c=4)
            b.kT = UB[:, 3904:4032]
            b.vt = UB[:, 4032:4160].rearrange("p (t d) -> p t d", t=1)
            b.key = "ubs"
        return b

    bank_i = [0]
    dg_cnt = [0]
    xin_cnt = [0]

    CONV_BANK = 7

    def nb():
        i = bank_i[0] % 7
        bank_i[0] += 1
        return i

    def stage_slot():
        i = xin_cnt[0] % 2
        xin_cnt[0] += 1
        return i

    def ld(eng, key, out, in_, w, mode="group"):
        P.dma(eng, key, lambda e: e.dma_start(out=out, in_=in_), writes=w, mode=mode)

    cT = tmp[4][:, 0:136]
    scT = tmp[3][:, 0:68].bitcast(BF16)
    poolw = xst[0][:, 0:512]
    ld("sp", "c0", small[:], small_d, ["small"]); ld("sp", "c0", cT, cT_d, ["tmp4"])
    ld("sp", "c0", identf[:], identf_d, ["identf"]); ld("sp", "c0", poolw, poolw_d, ["xst0"])
    ld("sp", "c0", biasT[:], biasp_d, ["biasT"])

    def load_wpiece(src_d, ncols_total, c0, c1, slot, key):
        P.dma("pool", key, lambda e: e.dma_start(out=wout3[:, :, slot * 256:slot * 256 + (c1 - c0)],
                                                 in_=src_d.rearrange("p (k n) -> p k n", k=8)[:, :, c0:c1]), writes=[f"WOUT{slot}"], mode="slot")

    P.act(lambda e: e.copy(out=identb[:], in_=identf[:]), reads=["identf"], writes=["identb"])
    P.dve(lambda e: e.memset(onesb[:], 1.0), writes=["onesb"])
    P.dve(lambda e: e.memset(epsc[:, 0:1], 1e-6), writes=["epsc"])
    P.dve(lambda e: e.memset(epsc[:, 1:2], 1e-5), writes=["epsc"])
    P.act(lambda e: e.activation(out=scT, in_=cT, func=AF.Silu), reads=["tmp4"], writes=["tmp3"])
    P.act(lambda e: e.activation(out=esk[:], in_=sm("snk"), func=AF.Exp), reads=["small"], writes=["esk"])
    P.dve(lambda e: e.tensor_scalar(out=wb2[:], in0=sm("cbw"), scalar1=0.5, scalar2=None, op0=ALU.mult), reads=["small"], writes=["wb2"])
    for c in range(4):
        for j in range(3):
            P.dve(lambda e, c=c, j=j: e.tensor_scalar(out=diag3[:, (c * 3 + j) * 128:(c * 3 + j + 1) * 128], in0=identb[:],
                                                      scalar1=sm("caw", c * 3 + j, c * 3 + j + 1), scalar2=None, op0=ALU.mult),
                  reads=["identb", "small"], writes=["diag3"])
    for g, w in enumerate(POOL_W):
        P.dve(lambda e, g=g, w=w: e.tensor_scalar(out=Wp[:, (2 * g) * 128:(2 * g + 1) * 128], in0=xst[0][:, g * 128:(g + 1) * 128],
                                                  scalar1=(1.0 / w - 1.0), scalar2=None, op0=ALU.mult), reads=["xst0"], writes=["Wp"])
        P.dve(lambda e, g=g, w=w: e.tensor_scalar(out=Wp[:, (2 * g + 1) * 128:(2 * g + 2) * 128], in0=xst[0][:, g * 128:(g + 1) * 128],
                                                  scalar1=(1.0 / w), scalar2=None, op0=ALU.mult), reads=["xst0"], writes=["Wp"])
    scT3 = scT.rearrange("p (k b) -> p k b", k=8)

    def do_mod(l):
        bm_, gpre, gpost = (("bmod_e", "gpre_e", "gpost_e"), ("bmod_o", "gpre_o", "gpost_o"))[l]
        P.dve(lambda e: e.tensor_scalar(out=b1sc[:, l * 8:(l + 1) * 8], in0=sm(bm_, 8, 16), scalar1=1.0, scalar2=None, op0=ALU.add),
              reads=["small"], writes=["b1sc"])
        for pc_ in range(12):
            slot = pc_ % 4
            load_wpiece(wmod_d[l], 3072, pc_ * 256, (pc_ + 1) * 256, slot, f"wmod{slot}")
            for q in range(2):
                fj = pc_ * 2 + q
                bi_ = nb()
                for kc in range(8):
                    P.pe(lambda e, bi_=bi_, q=q, kc=kc, slot=slot: e.matmul(ps[bi_][:, 0:17], lhsT=wout3[:, kc, slot * 256 + q * 128:slot * 256 + (q + 1) * 128],
                                                                         rhs=scT3[:, kc, :], start=(kc == 0), stop=(kc == 7)),
                         reads=[f"WOUT{slot}", "tmp3"], writes=[f"ps{bi_}"])
                j = fj % 8
                if fj < 8:
                    P.act(lambda e, bi_=bi_, fj=fj, j=j: e.activation(out=mv(l, 0, j, 0, 17), in_=ps[bi_][:, 0:17], func=AF.Identity,
                                                                    bias=sm(bm_, fj, fj + 1), scale=1.0), reads=[f"ps{bi_}", "small"], writes=["modv"])
                elif fj < 16:
                    P.dve(lambda e, bi_=bi_, j=j: e.tensor_scalar(out=mv(l, 1, j, 0, 17), in0=ps[bi_][:, 0:17], scalar1=b1sc[:, l * 8 + j:l * 8 + j + 1],
                                                                scalar2=sm(gpre, j, j + 1), op0=ALU.add, op1=ALU.mult),
                          reads=[f"ps{bi_}", "small", "b1sc"], writes=["modv"])
                else:
                    P.dve(lambda e, bi_=bi_, fj=fj, j=j: e.tensor_scalar(out=mv(l, 2, j, 0, 17), in0=ps[bi_][:, 0:17], scalar1=sm(bm_, fj, fj + 1),
                                                                       scalar2=sm(gpost, j, j + 1), op0=ALU.add, op1=ALU.mult),
                          reads=[f"ps{bi_}", "small"], writes=["modv"])

    do_mod(0)
    do_mod(1)
    for i in range(7):
        ld("pool", f"we{i}", win_e[:, :, i * 512:(i + 1) * 512],
           win_e_d.rearrange("p (k n) -> p k n", k=8)[:, :, i * 512:(i + 1) * 512], [f"WE{i}"])
    P.dma("pool", "wcast0", lambda e: e.dma_start(out=wout_bf[0], in_=wout_d[0]), writes=["woutbf0"], mode="slot")
    ld("pool", "wo_i", win_o, win_o_d.rearrange("p (k n) -> p k n", k=8), ["WOi"])
    P.dma("pool", "wcast1", lambda e: e.dma_start(out=wout_bf[1], in_=wout_d[1]), writes=["woutbf1"], mode="slot")
    P.dma("sp", "d2d", lambda e: e.dma_start(out=sb_o[0:22 * 16, :], in_=sb_d[8 * 16:30 * 16, :]), final=True)
    P.dma("sp", "d2d", lambda e: e.dma_start(out=sc_o[0:7 * 16, :], in_=sc_d[8 * 16:15 * 16, :]), final=True)
    P.dma("sp", "d2d", lambda e: e.dma_start(out=sk_o[0:120 * 16, :], in_=ck_d.rearrange("p (s d) -> (p s) d", s=16)[8 * 16:128 * 16, :]), final=True)
    P.dma("sp", "d2d", lambda e: e.dma_start(out=sv_o[0:120 * 16, :], in_=cv_d.rearrange("p (s d) -> (p s) d", s=16)[8 * 16:128 * 16, :]), final=True)

    def load_x(src_rows, par, tcol):
        x3 = xT3(par)
        for h in range(2):
            s = stage_slot()
            P.dma("sp", f"xst{s}", lambda e, s=s, h=h: e.dma_start(out=xst[s][:], in_=src_rows[:, h * 512:(h + 1) * 512]), writes=[f"xst{s}"])
            bk = nb()
            for q in range(4):
                P.pe(lambda e, bk=bk, q=q, s=s: e.transpose(out=ps[bk][:, q * 128:(q + 1) * 128], in_=xst[s][:, q * 128:(q + 1) * 128], identity=identf[:]),
                     reads=[f"xst{s}", "identf"], writes=[f"ps{bk}"])
            P.act(lambda e, bk=bk, h=h: e.copy(out=x3[:, h * 4:(h + 1) * 4, tcol:tcol + 128], in_=ps[bk][:].rearrange("p (q t) -> p q t", q=4)),
                  reads=[f"ps{bk}"], writes=[xkc(par, h * 4 + q_) for q_ in range(4)])

    def store_y(dst_rows, par, tcol):
        x3 = xT3(par)
        for h in range(2):
            s = stage_slot()
            bk = nb()
            for q in range(4):
                kc = h * 4 + q
                P.pe(lambda e, bk=bk, q=q, kc=kc: e.transpose(out=ps[bk][:, q * 128:(q + 1) * 128], in_=x3[:, kc, tcol:tcol + 128], identity=identf[:]),
                     reads=[xkc(par, kc), "identf"], writes=[f"ps{bk}"])
            P.act(lambda e, bk=bk, s=s: e.copy(out=xst[s][:], in_=ps[bk][:]), reads=[f"ps{bk}"], writes=[f"xst{s}"])
            P.dma("sp", f"xst{s}", lambda e, s=s, h=h: e.dma_start(out=dst_rows[:, h * 512:(h + 1) * 512], in_=xst[s][:]), reads=[f"xst{s}"], final=True)

    def stats_tail(cx, sqv3, sqkeys, nt):
        bk = nb()
        for kc in range(8):
            P.pe(lambda e, kc=kc: e.matmul(ps[bk][:, 0:nt], lhsT=onesb[:], rhs=sqv3[:, kc, 0:nt], start=(kc == 0), stop=(kc == 7)),
                 reads=[sqkeys[kc] if len(sqkeys) == 8 else sqkeys[kc // 2], "onesb"], writes=[f"ps{bk}"])
        P.act(lambda e: e.activation(out=cx.rstd[:, 0:nt], in_=ps[bk][:, 0:nt], func=AF.Sqrt, bias=epsc[:, 0:1], scale=1.0 / 1024),
              reads=[f"ps{bk}", "epsc"], writes=[cx.kr])
        P.dve(lambda e: e.reciprocal(out=cx.rstd[:, 0:nt], in_=cx.rstd[:, 0:nt]), reads=[cx.kr], writes=[cx.kr])

    def bc_mod(l, kind, kc):
        return mv(l, kind, kc, 1, 17).unsqueeze(1).broadcast_to([128, 8, 16])

    def tok3(ap2):
        return ap2.rearrange("p (i s) -> p i s", s=16)

    def prenorm(cx, l, nt, sample, par):
        x3 = xT3(par)
        for hh in range(4):
            P.act(lambda e, hh=hh: e.activation(out=cx.sq3[:, hh * 2:(hh + 1) * 2, 0:nt], in_=x3[:, hh * 2:(hh + 1) * 2, 0:nt], func=AF.Square),
                  reads=xk_all(par)[hh * 2:(hh + 1) * 2], writes=cx.ks_all[hh * 2:(hh + 1) * 2])
        yield
        stats_tail(cx, cx.sq3, cx.ks_all, nt)
        yield
        for kc in range(8):
            t, tk = cx.tmp[kc % 2]
            if not sample:
                P.dve(lambda e, kc=kc, t=t: e.scalar_tensor_tensor(out=t[:, 0:nt], in0=x3[:, kc, 0:nt], scalar=mv(l, 1, kc, 0, 1), in1=cx.rstd[:, 0:nt],
                                                                   op0=ALU.mult, op1=ALU.mult), reads=[xkc(par, kc), cx.kr, "modv"], writes=[tk])
                P.act(lambda e, kc=kc, t=t: e.activation(out=cx.hT3[:, kc, 0:nt], in_=t[:, 0:nt], func=AF.Identity, bias=mv(l, 0, kc, 0, 1), scale=1.0),
                      reads=[tk, "modv"], writes=[cx.khc(kc)])
            else:
                P.dve(lambda e, kc=kc, t=t: e.tensor_tensor(out=t[:, 0:nt], in0=x3[:, kc, 0:nt], in1=cx.rstd[:, 0:nt], op=ALU.mult), reads=[xkc(par, kc), cx.kr], writes=[tk])
                P.dve(lambda e, kc=kc, t=t: e.tensor_tensor(out=tok3(t[:, 0:nt]), in0=tok3(t[:, 0:nt]), in1=bc_mod(l, 1, kc), op=ALU.mult),
                      reads=[tk, "modv"], writes=[tk])
                P.dve(lambda e, kc=kc, t=t: e.tensor_tensor(out=tok3(cx.hT3[:, kc, 0:nt]), in0=tok3(t[:, 0:nt]), in1=bc_mod(l, 0, kc), op=ALU.add),
                      reads=[tk, "modv"], writes=[cx.khc(kc)])
        yield

    def group(cx, W3, wkey, col0, nt, ncols=128):
        bk = nb()
        for kc in range(8):
            P.pe(lambda e, kc=kc: e.matmul(ps[bk][0:ncols, 0:nt], lhsT=W3[:, kc, col0:col0 + ncols], rhs=cx.hT3[:, kc, 0:nt], start=(kc == 0), stop=(kc == 7)),
                 reads=[wkey, cx.khc(kc)], writes=[f"ps{bk}"])
        return bk

    def load_wout_slot(l, sl):
        P.dma("sp", f"wout{sl}", lambda e: e.dma_start(out=wout3[:, :, sl * 256:(sl + 1) * 256],
                                                      in_=wout_bf[l].rearrange("p (k n) -> p k n", k=8)[:, :, sl * 256:(sl + 1) * 256]),
              reads=[f"woutbf{l}"], writes=[f"WOUT{sl}"], mode="slot")

    wout_lock = [None] * 4

    def try_wout(cx, l, st):
        for sl in range(4):
            if not st["held"][sl] and wout_lock[sl] is None:
                wout_lock[sl] = cx.n
                load_wout_slot(l, sl)
                st["held"][sl] = True

    def out_proj(cx, l, nt, sample, par, st):
        x3 = xT3(par)
        while not all(st["held"]):
            try_wout(cx, l, st)
            if not all(st["held"]):
                yield
        for dc in range(8):
            bk = nb()
            for kc in range(8):
                P.pe(lambda e, kc=kc, dc=dc, bk=bk: e.matmul(ps[bk][:, 0:nt], lhsT=wout3[:, kc, dc * 128:(dc + 1) * 128], rhs=cx.sq3[:, kc, 0:nt], start=(kc == 0), stop=(kc == 7)),
                     reads=[f"WOUT{dc // 2}", cx.ksc(kc)], writes=[f"ps{bk}"])
            P.act(lambda e, dc=dc, bk=bk: e.copy(out=cx.hT3[:, dc, 0:nt], in_=ps[bk][:, 0:nt]), reads=[f"ps{bk}"], writes=[cx.khc(dc)])
            P.act(lambda e, dc=dc, bk=bk: e.activation(out=cx.sqB3[:, dc, 0:nt], in_=ps[bk][:, 0:nt], func=AF.Square), reads=[f"ps{bk}"], writes=[cx.sk(dc // 2)])
            if dc % 2 == 1:
                wout_lock[dc // 2] = None
            yield
        stats_tail(cx, cx.sqB3, [cx.sk(i) for i in range(4)], nt)
        yield
        for dc in range(8):
            t, tk = cx.tmp[dc % 2]
            if not sample:
                P.dve(lambda e, dc=dc, t=t: e.scalar_tensor_tensor(out=t[:, 0:nt], in0=cx.hT3[:, dc, 0:nt], scalar=mv(l, 2, dc, 0, 1), in1=cx.rstd[:, 0:nt],
                                                                   op0=ALU.mult, op1=ALU.mult), reads=[cx.khc(dc), cx.kr, "modv"], writes=[tk])
            else:
                P.dve(lambda e, dc=dc, t=t: e.tensor_tensor(out=t[:, 0:nt], in0=cx.hT3[:, dc, 0:nt], in1=cx.rstd[:, 0:nt], op=ALU.mult), reads=[cx.khc(dc), cx.kr], writes=[tk])
                P.dve(lambda e, dc=dc, t=t: e.tensor_tensor(out=tok3(t[:, 0:nt]), in0=tok3(t[:, 0:nt]), in1=bc_mod(l, 2, dc), op=ALU.mult),
                      reads=[tk, "modv"], writes=[tk])
            P.pool(lambda e, dc=dc, t=t: e.tensor_tensor(out=x3[:, dc, 0:nt], in0=x3[:, dc, 0:nt], in1=t[:, 0:nt], op=ALU.add), reads=[xkc(par, dc), tk], writes=[xkc(par, dc)])
        yield

    def ld_(ap, a, b):
        return ap[:, :, a:b] if len(ap.shape) == 3 else ap[:, a:b]

    def carry(buf, S, L, first, use_hm, keys):
        if first:
            P.pool(lambda e: e.memset(ld_(buf, 0, S), 0.0), writes=keys)
        elif use_hm:
            P.act(lambda e: e.activation(out=ld_(buf, 0, S), in_=ld_(buf, L, L + S), func=AF.Copy, scale=sm("hm")),
                  reads=keys + ["small"], writes=keys)
        else:
            P.pool(lambda e: e.tensor_copy(out=ld_(buf, 0, S), in_=ld_(buf, L, L + S)), reads=keys, writes=keys)

    def state_out(src3, nch, tcol0, srckeys, dmas, scale=1.0):
        s = stage_slot()
        bk = nb()
        pb = ps[bk][:].bitcast(BF16)
        for c in range(nch):
            P.pe(lambda e, c=c: e.transpose(out=pb[:, c * 128:(c + 1) * 128], in_=src3[:, c, tcol0:tcol0 + 128], identity=identb[:]),
                 reads=list(srckeys) + ["identb"], writes=[f"ps{bk}"])
        P.act(lambda e: e.activation(out=xst[s][:, 0:nch * 128], in_=pb[:, 0:nch * 128], func=AF.Copy, scale=scale), reads=[f"ps{bk}"], writes=[f"xst{s}"])
        for (dst, r0, r1) in dmas:
            P.dma("sp", f"xst{s}", lambda e, dst=dst, r0=r0, r1=r1: e.dma_start(out=dst, in_=xst[s][r0:r1, 0:nch * 128]), reads=[f"xst{s}"], final=True)

    def blk(bi):
        b = Bufs()
        b.sample = (bi == NBLK_P)
        b.nt = 128 if b.sample else BT
        b.ntile = b.nt // 128
        b.par = bi % NXT
        b.B = carve("s" if b.sample else "p")
        b.first, b.use_hm, b.last_p, b.halo = (bi == 0), (bi == 1), (bi == NBLK_P - 1), (bi == 0)
        k = b.B.key
        b.kA, b.kB, b.kC, b.kK, b.kV = k + "A", k + "B", k + "C", k + "K", k + "V"
        b.kBc = [k + "B" + str(c) for c in range(4)]
        pL0 = ["ubpA", "ubpB"] + ["ubpB" + str(c) for c in range(4)]
        pL1 = ["ubpC", "ubpK", "ubpV"]
        b.wA = [b.kA]
        b.wB = [[b.kBc[c]] + (pL0 if b.sample else []) for c in range(4)]
        b.wC = [b.kC] + (pL1 if b.sample else [])
        b.wK = [b.kK] + (pL1 if b.sample else [])
        b.wV = [b.kV] + (pL1 if b.sample else [])
        return b

    def load_state(src_d, nrows, dst3, col0, dkeys):
        r = 0
        while r < nrows:
            n = min(128, nrows - r)
            s = stage_slot()
            P.dma("sp", f"xst{s}", lambda e, r=r, n=n, s=s: e.dma_start(out=xst[s][0:n, 0:512], in_=src_d[r:r + n, :]), writes=[f"xst{s}"])
            bk = nb()
            for c in range(4):
                P.pe(lambda e, c=c, n=n, s=s, bk=bk: e.transpose(out=ps[bk][:, c * 128:c * 128 + n], in_=xst[s][0:n, c * 128:(c + 1) * 128], identity=identf[0:n, 0:n]),
                     reads=[f"xst{s}", "identf"], writes=[f"ps{bk}"])
            P.act(lambda e, r=r, n=n, bk=bk: e.copy(out=dst3[:, :, col0 + r:col0 + r + n], in_=ps[bk][:].rearrange("p (c t) -> p c t", c=4)[:, :, 0:n]),
                  reads=[f"ps{bk}"], writes=dkeys)
            r += n

    xoi = (NBLK_P + 1) % NXT
    xo = xTs[xoi][:].bitcast(BF16)
    kTc = xo[:, 0:2048].rearrange("p (s t) -> p s t", s=16)
    Vc = xo[:, 2048:4096].rearrange("p (s d) -> p s d", s=16)
    xok_all = xk_all(xoi)

    def sample_loads_L0(b):
        pL0 = ["ubpA", "ubpB"] + ["ubpB" + str(c) for c in range(4)]
        load_state(sa_d, 32, b.B.axc, 0, ["ubsA"])
        load_state(sb_d, 480, b.B.ub, 0, ["ubsB"] + pL0)
        P.act(lambda e: e.activation(out=b.B.ub[:, :, 0:480], in_=b.B.ub[:, :, 0:480], func=AF.Copy, scale=2.0), reads=["ubsB"], writes=["ubsB"])

    def sample_loads_L1(b):
        pL1 = ["ubpC", "ubpK", "ubpV"]
        ld("sp", "c1", biasT[:], biass_d, ["biasT"], mode="slot")
        load_state(sc_d, 240, b.B.cu, 0, ["ubsC"] + pL1)
        for s_ in range(0, 16, 4):
            sl = stage_slot()
            P.dma("sp", f"xst{sl}", lambda e, s_=s_, sl=sl: e.dma_start(out=xst[sl][:, :], in_=ck_d[:, s_ * 128:(s_ + 4) * 128]), writes=[f"xst{sl}"])
            bk = nb()
            for q in range(4):
                P.pe(lambda e, q=q, sl=sl, bk=bk: e.transpose(out=ps[bk][:, q * 128:(q + 1) * 128], in_=xst[sl][:, q * 128:(q + 1) * 128], identity=identf[:]),
                     reads=[f"xst{sl}", "identf"], writes=[f"ps{bk}"])
            P.act(lambda e, s_=s_, bk=bk: e.copy(out=kTc[:, s_:s_ + 4, :], in_=ps[bk][:].rearrange("p (q t) -> p q t", q=4)), reads=[f"ps{bk}"], writes=["kTc"] + xok_all)
            sl = stage_slot()
            P.dma("sp", f"xst{sl}", lambda e, s_=s_, sl=sl: e.dma_start(out=xst[sl][:, :], in_=cv_d[:, s_ * 128:(s_ + 4) * 128]), writes=[f"xst{sl}"])
            P.act(lambda e, s_=s_, sl=sl: e.copy(out=Vc[:, s_:s_ + 4, :], in_=xst[sl][:].rearrange("p (s d) -> p s d", s=4)), reads=[f"xst{sl}"], writes=["Vc"] + xok_all)

    def gen_L0(bi):
        b = blk(bi)
        cx, B, nt, sample, par, ts = C0, b.B, b.nt, b.sample, b.par, b.B.ts
        st = {"held": [False] * 4}
        if sample:
            sample_loads_L0(b)
            yield
        for t in range(b.ntile):
            load_x(xs if sample else xp[bi * BT + t * 128: bi * BT + (t + 1) * 128, :], par, t * 128)
            yield
        if not sample:
            carry(B.axc, 2, BT, b.first, b.use_hm, [b.kA])
            carry(B.ub, 30, BT, b.first, b.use_hm, [b.kB] + b.kBc)
        yield from prenorm(cx, 0, nt, sample, par)
        t2, k2 = cx.tmp[2]; t3, k3 = cx.tmp[3]; t4, k4 = cx.tmp[4]
        for c in range(4):
            b1 = group(cx, win_e, "WE0", 0 * 512 + c * 128, nt)
            P.act(lambda e, b1=b1: e.copy(out=t2[:, 0:nt], in_=ps[b1][:, 0:nt]), reads=[f"ps{b1}"], writes=[k2])
            b2 = group(cx, win_e, "WE2", 2 * 512 + c * 128, nt)
            P.dve(lambda e, b2=b2, c=c: e.tensor_tensor(out=B.axc[:, c, 2 * ts:2 * ts + nt], in0=ps[b2][:, 0:nt], in1=t2[:, 0:nt], op=ALU.mult),
                  reads=[f"ps{b2}", k2], writes=[b.kA])
            yield
            b3 = group(cx, win_e, "WE1", 1 * 512 + c * 128, nt)
            b4 = group(cx, win_e, "WE3", 3 * 512 + c * 128, nt)
            P.act(lambda e, b4=b4: e.activation(out=t3[:, 0:nt], in_=ps[b4][:, 0:nt], func=AF.Silu), reads=[f"ps{b4}"], writes=[k3])
            P.dve(lambda e, b3=b3: e.tensor_tensor(out=t4[:, 0:nt], in0=ps[b3][:, 0:nt], in1=t3[:, 0:nt], op=ALU.mult), reads=[f"ps{b3}", k3], writes=[k4])
            yield
            b5 = nb()
            for j in range(3):
                P.pe(lambda e, j=j, c=c, b5=b5: e.matmul(ps[b5][:, 0:nt], lhsT=diag3[:, (c * 3 + j) * 128:(c * 3 + j + 1) * 128],
                                                        rhs=B.axc[:, c, j * ts:j * ts + nt], start=(j == 0), stop=(j == 2)), reads=["diag3", b.kA], writes=[f"ps{b5}"])
            P.dve(lambda e, b5=b5, c=c: e.tensor_tensor(out=cx.sq3[:, c, 0:nt], in0=ps[b5][:, 0:nt], in1=t4[:, 0:nt], op=ALU.mult),
                  reads=[f"ps{b5}", k4], writes=[cx.ksc(c)])
            yield
        if b.last_p or sample:
            state_out(B.axc, 4, 2 * ts + nt - 128, [b.kA], [(sa_o[0:32, :], 96, 128)] if sample else [(pa_o[:, :], 126, 128)])
        for c in range(4):
            bv = group(cx, win_e, "WE4", 4 * 512 + c * 128, nt)
            bg = group(cx, win_e, "WE5", 5 * 512 + c * 128, nt)
            P.act(lambda e, bg=bg: e.activation(out=t2[:, 0:nt], in_=ps[bg][:, 0:nt], func=AF.Tanh, scale=0.5), reads=[f"ps{bg}"], writes=[k2])
            P.dve(lambda e, bv=bv, c=c: e.scalar_tensor_tensor(out=B.ub[:, c, 30 * ts:30 * ts + nt], in0=t2[:, 0:nt], scalar=1.0, in1=ps[bv][:, 0:nt],
                                                              op0=ALU.add, op1=ALU.mult), reads=[f"ps{bv}", k2], writes=b.wB[c])
            yield
        if b.last_p or sample:
            state_out(B.ub, 4, 30 * ts + nt - 128, b.kBc, [(sb_o[22 * 16:30 * 16, :], 0, 128)] if sample else [(pb_o[:, :], 98, 128)], scale=0.5)
        pieces = []
        for c in range(4):
            j = 0
            while j < 31:
                n = min(8, 31 - j)
                pieces.append((c, j, n))
                j += n

        def gen_piece(pidx):
            c, j, n = pieces[pidx]
            pi = (dg_cnt[0] + pidx) % NDG
            dgp = dg[:, pi * 1024:pi * 1024 + n * 128]
            P.dve(lambda e, dgp=dgp, n=n, c=c, j=j: e.tensor_tensor(
                out=dgp.rearrange("p (j m) -> p j m", j=n), in0=identb[:].unsqueeze(1).broadcast_to([128, n, 128]),
                in1=wb2[:, c * 31 + j:c * 31 + j + n].unsqueeze(2).broadcast_to([128, n, 128]), op=ALU.mult),
                reads=["identb", "wb2"], writes=[f"dg{pi}"])

        gen_piece(0)
        gen_piece(1)
        yield
        bc_ = None
        for pidx, (c, j, n) in enumerate(pieces):
            if j == 0:
                bc_ = CONV_BANK
            pi = (dg_cnt[0] + pidx) % NDG
            dgp = dg[:, pi * 1024:pi * 1024 + n * 128]
            for jj in range(n):
                P.pe(lambda e, dgp=dgp, jj=jj, j=j, c=c, bc_=bc_: e.matmul(ps[bc_][:, 0:nt], lhsT=dgp[:, jj * 128:(jj + 1) * 128],
                                                                        rhs=B.ub[:, c, (j + jj) * ts:(j + jj) * ts + nt],
                                                                        start=(j + jj == 0), stop=(j + jj == 30)),
                     reads=[f"dg{pi}", b.kBc[c], b.kB], writes=[f"ps{bc_}"])
            if pidx + 2 < len(pieces):
                gen_piece(pidx + 2)
            if j + n == 31:
                P.act(lambda e, c=c, bc_=bc_: e.activation(out=acc3[:, c, 0:nt], in_=ps[bc_][:, 0:nt], func=AF.Identity, bias=sm("cbb", c, c + 1), scale=1.0),
                      reads=[f"ps{bc_}", "small"], writes=[cx.sk(c)])
                P.act(lambda e, c=c, bc_=bc_: e.activation(out=ybb3[:, c, 0:nt], in_=ps[bc_][:, 0:nt], func=AF.Identity, bias=sm("cbb", c, c + 1), scale=1.0),
                      reads=[f"ps{bc_}", "small"], writes=[cx.sk(4), cx.sk(5)])
                P.act(lambda e, c=c, bc_=bc_: e.activation(out=ysq3[:, c, 0:nt], in_=ps[bc_][:, 0:nt], func=AF.Square, bias=sm("cbb", c, c + 1), scale=1.0),
                      reads=[f"ps{bc_}", "small"], writes=[cx.sk(6), cx.sk(7)])
                bgt = group(cx, win_e, "WE6", 6 * 512 + c * 128, nt)
                P.act(lambda e, bgt=bgt, c=c: e.activation(out=cx.sq3[:, 4 + c, 0:nt], in_=ps[bgt][:, 0:nt], func=AF.Silu), reads=[f"ps{bgt}"], writes=[cx.ksc(4 + c)])
            yield
        dg_cnt[0] += len(pieces)
        yield "tail"
        bm = nb()
        for c in range(4):
            P.pe(lambda e, c=c: e.matmul(ps[bm][:, 0:nt], lhsT=onesb[:], rhs=ybb3[:, c, 0:nt], start=(c == 0), stop=(c == 3)),
                 reads=[cx.sk(4), cx.sk(5), "onesb"], writes=[f"ps{bm}"])
        be = nb()
        for c in range(4):
            P.pe(lambda e, c=c: e.matmul(ps[be][:, 0:nt], lhsT=onesb[:], rhs=ysq3[:, c, 0:nt], start=(c == 0), stop=(c == 3)),
                 reads=[cx.sk(6), cx.sk(7), "onesb"], writes=[f"ps{be}"])
        yield
        mean, var = t2, t3
        P.dve(lambda e: e.tensor_scalar(out=mean[:, 0:nt], in0=ps[bm][:, 0:nt], scalar1=1.0 / 512, scalar2=None, op0=ALU.mult), reads=[f"ps{bm}"], writes=[k2])
        P.dve(lambda e: e.tensor_tensor(out=var[:, 0:nt], in0=mean[:, 0:nt], in1=mean[:, 0:nt], op=ALU.mult), reads=[k2], writes=[k3])
        P.dve(lambda e: e.scalar_tensor_tensor(out=var[:, 0:nt], in0=ps[be][:, 0:nt], scalar=1.0 / 512, in1=var[:, 0:nt], op0=ALU.mult, op1=ALU.subtract),
              reads=[f"ps{be}", k3], writes=[k3])
        P.act(lambda e: e.activation(out=var[:, 0:nt], in_=var[:, 0:nt], func=AF.Sqrt, bias=epsc[:, 1:2], scale=1.0), reads=[k3, "epsc"], writes=[k3])
        P.dve(lambda e: e.reciprocal(out=var[:, 0:nt], in_=var[:, 0:nt]), reads=[k3], writes=[k3])
        try_wout(cx, 0, st)
        yield
        for c in range(4):
            P.dve(lambda e, c=c: e.tensor_tensor(out=acc3[:, c, 0:nt], in0=acc3[:, c, 0:nt], in1=mean[:, 0:nt], op=ALU.subtract), reads=[cx.sk(c), k2], writes=[cx.sk(c)])
            P.dve(lambda e, c=c: e.tensor_tensor(out=acc3[:, c, 0:nt], in0=acc3[:, c, 0:nt], in1=var[:, 0:nt], op=ALU.mult), reads=[cx.sk(c), k3], writes=[cx.sk(c)])
        for c in range(4):
            P.act(lambda e, c=c: e.activation(out=acc3[:, c, 0:nt], in_=acc3[:, c, 0:nt], func=AF.Silu, bias=sm("lnb", c, c + 1), scale=sm("lng", c, c + 1)),
                  reads=[cx.sk(c), "small"], writes=[cx.sk(c)])
        for c in range(4):
            P.dve(lambda e, c=c: e.tensor_tensor(out=cx.sq3[:, 4 + c, 0:nt], in0=acc3[:, c, 0:nt], in1=cx.sq3[:, 4 + c, 0:nt], op=ALU.mult), reads=[cx.sk(c), cx.ksc(4 + c)], writes=[cx.ksc(4 + c)])
        yield
        yield from out_proj(cx, 0, nt, sample, par, st)

    def gen_L1(bi):
        b = blk(bi)
        cx, B, nt, sample, par, ts = C1, b.B, b.nt, b.sample, b.par, b.B.ts
        st = {"held": [False] * 4}
        ntile = b.ntile
        if sample:
            sample_loads_L1(b)
            yield
        if not sample:
            carry(B.cu, 15, BT, b.first, b.use_hm, [b.kC])
            carry(B.kT, 128, BT, b.first, False, [b.kK])
            if b.first:
                P.pool(lambda e: e.memset(B.vt[:, 0, :], 0.0), writes=[b.kV])
            else:
                P.pool(lambda e: e.tensor_copy(out=B.vt[:, 0, :], in_=B.vt[:, 2, :]), reads=[b.kV], writes=[b.kV])
        yield from prenorm(cx, 1, nt, sample, par)
        t3, k3 = cx.tmp[0]; t4, k4 = cx.tmp[1]
        for g, w in enumerate(POOL_W):
            bu = group(cx, win_o, "WOi", 0 + g * 128, nt)
            P.act(lambda e, bu=bu, g=g: e.copy(out=B.cu[:, g, 15 * ts:15 * ts + nt], in_=ps[bu][:, 0:nt]), reads=[f"ps{bu}"], writes=b.wC)
            if b.halo:
                continue
            bgc = group(cx, win_o, "WOi", 512 + g * 128, nt)
            P.act(lambda e, bgc=bgc, g=g: e.activation(out=cx.sq3[:, g, 0:nt], in_=ps[bgc][:, 0:nt], func=AF.Silu), reads=[f"ps{bgc}"], writes=[cx.ksc(g)])
            yield
            bp = nb()
            for j in range(w):
                P.pe(lambda e, j=j, g=g, bp=bp, w=w: e.matmul(ps[bp][:, 0:nt], lhsT=Wp[:, (2 * g + (1 if j else 0)) * 128:(2 * g + (1 if j else 0) + 1) * 128],
                                                        rhs=B.cu[:, g, (15 - j) * ts:(15 - j) * ts + nt], start=(j == 0), stop=(j == w - 1)),
                     reads=["Wp", b.kC], writes=[f"ps{bp}"])
            if b.use_hm:
                bq = nb()
                for j in range(w):
                    P.pe(lambda e, j=j, g=g, bq=bq, w=w: e.matmul(ps[bq][:, 0:16], lhsT=Wp[:, (2 * g + 1) * 128:(2 * g + 2) * 128],
                                                            rhs=B.cu[:, g, (15 - j):(15 - j) + 16], start=(j == 0), stop=(j == w - 1)),
                         reads=["Wp", b.kC], writes=[f"ps{bq}"])
                P.dve(lambda e, bq=bq, g=g: e.tensor_tensor(out=t3[:, 0:16], in0=ps[bq][:, 0:16], in1=sm("facm1", g * 16, g * 16 + 16), op=ALU.mult),
                      reads=[f"ps{bq}", "small"], writes=[k3])
                P.act(lambda e, bp=bp: e.copy(out=t4[:, 0:nt], in_=ps[bp][:, 0:nt]), reads=[f"ps{bp}"], writes=[k4])
                P.dve(lambda e: e.tensor_tensor(out=t4[:, 0:16], in0=t4[:, 0:16], in1=t3[:, 0:16], op=ALU.add), reads=[k3, k4], writes=[k4])
                P.dve(lambda e, g=g: e.scalar_tensor_tensor(out=cx.sq3[:, g, 0:nt], in0=t4[:, 0:nt], scalar=sm("psc", g, g + 1), in1=cx.sq3[:, g, 0:nt], op0=ALU.mult, op1=ALU.mult),
                      reads=[k4, cx.ksc(g), "small"], writes=[cx.ksc(g)])
            else:
                P.dve(lambda e, bp=bp, g=g: e.scalar_tensor_tensor(out=cx.sq3[:, g, 0:nt], in0=ps[bp][:, 0:nt], scalar=sm("psc", g, g + 1), in1=cx.sq3[:, g, 0:nt], op0=ALU.mult, op1=ALU.mult),
                      reads=[f"ps{bp}", cx.ksc(g), "small"], writes=[cx.ksc(g)])
            yield
        if b.last_p or sample:
            state_out(B.cu, 4, 15 * ts + nt - 128, [b.kC], [(sc_o[7 * 16:15 * 16, :], 0, 128)] if sample else [(pc_o[:, :], 113, 128)])
        kcol0 = 0 if sample else 128
        bk_ = group(cx, win_o, "WOi", 1536, nt)
        P.act(lambda e: e.copy(out=B.kT[:, kcol0:kcol0 + nt], in_=ps[bk_][:, 0:nt]), reads=[f"ps{bk_}"], writes=b.wK)
        for t in range(ntile):
            bkv = nb()
            for kc in range(8):
                P.pe(lambda e, kc=kc, t=t, bkv=bkv: e.matmul(ps[bkv][:, 0:256], lhsT=cx.hT3[:, kc, t * 128:(t + 1) * 128], rhs=win_o[:, kc, 1536:1792],
                                                            start=(kc == 0), stop=(kc == 7)), reads=["WOi", cx.khc(kc)], writes=[f"ps{bkv}"])
            vslot = t if sample else 1 + t
            P.act(lambda e, bkv=bkv, vslot=vslot: e.copy(out=B.vt[:, vslot, :], in_=ps[bkv][:, 128:256]), reads=[f"ps{bkv}"], writes=b.wV)
            if sample or (b.last_p and t == ntile - 1):
                s = stage_slot()
                P.act(lambda e, bkv=bkv, s=s: e.copy(out=xst[s][:, 0:256], in_=ps[bkv][:, 0:256]), reads=[f"ps{bkv}"], writes=[f"xst{s}"])
                dk, dv = (sk_o[120 * 16:128 * 16, :], sv_o[120 * 16:128 * 16, :]) if sample else (pk_o[:, :], pv_o[:, :])
                P.dma("sp", f"xst{s}", lambda e, s=s, dk=dk: e.dma_start(out=dk, in_=xst[s][:, 0:128]), reads=[f"xst{s}"], final=True)
                P.dma("sp", f"xst{s}", lambda e, s=s, dv=dv: e.dma_start(out=dv, in_=xst[s][:, 128:256]), reads=[f"xst{s}"], final=True)
        yield
        if b.halo:
            return
        kq = [cx.sk(4), cx.sk(5)]
        kg = [cx.sk(2), cx.sk(3)]
        for r in range(4):
            bq_ = group(cx, win_o, "WOi", 1024 + r * 128, nt)
            P.act(lambda e, bq_=bq_, r=r: e.copy(out=qT3[:, r, 0:nt], in_=ps[bq_][:, 0:nt]), reads=[f"ps{bq_}"], writes=kq)
            bd_ = group(cx, win_o, "WOi", 1792 + r * 128, nt)
            P.act(lambda e, bd_=bd_, r=r: e.activation(out=sgd3[:, r, 0:nt], in_=ps[bd_][:, 0:nt], func=AF.Silu), reads=[f"ps{bd_}"], writes=kg)
            try_wout(cx, 1, st)
            yield
        bias4 = biasT[:].rearrange("p (b h q) -> p b h q", b=2, h=8)
        PT4 = PT[:].rearrange("p (b h q) -> p b h q", b=2, h=8)
        for t in range(ntile):
            q0 = t * 128
            for kb in range(2):
                for g in range(2):
                    bs = nb()
                    P.pe(lambda e, kb=kb, g=g, bs=bs: e.matmul(ps[bs][:, :], lhsT=identb[:], rhs=bias4[:, kb, 4 * g:4 * g + 4, :], start=True, stop=False),
                         reads=["identb", "biasT"], writes=[f"ps{bs}"])
                    if sample and kb == 1:
                        for s_ in range(16):
                            P.pe(lambda e, g=g, bs=bs, s_=s_: e.matmul(ps[bs][:].rearrange("p (r i s) -> p r i s", r=4, s=16)[:, :, :, s_],
                                                                      lhsT=kTc[g * 64:(g + 1) * 64, s_, :],
                                                                      rhs=qT3[g * 64:(g + 1) * 64, :, 0:128].rearrange("p r (i s) -> p r i s", s=16)[:, :, :, s_],
                                                                      start=False, stop=(s_ == 15)), reads=["kTc"] + kq, writes=[f"ps{bs}"])
                    else:
                        kc0 = (kcol0 + q0) if kb == 0 else (kcol0 + q0 - 128)
                        for r in range(4):
                            P.pe(lambda e, g=g, r=r, bs=bs, kc0=kc0, q0=q0: e.matmul(ps[bs][:, r * 128:(r + 1) * 128], lhsT=B.kT[g * 64:(g + 1) * 64, kc0:kc0 + 128],
                                                                                    rhs=qT3[g * 64:(g + 1) * 64, r, q0:q0 + 128], start=False, stop=(r == 3)),
                                 reads=[b.kK] + kq, writes=[f"ps{bs}"])
                    P.act(lambda e, kb=kb, g=g, bs=bs: e.activation(out=PT4[:, kb, 4 * g:4 * g + 4, :], in_=ps[bs][:].rearrange("p (h q) -> p h q", h=4),
                                                                    func=AF.Exp, scale=0.125), reads=[f"ps{bs}"], writes=[f"PT{kb}"])
                if kb == 1 and b.use_hm and t == 0:
                    P.act(lambda e: e.activation(out=PT[:, 1024:2048], in_=PT[:, 1024:2048], func=AF.Copy, scale=sm("hm")),
                          reads=["PT1", "small"], writes=["PT1"])
                try_wout(cx, 1, st)
                yield
            bnum, bden = nb(), nb()
            for (bo, isden) in ((bnum, False), (bden, True)):
                for g in range(2):
                    vcur = B.vt[:, (t if sample else 1 + t), g * 64:(g + 1) * 64]
                    P.pe(lambda e, g=g, bo=bo, isden=isden, vcur=vcur: e.matmul(ps[bo][g * 64:(g + 1) * 64, :], lhsT=(onesb[:, 0:64] if isden else vcur),
                                                                              rhs=PT4[:, 0, 4 * g:4 * g + 4, :], start=True, stop=False),
                         reads=[b.kV, "PT0", "onesb"], writes=[f"ps{bo}"])
                    if sample:
                        for s_ in range(16):
                            P.pe(lambda e, g=g, bo=bo, isden=isden, s_=s_: e.matmul(
                                ps[bo][g * 64:(g + 1) * 64, :].rearrange("p (r i s) -> p r i s", r=4, s=16)[:, :, :, s_],
                                lhsT=(onesb[:, 0:64] if isden else Vc[:, s_, g * 64:(g + 1) * 64]),
                                rhs=PT4[:, 1, 4 * g:4 * g + 4, :].rearrange("p r (i s) -> p r i s", s=16)[:, :, :, s_],
                                start=False, stop=(s_ == 15)), reads=["Vc", "PT1", "onesb"], writes=[f"ps{bo}"])
                    else:
                        vprev = B.vt[:, t, g * 64:(g + 1) * 64]
                        P.pe(lambda e, g=g, bo=bo, isden=isden, vprev=vprev: e.matmul(ps[bo][g * 64:(g + 1) * 64, :], lhsT=(onesb[:, 0:64] if isden else vprev),
                                                                                    rhs=PT4[:, 1, 4 * g:4 * g + 4, :], start=False, stop=True),
                             reads=[b.kV, "PT1", "onesb"], writes=[f"ps{bo}"])
            yield
            kd = [cx.sk(0), cx.sk(1)]
            P.dve(lambda e, bden=bden: e.tensor_tensor(out=dn.rearrange("p (r q) -> p r q", r=4), in0=ps[bden][:].rearrange("p (r q) -> p r q", r=4),
                                                       in1=esk[:].unsqueeze(2).broadcast_to([128, 4, 128]), op=ALU.add), reads=[f"ps{bden}", "esk"], writes=kd)
            P.dve(lambda e: e.reciprocal(out=dn, in_=dn), reads=kd, writes=kd)
            P.dve(lambda e, bnum=bnum: e.tensor_tensor(out=dn, in0=ps[bnum][:], in1=dn, op=ALU.mult), reads=[f"ps{bnum}"] + kd, writes=kd)
            P.dve(lambda e, q0=q0: e.tensor_tensor(out=cx.sq3[:, 4:8, q0:q0 + 128], in0=dn.rearrange("p (r q) -> p r q", r=4), in1=sgd3[:, :, q0:q0 + 128], op=ALU.mult),
                  reads=kd + kg, writes=[cx.ksc(4 + r_) for r_ in range(4)])
            yield
        yield "hold"
        yield from out_proj(cx, 1, nt, sample, par, st)
        for t in range(ntile):
            if sample:
                store_y(ys_o[:, :], par, 0)
            else:
                r0 = (bi - 1) * BT + t * 128
                store_y(yp_o[r0:r0 + 128, :], par, t * 128)
            yield

    def run(g):
        for _ in g:
            pass

    def interleave(ga, gb):
        da = db = False
        while not (da and db):
            if not da:
                try:
                    next(ga)
                except StopIteration:
                    da = True
            if not db:
                try:
                    next(gb)
                except StopIteration:
                    db = True

    run(gen_L0(0))
    run(gen_L0(1))
    doneL0, doneL1 = {0, 1}, set()
    a, bq = 2, 0
    gA = gB = None
    a_tail = False
    b_hold = False
    while len(doneL1) < NBLK_P + 1:
        if gA is None and a < NBLK_P + 1 and ((a - NXT) < 0 or (a - NXT) in doneL1):
            gA = gen_L0(a)
            a_tail = False
        if gB is None and bq < NBLK_P + 1 and bq in doneL0:
            gB = gen_L1(bq)
            b_hold = False
        if gA is not None:
            try:
                if next(gA) == "tail":
                    a_tail = True
            except StopIteration:
                doneL0.add(a); a += 1; gA = None
        if b_hold and (a_tail or gA is None):
            b_hold = False
        if gB is not None and not b_hold:
            try:
                if next(gB) == "hold" and gA is not None and not a_tail:
                    b_hold = True
            except StopIteration:
                doneL1.add(bq); bq += 1; gB = None
    P.emit(nc)
    es.close()
    return nc


def _wl(w):
    n = w.shape[1]
    return np.ascontiguousarray(w.reshape(8, 128, n).transpose(1, 0, 2).reshape(128, 8 * n))


def _vl(v, nch):
    return np.ascontiguousarray(v.reshape(nch, 128).T)


def _bias_tables():
    k = np.arange(128)[:, None]
    q = np.arange(128)[None, :]
    slopes = 2.0 ** (-(np.arange(8) + 1.0))
    bp = np.full((128, 2, 8, 128), NEG, np.float32)
    bs = np.full((128, 2, 8, 128), NEG, np.float32)
    qi, qs = q // 16, q % 16
    ki, ks = k // 16, k % 16
    for h in range(8):
        sl = 8.0 * slopes[h]
        bp[:, 0, h, :] = np.where(q >= k, -sl * (q - k), NEG)
        bp[:, 1, h, :] = np.where(k > q, -sl * (q + 128 - k), NEG)
        bs[:, 0, h, :] = np.where((qs == ks) & (ki <= qi), -sl * (qi - ki), NEG)
        bs[:, 1, h, :] = np.where(k > qi, -sl * (128 + qi - k), NEG)
    return (bp.reshape(128, 2048).astype(ml_dtypes.bfloat16), bs.reshape(128, 2048).astype(ml_dtypes.bfloat16))


_NC_CACHE = {}


def kernel(x_prompt, x_sample, state_conv_a, state_conv_b, state_pool_c, cache_win_k, cache_win_v,
           c_prompt, c_sample, w_mod_e, b_mod_e, g_pre_e, g_post_e, w_in_e, conv_a_w, conv_b_w, conv_b_b,
           ln_b_g, ln_b_b, w_out_e, w_mod_o, b_mod_o, g_pre_o, g_post_o, w_in_o, pool_w, pool_scale,
           sinks, w_out_o):
    f32 = np.float32
    A = lambda a: np.asarray(a, dtype=f32)
    x_prompt, x_sample = A(x_prompt), A(x_sample)
    hp = np.array([(g * 4 + r) * 64 + d for r in range(4) for g in range(2) for d in range(64)])
    cols = np.concatenate([np.arange(0, 1024), 1024 + hp, np.arange(1536, 1792), 1792 + hp])
    wino = A(w_in_o)[0][:, cols]
    rows = np.concatenate([np.arange(0, 512), 512 + hp])
    wouto = A(w_out_o)[0][rows, :]
    shared = {
        "wmod_e": _wl(A(w_mod_e)[0]), "wmod_o": _wl(A(w_mod_o)[0]),
        "win_e": _wl(A(w_in_e)[0]), "wout_e": _wl(A(w_out_e)[0]),
        "win_o": _wl(wino), "wout_o": _wl(wouto),
        "identf": np.eye(128, dtype=f32),
        "poolw": np.ascontiguousarray(A(pool_w)[0].transpose(1, 0, 2).reshape(128, 512)),
    }
    shared["biasp"], shared["biass"] = _bias_tables()
    sm_base = np.zeros((128, NSM), f32)

    def put(name, arr):
        o, w = SM[name]
        sm_base[:, o:o + w] = arr

    put("bmod_e", _vl(A(b_mod_e)[0], 24)); put("bmod_o", _vl(A(b_mod_o)[0], 24))
    put("gpre_e", _vl(A(g_pre_e)[0], 8)); put("gpost_e", _vl(A(g_post_e)[0], 8))
    put("gpre_o", _vl(A(g_pre_o)[0], 8)); put("gpost_o", _vl(A(g_post_o)[0], 8))
    put("caw", A(conv_a_w)[0].reshape(3, 4, 128).transpose(2, 1, 0).reshape(128, 12))
    put("cbw", A(conv_b_w)[0].reshape(31, 4, 128).transpose(2, 1, 0).reshape(128, 124))
    put("cbb", _vl(A(conv_b_b)[0], 4)); put("lng", _vl(A(ln_b_g)[0], 4)); put("lnb", _vl(A(ln_b_b)[0], 4))
    put("psc", _vl(A(pool_scale)[0], 4))
    put("snk", np.repeat(A(sinks)[0].reshape(2, 4), 64, axis=0))
    fac = np.zeros((4, 16), f32)
    for g, w in enumerate(POOL_W):
        for t in range(16):
            fac[g, t] = w / min(w, t + 1) - 1.0
    in_maps = []
    for c in range(NCORES):
        b, hf = c // 2, c % 2
        xp = np.zeros((HALO + 2048, 1024), f32)
        if hf == 1:
            xp[:] = x_prompt[b, 2048 - HALO:4096]
        else:
            xp[HALO:] = x_prompt[b, 0:2048]
        sl = slice(16 * c, 16 * c + 16)
        xs = np.ascontiguousarray(x_sample[sl].transpose(1, 0, 2).reshape(128, 1024))
        call = np.concatenate([A(c_prompt)[b:b + 1], A(c_sample)[sl]], axis=0)
        cT = np.ascontiguousarray(call.reshape(17, 8, 128).transpose(2, 1, 0).reshape(128, 136))
        smc = sm_base.copy()
        o, w = SM["hm"]; smc[:, o] = float(hf)
        o, w = SM["facm1"]; smc[:, o:o + w] = (fac.reshape(1, 64) if hf == 0 else 0.0)
        m = dict(shared)
        m.update({
            "xp": xp, "xs": xs, "cT": cT, "small": smc,
            "sa": np.ascontiguousarray(A(state_conv_a)[0, sl].transpose(1, 0, 2).reshape(32, 512)),
            "sb": np.ascontiguousarray(A(state_conv_b)[0, sl].transpose(1, 0, 2).reshape(480, 512)),
            "sc": np.ascontiguousarray(A(state_pool_c)[0, sl].transpose(1, 0, 2).reshape(240, 512)),
            "ck": np.ascontiguousarray(A(cache_win_k)[0, sl].reshape(16, 128, 128).transpose(1, 0, 2).reshape(128, 2048)),
            "cv": np.ascontiguousarray(A(cache_win_v)[0, sl].reshape(16, 128, 128).transpose(1, 0, 2).reshape(128, 2048)),
        })
        in_maps.append(m)
    if "nc" not in _NC_CACHE:
        _NC_CACHE["nc"] = build_program()
    res = run_bass_kernel_spmd(_NC_CACHE["nc"], in_maps, core_ids=list(range(NCORES)))
    R = res.results
    y_prompt = np.zeros((4, 4096, 1024), f32); y_sample = np.zeros((128, 8, 1024), f32)
    pa = np.zeros((1, 4, 2, 512), f32); sa = np.zeros((1, 128, 2, 512), f32)
    pb = np.zeros((1, 4, 30, 512), f32); sbo = np.zeros((1, 128, 30, 512), f32)
    pc = np.zeros((1, 4, 15, 512), f32); sco = np.zeros((1, 128, 15, 512), f32)
    pk = np.zeros((1, 4, 128, 2, 64), f32); sk = np.zeros((1, 128, 128, 2, 64), f32)
    pv = np.zeros((1, 4, 128, 2, 64), f32); sv = np.zeros((1, 128, 128, 2, 64), f32)
    for c in range(NCORES):
        b, hf = c // 2, c % 2
        r = R[c]
        sl = slice(16 * c, 16 * c + 16)
        y_prompt[b, hf * 2048:(hf + 1) * 2048] = r["yp"]
        y_sample[sl] = r["ys"].reshape(8, 16, 1024).transpose(1, 0, 2)
        sa[0, sl] = r["sa_o"].reshape(2, 16, 512).transpose(1, 0, 2)
        sbo[0, sl] = r["sb_o"].reshape(30, 16, 512).transpose(1, 0, 2)
        sco[0, sl] = r["sc_o"].reshape(15, 16, 512).transpose(1, 0, 2)
        sk[0, sl] = r["sk_o"].reshape(128, 16, 2, 64).transpose(1, 0, 2, 3)
        sv[0, sl] = r["sv_o"].reshape(128, 16, 2, 64).transpose(1, 0, 2, 3)
        if hf == 1:
            pa[0, b] = r["pa"]; pb[0, b] = r["pb"]; pc[0, b] = r["pc"]
            pk[0, b] = r["pk"].reshape(128, 2, 64); pv[0, b] = r["pv"].reshape(128, 2, 64)
    return (y_prompt, y_sample, pa, sa, pb, sbo, pc, sco, pk, sk, pv, sv)
```

```python
import numpy as np
from contextlib import ExitStack
import concourse.bass as bass
import concourse.mybir as mybir
from concourse.bass_utils import run_bass_kernel_spmd

F32 = mybir.dt.float32
BF16 = mybir.dt.bfloat16
AF = mybir.ActivationFunctionType
ALU = mybir.AluOpType
AX = mybir.AxisListType

ENGS = ("pe", "act", "dve", "pool", "sp")


class Op:
    __slots__ = ("eng", "fn", "reads", "writes", "dma_key", "dma_k", "pos",
                 "waits", "signal", "sigval", "name")

    def __init__(self, eng, fn, reads, writes, dma_key, name):
        self.eng = eng
        self.fn = fn
        self.reads = tuple(reads)
        self.writes = tuple(writes)
        self.dma_key = dma_key
        self.dma_k = 0
        self.pos = 0
        self.waits = []
        self.signal = False
        self.sigval = 0
        self.name = name


class Prog:
    def __init__(self):
        self.ops = []
        self.dma_mode = {}
        self.final_keys = []
        self.barriers = set()

    def add(self, eng, fn, reads=(), writes=(), dma_key=None, name=""):
        op = Op(eng, fn, reads, writes, dma_key, name)
        self.ops.append(op)
        return op

    def barrier(self):
        self.barriers.add(len(self.ops))

    def pe(self, fn, reads=(), writes=(), name=""):
        return self.add("pe", fn, reads, writes, name=name)

    def act(self, fn, reads=(), writes=(), name=""):
        return self.add("act", fn, reads, writes, name=name)

    def dve(self, fn, reads=(), writes=(), name=""):
        return self.add("dve", fn, reads, writes, name=name)

    def pool(self, fn, reads=(), writes=(), name=""):
        return self.add("pool", fn, reads, writes, name=name)

    def dma(self, eng, key, fn, reads=(), writes=(), mode="slot", final=False, name=""):
        self.dma_mode.setdefault(key, mode)
        assert self.dma_mode[key] == mode
        if final and key not in self.final_keys:
            self.final_keys.append(key)
        return self.add(eng, fn, reads, writes, dma_key=key, name=name)

    def analyze(self):
        last_writer = {}
        readers = {}
        eng_pos = {e: 0 for e in ENGS}
        dma_cnt = {}
        dma_last = {}
        waited = {e: {} for e in ENGS}
        last_on = {}
        bar_ops = []
        for oi, op in enumerate(self.ops):
            if oi in self.barriers:
                bar_ops = list(last_on.values())
            op.pos = eng_pos[op.eng]
            eng_pos[op.eng] += 1
            raw = set()
            other = set(bar_ops)
            for r in op.reads:
                if r in last_writer:
                    raw.add(last_writer[r])
            for w in op.writes:
                if w in last_writer:
                    other.add(last_writer[w])
                for rd in readers.get(w, ()):
                    other.add(rd)
            if op.dma_key is not None:
                k = dma_cnt.get(op.dma_key, 0) + 1
                dma_cnt[op.dma_key] = k
                op.dma_k = k
                if self.dma_mode[op.dma_key] == "slot" and op.dma_key in dma_last:
                    other.add(dma_last[op.dma_key])
                dma_last[op.dma_key] = op
            need = {}
            for d in raw | other:
                if d is op:
                    continue
                if d.dma_key is not None:
                    sk = ("dma", d.dma_key)
                    v = d.dma_k if self.dma_mode[d.dma_key] == "slot" else -1
                    if sk not in need or (need[sk] != -1 and (v == -1 or v > need[sk])):
                        need[sk] = v
                    continue
                if d.eng == op.eng and op.dma_key is None:
                    if op.eng == "pe":
                        continue
                sk = ("eng", d.eng)
                if sk not in need or d.pos > need[sk].pos:
                    need[sk] = d
            for sk, v in need.items():
                op.waits.append((sk, v))
            for r in op.reads:
                readers.setdefault(r, []).append(op)
            for w in op.writes:
                last_writer[w] = op
                readers[w] = []
            last_on[(op.eng, op.dma_key)] = op
        self.dma_cnt = dma_cnt
        for op in self.ops:
            ws = []
            wd = waited[op.eng]
            for sk, v in op.waits:
                if sk[0] == "dma":
                    val = (self.dma_cnt[sk[1]] if v == -1 else v) * 16
                    if wd.get(sk, 0) >= val:
                        continue
                    wd[sk] = val
                    ws.append((sk, val))
                else:
                    if wd.get(sk, -1) >= v.pos:
                        continue
                    wd[sk] = v.pos
                    v.signal = True
                    ws.append((sk, v))
            op.waits = ws
        cnt = {e: 0 for e in ENGS}
        for op in self.ops:
            if op.signal:
                cnt[op.eng] += 1
                op.sigval = cnt[op.eng]
        self.sig_cnt = cnt

    def emit(self, nc):
        self.analyze()
        with ExitStack() as es:
            sems = {}
            for e in ENGS:
                if self.sig_cnt[e] > 0:
                    sems[("eng", e)] = es.enter_context(nc.semaphore(f"s_{e}"))
            for i, key in enumerate(self.dma_cnt):
                sems[("dma", key)] = es.enter_context(nc.semaphore(f"d_{i}"))
            block = es.enter_context(nc.Block())
            streams = {e: [op for op in self.ops if op.eng == e] for e in ENGS}

            def run(eng, ename):
                for op in streams[ename]:
                    for sk, v in op.waits:
                        if sk[0] == "dma":
                            eng.wait_ge(sems[sk], v)
                        else:
                            eng.wait_ge(sems[sk], v.sigval)
                    ins = op.fn(eng)
                    if op.dma_key is not None:
                        ins.then_inc(sems[("dma", op.dma_key)], 16)
                    elif op.signal:
                        ins.then_inc(sems[("eng", ename)], 1)
                if ename == "sp":
                    for key in self.final_keys:
                        eng.wait_ge(sems[("dma", key)], self.dma_cnt[key] * 16)

            @block.tensor
            def _(eng):
                run(eng, "pe")

            @block.scalar
            def _(eng):
                run(eng, "act")

            @block.vector
            def _(eng):
                run(eng, "dve")

            @block.gpsimd
            def _(eng):
                run(eng, "pool")

            @block.sync
            def _(eng):
                run(eng, "sp")


import ml_dtypes

NCORES = 8
HALO = 256
BT = 256
NBLK_P = (HALO + 2048) // BT
POOL_W = (2, 4, 8, 16)
NEG = -240000.0

SM = {}
_o = 0
for _n, _w in (("bmod_e", 24), ("bmod_o", 24), ("gpre_e", 8), ("gpost_e", 8), ("gpre_o", 8), ("gpost_o", 8),
               ("caw", 12), ("cbw", 124), ("cbb", 4), ("lng", 4), ("lnb", 4), ("psc", 4), ("snk", 4),
               ("hm", 1), ("facm1", 64)):
    SM[_n] = (_o, _w)
    _o += _w
NSM = _o


def build_program():
    nc = bass.Bass("TRN2", target_bir_lowering=False)
    P = Prog()
    es = ExitStack()

    def din(name, shape, dt=F32):
        return nc.dram_tensor(name, list(shape), dt, kind="ExternalInput").ap()

    def dout(name, shape, dt=F32):
        return nc.dram_tensor(name, list(shape), dt, kind="ExternalOutput").ap()

    def sb(name, shape, dt=F32):
        return es.enter_context(nc.sbuf_tensor("sb_" + name, list(shape), dt))

    xp = din("xp", [HALO + 2048, 1024]); xs = din("xs", [128, 1024])
    cT_d = din("cT", [128, 8 * 17]); small_d = din("small", [128, NSM]); poolw_d = din("poolw", [128, 512])
    identf_d = din("identf", [128, 128]); biasp_d = din("biasp", [128, 2048], BF16); biass_d = din("biass", [128, 2048], BF16)
    sa_d = din("sa", [32, 512]); sb_d = din("sb", [480, 512]); sc_d = din("sc", [240, 512])
    ck_d = din("ck", [128, 16 * 128]); cv_d = din("cv", [128, 16 * 128])
    wmod_d = [din("wmod_e", [128, 8 * 3072]), din("wmod_o", [128, 8 * 3072])]
    win_e_d = din("win_e", [128, 8 * 3584]); wout_e_d = din("wout_e", [128, 8 * 1024])
    win_o_d = din("win_o", [128, 8 * 2304]); wout_o_d = din("wout_o", [128, 8 * 1024])
    wout_d = [wout_e_d, wout_o_d]
    wout_bf = [nc.dram_tensor(f"wout_bf{l}", [128, 8 * 1024], BF16, kind="Internal").ap() for l in range(2)]
    yp_o = dout("yp", [2048, 1024]); ys_o = dout("ys", [128, 1024])
    pa_o = dout("pa", [2, 512]); sa_o = dout("sa_o", [32, 512])
    pb_o = dout("pb", [30, 512]); sb_o = dout("sb_o", [480, 512])
    pc_o = dout("pc", [15, 512]); sc_o = dout("sc_o", [240, 512])
    pk_o = dout("pk", [128, 128]); sk_o = dout("sk_o", [2048, 128])
    pv_o = dout("pv", [128, 128]); sv_o = dout("sv_o", [2048, 128])

    WE = sb("WE", [128, 8 * 3584], BF16); WOi = sb("WOi", [128, 8 * 2304], BF16); WOUT = sb("WOUT", [128, 8 * 1024], BF16)
    win_e = WE[:].rearrange("p (k n) -> p k n", k=8)
    win_o = WOi[:].rearrange("p (k n) -> p k n", k=8)
    wout3 = WOUT[:].rearrange("p (k n) -> p k n", k=8)
    identf = sb("identf", [128, 128]); identb = sb("identb", [128, 128], BF16); onesb = sb("onesb", [128, 128], BF16)
    biasT = sb("biasT", [128, 2048], BF16)
    small = sb("small", [128, NSM])
    Wp = sb("Wp", [128, 8 * 128], BF16)
    diag3 = sb("diag3", [128, 12 * 128], BF16)
    wb2 = sb("wb2", [128, 124], BF16); b1sc = sb("b1sc", [128, 16]); esk = sb("esk", [128, 4]); epsc = sb("epsc", [128, 2])
    modv = sb("modv", [128, 2 * 3 * 8 * 17])
    xst = [sb(f"xst{i}", [128, 512]) for i in range(2)]
    NXT = 3
    xTs = [sb(f"xT{i}", [128, 8 * BT]) for i in range(NXT)]
    tmp = [None, None] + [sb(f"tmp{i}", [128, BT]) for i in range(2, 7)]
    PT = sb("PT", [128, 2 * 1024], BF16)
    UB = sb("UB", [128, 4928], BF16)
    NDG = 3
    dg = sb("dg", [128, NDG * 1024], BF16)
    ps = [es.enter_context(nc.psum_tensor(f"ps{i}", [128, 512], F32)) for i in range(8)]

    class Ctx:
        pass

    def mkctx(n, nslots, tmps):
        cx = Ctx()
        cx.n = n
        cx.hT = sb(n + "hT", [128, 8 * BT], BF16)
        cx.hT3 = cx.hT[:].rearrange("p (k t) -> p k t", k=8)
        cx.sqmix = sb(n + "sqmix", [128, 8 * BT], BF16)
        cx.sq3 = cx.sqmix[:].rearrange("p (k t) -> p k t", k=8)
        cx.rstd = sb(n + "rstd", [128, BT])
        cx.scr = sb(n + "scr", [128, nslots * BT])
        cx.sqB3 = cx.scr[:, 0:4 * BT].bitcast(BF16).rearrange("p (k t) -> p k t", k=8)
        cx.tmp = tmps
        cx.kr = n + "rstd"
        cx.khc = lambda c: f"{n}hT{c}"
        cx.ksc = lambda c: f"{n}sq{c}"
        cx.kh_all = [f"{n}hT{c}" for c in range(8)]
        cx.ks_all = [f"{n}sq{c}" for c in range(8)]
        cx.sk = lambda i: f"{n}scr{i}"
        return cx

    C0 = mkctx("a", 8, [(tmp[3], "tmp3"), (tmp[4], "tmp4"), (tmp[2], "tmp2"), (tmp[3], "tmp3"), (tmp[4], "tmp4")])
    C1 = mkctx("b", 6, [(tmp[5], "tmp5"), (tmp[6], "tmp6")])
    acc3 = C0.scr[:, 0:4 * BT].rearrange("p (k t) -> p k t", k=4)
    ybs = C0.scr[:, 4 * BT:8 * BT].bitcast(BF16)
    ybb3 = ybs[:, 0:4 * BT].rearrange("p (k t) -> p k t", k=4)
    ysq3 = ybs[:, 4 * BT:8 * BT].rearrange("p (k t) -> p k t", k=4)
    dn = C1.scr[:, 0:512]
    sgd3 = C1.scr[:, 2 * BT:4 * BT].bitcast(BF16).rearrange("p (k t) -> p k t", k=4)
    qT3 = C1.scr[:, 4 * BT:6 * BT].bitcast(BF16).rearrange("p (k t) -> p k t", k=4)

    def sm(name, a=0, b=None):
        o, w = SM[name]
        return small[:, o + a:o + (w if b is None else b)]

    def mv(l, kind, kc, b0, b1):
        o = ((l * 3 + kind) * 8 + kc) * 17
        return modv[:, o + b0:o + b1]

    def xT3(par):
        return xTs[par][:].rearrange("p (k t) -> p k t", k=8)

    def xkc(par, c):
        return f"xT{par}_{c}"

    def xk_all(par):
        return [f"xT{par}_{c}" for c in range(8)]

    class Bufs:
        pass

    def carve(kind):
        b = Bufs()
        if kind == "p":
            b.ts = 1
            b.axc = UB[:, 0:1032].rearrange("p (c t) -> p c t", c=4)
            b.ub = UB[:, 1032:2176].rearrange("p (c t) -> p c t", c=4)
            b.cu = UB[:, 2432:3516].rearrange("p (c t) -> p c t", c=4)
            b.kT = UB[:, 3516:3900]
            b.vt = UB[:, 3900:4284].rearrange("p (t d) -> p t d", t=3)
            b.key = "ubp"
        else:
            b.ts = 16
            b.ub = UB[:, 0:2432].rearrange("p (c t) -> p c t", c=4)
            b.axc = UB[:, 4284:4924].rearrange("p (c t) -> p c t", c=4)
            b.cu = UB[:, 2432:3904].rearrange("p (c t) -> p c t", c=4)
            b.kT = UB[:, 3904:4032]
            b.vt = UB[:, 4032:4160].rearrange("p (t d) -> p t d", t=1)
            b.key = "ubs"
        return b

    bank_i = [0]
    dg_cnt = [0]
    xin_cnt = [0]

    CONV_BANK = 7

    def nb():
        i = bank_i[0] % 7
        bank_i[0] += 1
        return i

    def stage_slot():
        i = xin_cnt[0] % 2
        xin_cnt[0] += 1
        return i

    def ld(eng, key, out, in_, w, mode="group"):
        P.dma(eng, key, lambda e: e.dma_start(out=out, in_=in_), writes=w, mode=mode)

    cT = tmp[4][:, 0:136]
    scT = tmp[3][:, 0:68].bitcast(BF16)
    poolw = xst[0][:, 0:512]
    ld("sp", "c0", small[:], small_d, ["small"]); ld("sp", "c0", cT, cT_d, ["tmp4"])
    ld("sp", "c0", identf[:], identf_d, ["identf"]); ld("sp", "c0", poolw, poolw_d, ["xst0"])
    ld("sp", "c0", biasT[:], biasp_d, ["biasT"])

    def load_wpiece(src_d, ncols_total, c0, c1, slot, key):
        P.dma("pool", key, lambda e: e.dma_start(out=wout3[:, :, slot * 256:slot * 256 + (c1 - c0)],
                                                 in_=src_d.rearrange("p (k n) -> p k n", k=8)[:, :, c0:c1]), writes=[f"WOUT{slot}"], mode="slot")

    P.act(lambda e: e.copy(out=identb[:], in_=identf[:]), reads=["identf"], writes=["identb"])
    P.dve(lambda e: e.memset(onesb[:], 1.0), writes=["onesb"])
    P.dve(lambda e: e.memset(epsc[:, 0:1], 1e-6), writes=["epsc"])
    P.dve(lambda e: e.memset(epsc[:, 1:2], 1e-5), writes=["epsc"])
    P.act(lambda e: e.activation(out=scT, in_=cT, func=AF.Silu), reads=["tmp4"], writes=["tmp3"])
    P.act(lambda e: e.activation(out=esk[:], in_=sm("snk"), func=AF.Exp), reads=["small"], writes=["esk"])
    P.dve(lambda e: e.tensor_scalar(out=wb2[:], in0=sm("cbw"), scalar1=0.5, scalar2=None, op0=ALU.mult), reads=["small"], writes=["wb2"])
    for c in range(4):
        for j in range(3):
            P.dve(lambda e, c=c, j=j: e.tensor_scalar(out=diag3[:, (c * 3 + j) * 128:(c * 3 + j + 1) * 128], in0=identb[:],
                                                      scalar1=sm("caw", c * 3 + j, c * 3 + j + 1), scalar2=None, op0=ALU.mult),
                  reads=["identb", "small"], writes=["diag3"])
    for g, w in enumerate(POOL_W):
        P.dve(lambda e, g=g, w=w: e.tensor_scalar(out=Wp[:, (2 * g) * 128:(2 * g + 1) * 128], in0=xst[0][:, g * 128:(g + 1) * 128],
                                                  scalar1=(1.0 / w - 1.0), scalar2=None, op0=ALU.mult), reads=["xst0"], writes=["Wp"])
        P.dve(lambda e, g=g, w=w: e.tensor_scalar(out=Wp[:, (2 * g + 1) * 128:(2 * g + 2) * 128], in0=xst[0][:, g * 128:(g + 1) * 128],
                                                  scalar1=(1.0 / w), scalar2=None, op0=ALU.mult), reads=["xst0"], writes=["Wp"])
    scT3 = scT.rearrange("p (k b) -> p k b", k=8)

    def do_mod(l):
        bm_, gpre, gpost = (("bmod_e", "gpre_e", "gpost_e"), ("bmod_o", "gpre_o", "gpost_o"))[l]
        P.dve(lambda e: e.tensor_scalar(out=b1sc[:, l * 8:(l + 1) * 8], in0=sm(bm_, 8, 16), scalar1=1.0, scalar2=None, op0=ALU.add),
              reads=["small"], writes=["b1sc"])
        for pc_ in range(12):
            slot = pc_ % 4
            load_wpiece(wmod_d[l], 3072, pc_ * 256, (pc_ + 1) * 256, slot, f"wmod{slot}")
            for q in range(2):
                fj = pc_ * 2 + q
                bi_ = nb()
                for kc in range(8):
                    P.pe(lambda e, bi_=bi_, q=q, kc=kc, slot=slot: e.matmul(ps[bi_][:, 0:17], lhsT=wout3[:, kc, slot * 256 + q * 128:slot * 256 + (q + 1) * 128],
                                                                         rhs=scT3[:, kc, :], start=(kc == 0), stop=(kc == 7)),
                         reads=[f"WOUT{slot}", "tmp3"], writes=[f"ps{bi_}"])
                j = fj % 8
                if fj < 8:
                    P.act(lambda e, bi_=bi_, fj=fj, j=j: e.activation(out=mv(l, 0, j, 0, 17), in_=ps[bi_][:, 0:17], func=AF.Identity,
                                                                    bias=sm(bm_, fj, fj + 1), scale=1.0), reads=[f"ps{bi_}", "small"], writes=["modv"])
                elif fj < 16:
                    P.dve(lambda e, bi_=bi_, j=j: e.tensor_scalar(out=mv(l, 1, j, 0, 17), in0=ps[bi_][:, 0:17], scalar1=b1sc[:, l * 8 + j:l * 8 + j + 1],
                                                                scalar2=sm(gpre, j, j + 1), op0=ALU.add, op1=ALU.mult),
                          reads=[f"ps{bi_}", "small", "b1sc"], writes=["modv"])
                else:
                    P.dve(lambda e, bi_=bi_, fj=fj, j=j: e.tensor_scalar(out=mv(l, 2, j, 0, 17), in0=ps[bi_][:, 0:17], scalar1=sm(bm_, fj, fj + 1),
                                                                       scalar2=sm(gpost, j, j + 1), op0=ALU.add, op1=ALU.mult),
                          reads=[f"ps{bi_}", "small"], writes=["modv"])

    do_mod(0)
    do_mod(1)
    for i in range(7):
        ld("pool", f"we{i}", win_e[:, :, i * 512:(i + 1) * 512],
           win_e_d.rearrange("p (k n) -> p k n", k=8)[:, :, i * 512:(i + 1) * 512], [f"WE{i}"])
    P.dma("pool", "wcast0", lambda e: e.dma_start(out=wout_bf[0], in_=wout_d[0]), writes=["woutbf0"], mode="slot")
    ld("pool", "wo_i", win_o, win_o_d.rearrange("p (k n) -> p k n", k=8), ["WOi"])
    P.dma("pool", "wcast1", lambda e: e.dma_start(out=wout_bf[1], in_=wout_d[1]), writes=["woutbf1"], mode="slot")
    P.dma("sp", "d2d", lambda e: e.dma_start(out=sb_o[0:22 * 16, :], in_=sb_d[8 * 16:30 * 16, :]), final=True)
    P.dma("sp", "d2d", lambda e: e.dma_start(out=sc_o[0:7 * 16, :], in_=sc_d[8 * 16:15 * 16, :]), final=True)
    P.dma("sp", "d2d", lambda e: e.dma_start(out=sk_o[0:120 * 16, :], in_=ck_d.rearrange("p (s d) -> (p s) d", s=16)[8 * 16:128 * 16, :]), final=True)
    P.dma("sp", "d2d", lambda e: e.dma_start(out=sv_o[0:120 * 16, :], in_=cv_d.rearrange("p (s d) -> (p s) d", s=16)[8 * 16:128 * 16, :]), final=True)

    def load_x(src_rows, par, tcol):
        x3 = xT3(par)
        for h in range(2):
            s = stage_slot()
            P.dma("sp", f"xst{s}", lambda e, s=s, h=h: e.dma_start(out=xst[s][:], in_=src_rows[:, h * 512:(h + 1) * 512]), writes=[f"xst{s}"])
            bk = nb()
            for q in range(4):
                P.pe(lambda e, bk=bk, q=q, s=s: e.transpose(out=ps[bk][:, q * 128:(q + 1) * 128], in_=xst[s][:, q * 128:(q + 1) * 128], identity=identf[:]),
                     reads=[f"xst{s}", "identf"], writes=[f"ps{bk}"])
            P.act(lambda e, bk=bk, h=h: e.copy(out=x3[:, h * 4:(h + 1) * 4, tcol:tcol + 128], in_=ps[bk][:].rearrange("p (q t) -> p q t", q=4)),
                  reads=[f"ps{bk}"], writes=[xkc(par, h * 4 + q_) for q_ in range(4)])

    def store_y(dst_rows, par, tcol):
        x3 = xT3(par)
        for h in range(2):
            s = stage_slot()
            bk = nb()
            for q in range(4):
                kc = h * 4 + q
                P.pe(lambda e, bk=bk, q=q, kc=kc: e.transpose(out=ps[bk][:, q * 128:(q + 1) * 128], in_=x3[:, kc, tcol:tcol + 128], identity=identf[:]),
                     reads=[xkc(par, kc), "identf"], writes=[f"ps{bk}"])
            P.act(lambda e, bk=bk, s=s: e.copy(out=xst[s][:], in_=ps[bk][:]), reads=[f"ps{bk}"], writes=[f"xst{s}"])
            P.dma("sp", f"xst{s}", lambda e, s=s, h=h: e.dma_start(out=dst_rows[:, h * 512:(h + 1) * 512], in_=xst[s][:]), reads=[f"xst{s}"], final=True)

    def stats_tail(cx, sqv3, sqkeys, nt):
        bk = nb()
        for kc in range(8):
            P.pe(lambda e, kc=kc: e.matmul(ps[bk][:, 0:nt], lhsT=onesb[:], rhs=sqv3[:, kc, 0:nt], start=(kc == 0), stop=(kc == 7)),
                 reads=[sqkeys[kc] if len(sqkeys) == 8 else sqkeys[kc // 2], "onesb"], writes=[f"ps{bk}"])
        P.act(lambda e: e.activation(out=cx.rstd[:, 0:nt], in_=ps[bk][:, 0:nt], func=AF.Sqrt, bias=epsc[:, 0:1], scale=1.0 / 1024),
              reads=[f"ps{bk}", "epsc"], writes=[cx.kr])
        P.dve(lambda e: e.reciprocal(out=cx.rstd[:, 0:nt], in_=cx.rstd[:, 0:nt]), reads=[cx.kr], writes=[cx.kr])

    def bc_mod(l, kind, kc):
        return mv(l, kind, kc, 1, 17).unsqueeze(1).broadcast_to([128, 8, 16])

    def tok3(ap2):
        return ap2.rearrange("p (i s) -> p i s", s=16)

    def prenorm(cx, l, nt, sample, par):
        x3 = xT3(par)
        for hh in range(4):
            P.act(lambda e, hh=hh: e.activation(out=cx.sq3[:, hh * 2:(hh + 1) * 2, 0:nt], in_=x3[:, hh * 2:(hh + 1) * 2, 0:nt], func=AF.Square),
                  reads=xk_all(par)[hh * 2:(hh + 1) * 2], writes=cx.ks_all[hh * 2:(hh + 1) * 2])
        yield
        stats_tail(cx, cx.sq3, cx.ks_all, nt)
        yield
        for kc in range(8):
            t, tk = cx.tmp[kc % 2]
            if not sample:
                P.dve(lambda e, kc=kc, t=t: e.scalar_tensor_tensor(out=t[:, 0:nt], in0=x3[:, kc, 0:nt], scalar=mv(l, 1, kc, 0, 1), in1=cx.rstd[:, 0:nt],
                                                                   op0=ALU.mult, op1=ALU.mult), reads=[xkc(par, kc), cx.kr, "modv"], writes=[tk])
                P.act(lambda e, kc=kc, t=t: e.activation(out=cx.hT3[:, kc, 0:nt], in_=t[:, 0:nt], func=AF.Identity, bias=mv(l, 0, kc, 0, 1), scale=1.0),
                      reads=[tk, "modv"], writes=[cx.khc(kc)])
            else:
                P.dve(lambda e, kc=kc, t=t: e.tensor_tensor(out=t[:, 0:nt], in0=x3[:, kc, 0:nt], in1=cx.rstd[:, 0:nt], op=ALU.mult), reads=[xkc(par, kc), cx.kr], writes=[tk])
                P.dve(lambda e, kc=kc, t=t: e.tensor_tensor(out=tok3(t[:, 0:nt]), in0=tok3(t[:, 0:nt]), in1=bc_mod(l, 1, kc), op=ALU.mult),
                      reads=[tk, "modv"], writes=[tk])
                P.dve(lambda e, kc=kc, t=t: e.tensor_tensor(out=tok3(cx.hT3[:, kc, 0:nt]), in0=tok3(t[:, 0:nt]), in1=bc_mod(l, 0, kc), op=ALU.add),
                      reads=[tk, "modv"], writes=[cx.khc(kc)])
        yield

    def group(cx, W3, wkey, col0, nt, ncols=128):
        bk = nb()
        for kc in range(8):
            P.pe(lambda e, kc=kc: e.matmul(ps[bk][0:ncols, 0:nt], lhsT=W3[:, kc, col0:col0 + ncols], rhs=cx.hT3[:, kc, 0:nt], start=(kc == 0), stop=(kc == 7)),
                 reads=[wkey, cx.khc(kc)], writes=[f"ps{bk}"])
        return bk

    def load_wout_slot(l, sl):
        P.dma("sp", f"wout{sl}", lambda e: e.dma_start(out=wout3[:, :, sl * 256:(sl + 1) * 256],
                                                      in_=wout_bf[l].rearrange("p (k n) -> p k n", k=8)[:, :, sl * 256:(sl + 1) * 256]),
              reads=[f"woutbf{l}"], writes=[f"WOUT{sl}"], mode="slot")

    wout_lock = [None] * 4

    def try_wout(cx, l, st):
        for sl in range(4):
            if not st["held"][sl] and wout_lock[sl] is None:
                wout_lock[sl] = cx.n
                load_wout_slot(l, sl)
                st["held"][sl] = True

    def out_proj(cx, l, nt, sample, par, st):
        x3 = xT3(par)
        while not all(st["held"]):
            try_wout(cx, l, st)
            if not all(st["held"]):
                yield
        for dc in range(8):
            bk = nb()
            for kc in range(8):
                P.pe(lambda e, kc=kc, dc=dc, bk=bk: e.matmul(ps[bk][:, 0:nt], lhsT=wout3[:, kc, dc * 128:(dc + 1) * 128], rhs=cx.sq3[:, kc, 0:nt], start=(kc == 0), stop=(kc == 7)),
                     reads=[f"WOUT{dc // 2}", cx.ksc(kc)], writes=[f"ps{bk}"])
            P.act(lambda e, dc=dc, bk=bk: e.activation(out=cx.sqB3[:, dc, 0:nt], in_=ps[bk][:, 0:nt], func=AF.Square), reads=[f"ps{bk}"], writes=[cx.sk(dc // 2)])
            P.act(lambda e, dc=dc, bk=bk: e.copy(out=cx.hT3[:, dc, 0:nt], in_=ps[bk][:, 0:nt]), reads=[f"ps{bk}"], writes=[cx.khc(dc)])
            if dc % 2 == 1:
                wout_lock[dc // 2] = None
            yield
        stats_tail(cx, cx.sqB3, [cx.sk(i) for i in range(4)], nt)
        yield
        for dc in range(8):
            t, tk = cx.tmp[dc % 2]
            if not sample:
                P.dve(lambda e, dc=dc, t=t: e.scalar_tensor_tensor(out=t[:, 0:nt], in0=cx.hT3[:, dc, 0:nt], scalar=mv(l, 2, dc, 0, 1), in1=cx.rstd[:, 0:nt],
                                                                   op0=ALU.mult, op1=ALU.mult), reads=[cx.khc(dc), cx.kr, "modv"], writes=[tk])
            else:
                P.dve(lambda e, dc=dc, t=t: e.tensor_tensor(out=t[:, 0:nt], in0=cx.hT3[:, dc, 0:nt], in1=cx.rstd[:, 0:nt], op=ALU.mult), reads=[cx.khc(dc), cx.kr], writes=[tk])
                P.dve(lambda e, dc=dc, t=t: e.tensor_tensor(out=tok3(t[:, 0:nt]), in0=tok3(t[:, 0:nt]), in1=bc_mod(l, 2, dc), op=ALU.mult),
                      reads=[tk, "modv"], writes=[tk])
            P.pool(lambda e, dc=dc, t=t: e.tensor_tensor(out=x3[:, dc, 0:nt], in0=x3[:, dc, 0:nt], in1=t[:, 0:nt], op=ALU.add), reads=[xkc(par, dc), tk], writes=[xkc(par, dc)])
        yield

    def ld_(ap, a, b):
        return ap[:, :, a:b] if len(ap.shape) == 3 else ap[:, a:b]

    def carry(buf, S, L, first, use_hm, keys):
        if first:
            P.pool(lambda e: e.memset(ld_(buf, 0, S), 0.0), writes=keys)
        elif use_hm:
            P.act(lambda e: e.activation(out=ld_(buf, 0, S), in_=ld_(buf, L, L + S), func=AF.Copy, scale=sm("hm")),
                  reads=keys + ["small"], writes=keys)
        else:
            P.pool(lambda e: e.tensor_copy(out=ld_(buf, 0, S), in_=ld_(buf, L, L + S)), reads=keys, writes=keys)

    def state_out(src3, nch, tcol0, srckeys, dmas, scale=1.0):
        s = stage_slot()
        bk = nb()
        pb = ps[bk][:].bitcast(BF16)
        for c in range(nch):
            P.pe(lambda e, c=c: e.transpose(out=pb[:, c * 128:(c + 1) * 128], in_=src3[:, c, tcol0:tcol0 + 128], identity=identb[:]),
                 reads=list(srckeys) + ["identb"], writes=[f"ps{bk}"])
        P.act(lambda e: e.activation(out=xst[s][:, 0:nch * 128], in_=pb[:, 0:nch * 128], func=AF.Copy, scale=scale), reads=[f"ps{bk}"], writes=[f"xst{s}"])
        for (dst, r0, r1) in dmas:
            P.dma("sp", f"xst{s}", lambda e, dst=dst, r0=r0, r1=r1: e.dma_start(out=dst, in_=xst[s][r0:r1, 0:nch * 128]), reads=[f"xst{s}"], final=True)

    def blk(bi):
        b = Bufs()
        b.sample = (bi == NBLK_P)
        b.nt = 128 if b.sample else BT
        b.ntile = b.nt // 128
        b.par = bi % NXT
        b.B = carve("s" if b.sample else "p")
        b.first, b.use_hm, b.last_p, b.halo = (bi == 0), (bi == 1), (bi == NBLK_P - 1), (bi == 0)
        k = b.B.key
        b.kA, b.kB, b.kC, b.kK, b.kV = k + "A", k + "B", k + "C", k + "K", k + "V"
        b.kBc = [k + "B" + str(c) for c in range(4)]
        pL0 = ["ubpA", "ubpB"] + ["ubpB" + str(c) for c in range(4)]
        pL1 = ["ubpC", "ubpK", "ubpV"]
        b.wA = [b.kA]
        b.wB = [[b.kBc[c]] + (pL0 if b.sample else []) for c in range(4)]
        b.wC = [b.kC] + (pL1 if b.sample else [])
        b.wK = [b.kK] + (pL1 if b.sample else [])
        b.wV = [b.kV] + (pL1 if b.sample else [])
        return b

    def load_state(src_d, nrows, dst3, col0, dkeys):
        r = 0
        while r < nrows:
            n = min(128, nrows - r)
            s = stage_slot()
            P.dma("sp", f"xst{s}", lambda e, r=r, n=n, s=s: e.dma_start(out=xst[s][0:n, 0:512], in_=src_d[r:r + n, :]), writes=[f"xst{s}"])
            bk = nb()
            for c in range(4):
                P.pe(lambda e, c=c, n=n, s=s, bk=bk: e.transpose(out=ps[bk][:, c * 128:c * 128 + n], in_=xst[s][0:n, c * 128:(c + 1) * 128], identity=identf[0:n, 0:n]),
                     reads=[f"xst{s}", "identf"], writes=[f"ps{bk}"])
            P.act(lambda e, r=r, n=n, bk=bk: e.copy(out=dst3[:, :, col0 + r:col0 + r + n], in_=ps[bk][:].rearrange("p (c t) -> p c t", c=4)[:, :, 0:n]),
                  reads=[f"ps{bk}"], writes=dkeys)
            r += n

    xoi = (NBLK_P + 1) % NXT
    xo = xTs[xoi][:].bitcast(BF16)
    kTc = xo[:, 0:2048].rearrange("p (s t) -> p s t", s=16)
    Vc = xo[:, 2048:4096].rearrange("p (s d) -> p s d", s=16)
    xok_all = xk_all(xoi)

    def sample_loads_L0(b):
        pL0 = ["ubpA", "ubpB"] + ["ubpB" + str(c) for c in range(4)]
        load_state(sa_d, 32, b.B.axc, 0, ["ubsA"])
        load_state(sb_d, 480, b.B.ub, 0, ["ubsB"] + pL0)
        P.act(lambda e: e.activation(out=b.B.ub[:, :, 0:480], in_=b.B.ub[:, :, 0:480], func=AF.Copy, scale=2.0), reads=["ubsB"], writes=["ubsB"])

    def sample_loads_L1(b):
        pL1 = ["ubpC", "ubpK", "ubpV"]
        ld("sp", "c1", biasT[:], biass_d, ["biasT"], mode="slot")
        load_state(sc_d, 240, b.B.cu, 0, ["ubsC"] + pL1)
        for s_ in range(0, 16, 4):
            sl = stage_slot()
            P.dma("sp", f"xst{sl}", lambda e, s_=s_, sl=sl: e.dma_start(out=xst[sl][:, :], in_=ck_d[:, s_ * 128:(s_ + 4) * 128]), writes=[f"xst{sl}"])
            bk = nb()
            for q in range(4):
                P.pe(lambda e, q=q, sl=sl, bk=bk: e.transpose(out=ps[bk][:, q * 128:(q + 1) * 128], in_=xst[sl][:, q * 128:(q + 1) * 128], identity=identf[:]),
                     reads=[f"xst{sl}", "identf"], writes=[f"ps{bk}"])
            P.act(lambda e, s_=s_, bk=bk: e.copy(out=kTc[:, s_:s_ + 4, :], in_=ps[bk][:].rearrange("p (q t) -> p q t", q=4)), reads=[f"ps{bk}"], writes=["kTc"] + xok_all)
            sl = stage_slot()
            P.dma("sp", f"xst{sl}", lambda e, s_=s_, sl=sl: e.dma_start(out=xst[sl][:, :], in_=cv_d[:, s_ * 128:(s_ + 4) * 128]), writes=[f"xst{sl}"])
            P.act(lambda e, s_=s_, sl=sl: e.copy(out=Vc[:, s_:s_ + 4, :], in_=xst[sl][:].rearrange("p (s d) -> p s d", s=4)), reads=[f"xst{sl}"], writes=["Vc"] + xok_all)

    def gen_L0(bi):
        b = blk(bi)
        cx, B, nt, sample, par, ts = C0, b.B, b.nt, b.sample, b.par, b.B.ts
        st = {"held": [False] * 4}
        if sample:
            sample_loads_L0(b)
            yield
        for t in range(b.ntile):
            load_x(xs if sample else xp[bi * BT + t * 128: bi * BT + (t + 1) * 128, :], par, t * 128)
            yield
        if not sample:
            carry(B.axc, 2, BT, b.first, b.use_hm, [b.kA])
            carry(B.ub, 30, BT, b.first, b.use_hm, [b.kB] + b.kBc)
        yield from prenorm(cx, 0, nt, sample, par)
        t2, k2 = cx.tmp[2]; t3, k3 = cx.tmp[3]; t4, k4 = cx.tmp[4]
        for c in range(4):
            b1 = group(cx, win_e, "WE0", 0 * 512 + c * 128, nt)
            P.act(lambda e, b1=b1: e.copy(out=t2[:, 0:nt], in_=ps[b1][:, 0:nt]), reads=[f"ps{b1}"], writes=[k2])
            b2 = group(cx, win_e, "WE2", 2 * 512 + c * 128, nt)
            P.dve(lambda e, b2=b2, c=c: e.tensor_tensor(out=B.axc[:, c, 2 * ts:2 * ts + nt], in0=ps[b2][:, 0:nt], in1=t2[:, 0:nt], op=ALU.mult),
                  reads=[f"ps{b2}", k2], writes=[b.kA])
            yield
            b3 = group(cx, win_e, "WE1", 1 * 512 + c * 128, nt)
            b4 = group(cx, win_e, "WE3", 3 * 512 + c * 128, nt)
            P.act(lambda e, b4=b4: e.activation(out=t3[:, 0:nt], in_=ps[b4][:, 0:nt], func=AF.Silu), reads=[f"ps{b4}"], writes=[k3])
            P.dve(lambda e, b3=b3: e.tensor_tensor(out=t4[:, 0:nt], in0=ps[b3][:, 0:nt], in1=t3[:, 0:nt], op=ALU.mult), reads=[f"ps{b3}", k3], writes=[k4])
            yield
            b5 = nb()
            for j in range(3):
                P.pe(lambda e, j=j, c=c, b5=b5: e.matmul(ps[b5][:, 0:nt], lhsT=diag3[:, (c * 3 + j) * 128:(c * 3 + j + 1) * 128],
                                                        rhs=B.axc[:, c, j * ts:j * ts + nt], start=(j == 0), stop=(j == 2)), reads=["diag3", b.kA], writes=[f"ps{b5}"])
            P.dve(lambda e, b5=b5, c=c: e.tensor_tensor(out=cx.sq3[:, c, 0:nt], in0=ps[b5][:, 0:nt], in1=t4[:, 0:nt], op=ALU.mult),
                  reads=[f"ps{b5}", k4], writes=[cx.ksc(c)])
            yield
        if b.last_p or sample:
            state_out(B.axc, 4, 2 * ts + nt - 128, [b.kA], [(sa_o[0:32, :], 96, 128)] if sample else [(pa_o[:, :], 126, 128)])
        for c in range(4):
            bv = group(cx, win_e, "WE4", 4 * 512 + c * 128, nt)
            bg = group(cx, win_e, "WE5", 5 * 512 + c * 128, nt)
            P.act(lambda e, bg=bg: e.activation(out=t2[:, 0:nt], in_=ps[bg][:, 0:nt], func=AF.Tanh, scale=0.5), reads=[f"ps{bg}"], writes=[k2])
            P.dve(lambda e, bv=bv, c=c: e.scalar_tensor_tensor(out=B.ub[:, c, 30 * ts:30 * ts + nt], in0=t2[:, 0:nt], scalar=1.0, in1=ps[bv][:, 0:nt],
                                                              op0=ALU.add, op1=ALU.mult), reads=[f"ps{bv}", k2], writes=b.wB[c])
            yield
        if b.last_p or sample:
            state_out(B.ub, 4, 30 * ts + nt - 128, b.kBc, [(sb_o[22 * 16:30 * 16, :], 0, 128)] if sample else [(pb_o[:, :], 98, 128)], scale=0.5)
        pieces = []
        for c in range(4):
            j = 0
            while j < 31:
                n = min(8, 31 - j)
                pieces.append((c, j, n))
                j += n

        def gen_piece(pidx):
            c, j, n = pieces[pidx]
            pi = (dg_cnt[0] + pidx) % NDG
            dgp = dg[:, pi * 1024:pi * 1024 + n * 128]
            P.dve(lambda e, dgp=dgp, n=n, c=c, j=j: e.tensor_tensor(
                out=dgp.rearrange("p (j m) -> p j m", j=n), in0=identb[:].unsqueeze(1).broadcast_to([128, n, 128]),
                in1=wb2[:, c * 31 + j:c * 31 + j + n].unsqueeze(2).broadcast_to([128, n, 128]), op=ALU.mult),
                reads=["identb", "wb2"], writes=[f"dg{pi}"])

        gen_piece(0)
        gen_piece(1)
        yield
        bc_ = None
        for pidx, (c, j, n) in enumerate(pieces):
            if j == 0:
                bc_ = CONV_BANK
            pi = (dg_cnt[0] + pidx) % NDG
            dgp = dg[:, pi * 1024:pi * 1024 + n * 128]
            for jj in range(n):
                P.pe(lambda e, dgp=dgp, jj=jj, j=j, c=c, bc_=bc_: e.matmul(ps[bc_][:, 0:nt], lhsT=dgp[:, jj * 128:(jj + 1) * 128],
                                                                        rhs=B.ub[:, c, (j + jj) * ts:(j + jj) * ts + nt],
                                                                        start=(j + jj == 0), stop=(j + jj == 30)),
                     reads=[f"dg{pi}", b.kBc[c], b.kB], writes=[f"ps{bc_}"])
            if pidx + 2 < len(pieces):
                gen_piece(pidx + 2)
            if j + n == 31:
                P.act(lambda e, c=c, bc_=bc_: e.activation(out=ybb3[:, c, 0:nt], in_=ps[bc_][:, 0:nt], func=AF.Identity, bias=sm("cbb", c, c + 1), scale=1.0),
                      reads=[f"ps{bc_}", "small"], writes=[cx.sk(4), cx.sk(5)])
                P.act(lambda e, c=c, bc_=bc_: e.activation(out=ysq3[:, c, 0:nt], in_=ps[bc_][:, 0:nt], func=AF.Square, bias=sm("cbb", c, c + 1), scale=1.0),
                      reads=[f"ps{bc_}", "small"], writes=[cx.sk(6), cx.sk(7)])
                P.act(lambda e, c=c, bc_=bc_: e.activation(out=acc3[:, c, 0:nt], in_=ps[bc_][:, 0:nt], func=AF.Identity, bias=sm("cbb", c, c + 1), scale=1.0),
                      reads=[f"ps{bc_}", "small"], writes=[cx.sk(c)])
                bgt = group(cx, win_e, "WE6", 6 * 512 + c * 128, nt)
                P.act(lambda e, bgt=bgt, c=c: e.activation(out=cx.sq3[:, 4 + c, 0:nt], in_=ps[bgt][:, 0:nt], func=AF.Silu), reads=[f"ps{bgt}"], writes=[cx.ksc(4 + c)])
            yield
        dg_cnt[0] += len(pieces)
        yield "tail"
        bm = nb()
        for c in range(4):
            P.pe(lambda e, c=c: e.matmul(ps[bm][:, 0:nt], lhsT=onesb[:], rhs=ybb3[:, c, 0:nt], start=(c == 0), stop=(c == 3)),
                 reads=[cx.sk(4), cx.sk(5), "onesb"], writes=[f"ps{bm}"])
        be = nb()
        for c in range(4):
            P.pe(lambda e, c=c: e.matmul(ps[be][:, 0:nt], lhsT=onesb[:], rhs=ysq3[:, c, 0:nt], start=(c == 0), stop=(c == 3)),
                 reads=[cx.sk(6), cx.sk(7), "onesb"], writes=[f"ps{be}"])
        yield
        mean, var = t2, t3
        P.dve(lambda e: e.tensor_scalar(out=mean[:, 0:nt], in0=ps[bm][:, 0:nt], scalar1=1.0 / 512, scalar2=None, op0=ALU.mult), reads=[f"ps{bm}"], writes=[k2])
        P.dve(lambda e: e.tensor_tensor(out=var[:, 0:nt], in0=mean[:, 0:nt], in1=mean[:, 0:nt], op=ALU.mult), reads=[k2], writes=[k3])
        P.dve(lambda e: e.scalar_tensor_tensor(out=var[:, 0:nt], in0=ps[be][:, 0:nt], scalar=1.0 / 512, in1=var[:, 0:nt], op0=ALU.mult, op1=ALU.subtract),
              reads=[f"ps{be}", k3], writes=[k3])
        P.act(lambda e: e.activation(out=var[:, 0:nt], in_=var[:, 0:nt], func=AF.Sqrt, bias=epsc[:, 1:2], scale=1.0), reads=[k3, "epsc"], writes=[k3])
        P.dve(lambda e: e.reciprocal(out=var[:, 0:nt], in_=var[:, 0:nt]), reads=[k3], writes=[k3])
        try_wout(cx, 0, st)
        yield
        for c in range(4):
            P.dve(lambda e, c=c: e.tensor_tensor(out=acc3[:, c, 0:nt], in0=acc3[:, c, 0:nt], in1=mean[:, 0:nt], op=ALU.subtract), reads=[cx.sk(c), k2], writes=[cx.sk(c)])
            P.dve(lambda e, c=c: e.tensor_tensor(out=acc3[:, c, 0:nt], in0=acc3[:, c, 0:nt], in1=var[:, 0:nt], op=ALU.mult), reads=[cx.sk(c), k3], writes=[cx.sk(c)])
        for c in range(4):
            P.act(lambda e, c=c: e.activation(out=acc3[:, c, 0:nt], in_=acc3[:, c, 0:nt], func=AF.Silu, bias=sm("lnb", c, c + 1), scale=sm("lng", c, c + 1)),
                  reads=[cx.sk(c), "small"], writes=[cx.sk(c)])
        for c in range(4):
            P.dve(lambda e, c=c: e.tensor_tensor(out=cx.sq3[:, 4 + c, 0:nt], in0=acc3[:, c, 0:nt], in1=cx.sq3[:, 4 + c, 0:nt], op=ALU.mult), reads=[cx.sk(c), cx.ksc(4 + c)], writes=[cx.ksc(4 + c)])
        yield
        yield from out_proj(cx, 0, nt, sample, par, st)

    def gen_L1(bi):
        b = blk(bi)
        cx, B, nt, sample, par, ts = C1, b.B, b.nt, b.sample, b.par, b.B.ts
        st = {"held": [False] * 4}
        ntile = b.ntile
        if sample:
            sample_loads_L1(b)
            yield
        if not sample:
            carry(B.cu, 15, BT, b.first, b.use_hm, [b.kC])
            carry(B.kT, 128, BT, b.first, False, [b.kK])
            if b.first:
                P.pool(lambda e: e.memset(B.vt[:, 0, :], 0.0), writes=[b.kV])
            else:
                P.pool(lambda e: e.tensor_copy(out=B.vt[:, 0, :], in_=B.vt[:, 2, :]), reads=[b.kV], writes=[b.kV])
        yield from prenorm(cx, 1, nt, sample, par)
        t3, k3 = cx.tmp[0]; t4, k4 = cx.tmp[1]
        for g, w in enumerate(POOL_W):
            bu = group(cx, win_o, "WOi", 0 + g * 128, nt)
            P.act(lambda e, bu=bu, g=g: e.copy(out=B.cu[:, g, 15 * ts:15 * ts + nt], in_=ps[bu][:, 0:nt]), reads=[f"ps{bu}"], writes=b.wC)
            if b.halo:
                continue
            bgc = group(cx, win_o, "WOi", 512 + g * 128, nt)
            P.act(lambda e, bgc=bgc, g=g: e.activation(out=cx.sq3[:, g, 0:nt], in_=ps[bgc][:, 0:nt], func=AF.Silu), reads=[f"ps{bgc}"], writes=[cx.ksc(g)])
            yield
            bp = nb()
            for j in range(w):
                P.pe(lambda e, j=j, g=g, bp=bp, w=w: e.matmul(ps[bp][:, 0:nt], lhsT=Wp[:, (2 * g + (1 if j else 0)) * 128:(2 * g + (1 if j else 0) + 1) * 128],
                                                        rhs=B.cu[:, g, (15 - j) * ts:(15 - j) * ts + nt], start=(j == 0), stop=(j == w - 1)),
                     reads=["Wp", b.kC], writes=[f"ps{bp}"])
            if b.use_hm:
                bq = nb()
                for j in range(w):
                    P.pe(lambda e, j=j, g=g, bq=bq, w=w: e.matmul(ps[bq][:, 0:16], lhsT=Wp[:, (2 * g + 1) * 128:(2 * g + 2) * 128],
                                                            rhs=B.cu[:, g, (15 - j):(15 - j) + 16], start=(j == 0), stop=(j == w - 1)),
                         reads=["Wp", b.kC], writes=[f"ps{bq}"])
                P.dve(lambda e, bq=bq, g=g: e.tensor_tensor(out=t3[:, 0:16], in0=ps[bq][:, 0:16], in1=sm("facm1", g * 16, g * 16 + 16), op=ALU.mult),
                      reads=[f"ps{bq}", "small"], writes=[k3])
                P.act(lambda e, bp=bp: e.copy(out=t4[:, 0:nt], in_=ps[bp][:, 0:nt]), reads=[f"ps{bp}"], writes=[k4])
                P.dve(lambda e: e.tensor_tensor(out=t4[:, 0:16], in0=t4[:, 0:16], in1=t3[:, 0:16], op=ALU.add), reads=[k3, k4], writes=[k4])
                P.dve(lambda e, g=g: e.scalar_tensor_tensor(out=cx.sq3[:, g, 0:nt], in0=t4[:, 0:nt], scalar=sm("psc", g, g + 1), in1=cx.sq3[:, g, 0:nt], op0=ALU.mult, op1=ALU.mult),
                      reads=[k4, cx.ksc(g), "small"], writes=[cx.ksc(g)])
            else:
                P.dve(lambda e, bp=bp, g=g: e.scalar_tensor_tensor(out=cx.sq3[:, g, 0:nt], in0=ps[bp][:, 0:nt], scalar=sm("psc", g, g + 1), in1=cx.sq3[:, g, 0:nt], op0=ALU.mult, op1=ALU.mult),
                      reads=[f"ps{bp}", cx.ksc(g), "small"], writes=[cx.ksc(g)])
            yield
        if b.last_p or sample:
            state_out(B.cu, 4, 15 * ts + nt - 128, [b.kC], [(sc_o[7 * 16:15 * 16, :], 0, 128)] if sample else [(pc_o[:, :], 113, 128)])
        kcol0 = 0 if sample else 128
        bk_ = group(cx, win_o, "WOi", 1536, nt)
        P.act(lambda e: e.copy(out=B.kT[:, kcol0:kcol0 + nt], in_=ps[bk_][:, 0:nt]), reads=[f"ps{bk_}"], writes=b.wK)
        for t in range(ntile):
            bkv = nb()
            for kc in range(8):
                P.pe(lambda e, kc=kc, t=t, bkv=bkv: e.matmul(ps[bkv][:, 0:256], lhsT=cx.hT3[:, kc, t * 128:(t + 1) * 128], rhs=win_o[:, kc, 1536:1792],
                                                            start=(kc == 0), stop=(kc == 7)), reads=["WOi", cx.khc(kc)], writes=[f"ps{bkv}"])
            vslot = t if sample else 1 + t
            P.act(lambda e, bkv=bkv, vslot=vslot: e.copy(out=B.vt[:, vslot, :], in_=ps[bkv][:, 128:256]), reads=[f"ps{bkv}"], writes=b.wV)
            if sample or (b.last_p and t == ntile - 1):
                s = stage_slot()
                P.act(lambda e, bkv=bkv, s=s: e.copy(out=xst[s][:, 0:256], in_=ps[bkv][:, 0:256]), reads=[f"ps{bkv}"], writes=[f"xst{s}"])
                dk, dv = (sk_o[120 * 16:128 * 16, :], sv_o[120 * 16:128 * 16, :]) if sample else (pk_o[:, :], pv_o[:, :])
                P.dma("sp", f"xst{s}", lambda e, s=s, dk=dk: e.dma_start(out=dk, in_=xst[s][:, 0:128]), reads=[f"xst{s}"], final=True)
                P.dma("sp", f"xst{s}", lambda e, s=s, dv=dv: e.dma_start(out=dv, in_=xst[s][:, 128:256]), reads=[f"xst{s}"], final=True)
        yield
        if b.halo:
            return
        kq = [cx.sk(4), cx.sk(5)]
        kg = [cx.sk(2), cx.sk(3)]
        for r in range(4):
            bq_ = group(cx, win_o, "WOi", 1024 + r * 128, nt)
            P.act(lambda e, bq_=bq_, r=r: e.copy(out=qT3[:, r, 0:nt], in_=ps[bq_][:, 0:nt]), reads=[f"ps{bq_}"], writes=kq)
            bd_ = group(cx, win_o, "WOi", 1792 + r * 128, nt)
            P.act(lambda e, bd_=bd_, r=r: e.activation(out=sgd3[:, r, 0:nt], in_=ps[bd_][:, 0:nt], func=AF.Silu), reads=[f"ps{bd_}"], writes=kg)
            try_wout(cx, 1, st)
            yield
        bias4 = biasT[:].rearrange("p (b h q) -> p b h q", b=2, h=8)
        PT4 = PT[:].rearrange("p (b h q) -> p b h q", b=2, h=8)
        for t in range(ntile):
            q0 = t * 128
            for kb in range(2):
                for g in range(2):
                    bs = nb()
                    P.pe(lambda e, kb=kb, g=g, bs=bs: e.matmul(ps[bs][:, :], lhsT=identb[:], rhs=bias4[:, kb, 4 * g:4 * g + 4, :], start=True, stop=False),
                         reads=["identb", "biasT"], writes=[f"ps{bs}"])
                    if sample and kb == 1:
                        for s_ in range(16):
                            P.pe(lambda e, g=g, bs=bs, s_=s_: e.matmul(ps[bs][:].rearrange("p (r i s) -> p r i s", r=4, s=16)[:, :, :, s_],
                                                                      lhsT=kTc[g * 64:(g + 1) * 64, s_, :],
                                                                      rhs=qT3[g * 64:(g + 1) * 64, :, 0:128].rearrange("p r (i s) -> p r i s", s=16)[:, :, :, s_],
                                                                      start=False, stop=(s_ == 15)), reads=["kTc"] + kq, writes=[f"ps{bs}"])
                    else:
                        kc0 = (kcol0 + q0) if kb == 0 else (kcol0 + q0 - 128)
                        for r in range(4):
                            P.pe(lambda e, g=g, r=r, bs=bs, kc0=kc0, q0=q0: e.matmul(ps[bs][:, r * 128:(r + 1) * 128], lhsT=B.kT[g * 64:(g + 1) * 64, kc0:kc0 + 128],
                                                                                    rhs=qT3[g * 64:(g + 1) * 64, r, q0:q0 + 128], start=False, stop=(r == 3)),
                                 reads=[b.kK] + kq, writes=[f"ps{bs}"])
                    P.act(lambda e, kb=kb, g=g, bs=bs: e.activation(out=PT4[:, kb, 4 * g:4 * g + 4, :], in_=ps[bs][:].rearrange("p (h q) -> p h q", h=4),
                                                                    func=AF.Exp, scale=0.125), reads=[f"ps{bs}"], writes=[f"PT{kb}"])
                if kb == 1 and b.use_hm and t == 0:
                    P.act(lambda e: e.activation(out=PT[:, 1024:2048], in_=PT[:, 1024:2048], func=AF.Copy, scale=sm("hm")),
                          reads=["PT1", "small"], writes=["PT1"])
                try_wout(cx, 1, st)
                yield
            bnum, bden = nb(), nb()
            for (bo, isden) in ((bnum, False), (bden, True)):
                for g in range(2):
                    vcur = B.vt[:, (t if sample else 1 + t), g * 64:(g + 1) * 64]
                    P.pe(lambda e, g=g, bo=bo, isden=isden, vcur=vcur: e.matmul(ps[bo][g * 64:(g + 1) * 64, :], lhsT=(onesb[:, 0:64] if isden else vcur),
                                                                              rhs=PT4[:, 0, 4 * g:4 * g + 4, :], start=True, stop=False),
                         reads=[b.kV, "PT0", "onesb"], writes=[f"ps{bo}"])
                    if sample:
                        for s_ in range(16):
                            P.pe(lambda e, g=g, bo=bo, isden=isden, s_=s_: e.matmul(
                                ps[bo][g * 64:(g + 1) * 64, :].rearrange("p (r i s) -> p r i s", r=4, s=16)[:, :, :, s_],
                                lhsT=(onesb[:, 0:64] if isden else Vc[:, s_, g * 64:(g + 1) * 64]),
                                rhs=PT4[:, 1, 4 * g:4 * g + 4, :].rearrange("p r (i s) -> p r i s", s=16)[:, :, :, s_],
                                start=False, stop=(s_ == 15)), reads=["Vc", "PT1", "onesb"], writes=[f"ps{bo}"])
                    else:
                        vprev = B.vt[:, t, g * 64:(g + 1) * 64]
                        P.pe(lambda e, g=g, bo=bo, isden=isden, vprev=vprev: e.matmul(ps[bo][g * 64:(g + 1) * 64, :], lhsT=(onesb[:, 0:64] if isden else vprev),
                                                                                    rhs=PT4[:, 1, 4 * g:4 * g + 4, :], start=False, stop=True),
                             reads=[b.kV, "PT1", "onesb"], writes=[f"ps{bo}"])
            yield
            kd = [cx.sk(0), cx.sk(1)]
            P.dve(lambda e, bden=bden: e.tensor_tensor(out=dn.rearrange("p (r q) -> p r q", r=4), in0=ps[bden][:].rearrange("p (r q) -> p r q", r=4),
                                                       in1=esk[:].unsqueeze(2).broadcast_to([128, 4, 128]), op=ALU.add), reads=[f"ps{bden}", "esk"], writes=kd)
            P.dve(lambda e: e.reciprocal(out=dn, in_=dn), reads=kd, writes=kd)
            P.dve(lambda e, bnum=bnum: e.tensor_tensor(out=dn, in0=ps[bnum][:], in1=dn, op=ALU.mult), reads=[f"ps{bnum}"] + kd, writes=kd)
            P.dve(lambda e, q0=q0: e.tensor_tensor(out=cx.sq3[:, 4:8, q0:q0 + 128], in0=dn.rearrange("p (r q) -> p r q", r=4), in1=sgd3[:, :, q0:q0 + 128], op=ALU.mult),
                  reads=kd + kg, writes=[cx.ksc(4 + r_) for r_ in range(4)])
            yield
        yield "hold"
        yield from out_proj(cx, 1, nt, sample, par, st)
        for t in range(ntile):
            if sample:
                store_y(ys_o[:, :], par, 0)
            else:
                r0 = (bi - 1) * BT + t * 128
                store_y(yp_o[r0:r0 + 128, :], par, t * 128)
            yield

    def run(g):
        for _ in g:
            pass

    def interleave(ga, gb):
        da = db = False
        while not (da and db):
            if not da:
                try:
                    next(ga)
                except StopIteration:
                    da = True
            if not db:
                try:
                    next(gb)
                except StopIteration:
                    db = True

    run(gen_L0(0))
    run(gen_L0(1))
    doneL0, doneL1 = {0, 1}, set()
    a, bq = 2, 0
    gA = gB = None
    a_tail = False
    b_hold = False
    while len(doneL1) < NBLK_P + 1:
        if gA is None and a < NBLK_P + 1 and ((a - NXT) < 0 or (a - NXT) in doneL1):
            gA = gen_L0(a)
            a_tail = False
        if gB is None and bq < NBLK_P + 1 and bq in doneL0:
            gB = gen_L1(bq)
            b_hold = False
        if gA is not None:
            try:
                if next(gA) == "tail":
                    a_tail = True
            except StopIteration:
                doneL0.add(a); a += 1; gA = None
        if b_hold and (a_tail or gA is None):
            b_hold = False
        if gB is not None and not b_hold:
            try:
                if next(gB) == "hold" and gA is not None and not a_tail:
                    b_hold = True
            except StopIteration:
                doneL1.add(bq); bq += 1; gB = None
    P.emit(nc)
    es.close()
    return nc


def _wl(w):
    n = w.shape[1]
    return np.ascontiguousarray(w.reshape(8, 128, n).transpose(1, 0, 2).reshape(128, 8 * n))


def _vl(v, nch):
    return np.ascontiguousarray(v.reshape(nch, 128).T)


def _bias_tables():
    k = np.arange(128)[:, None]
    q = np.arange(128)[None, :]
    slopes = 2.0 ** (-(np.arange(8) + 1.0))
    bp = np.full((128, 2, 8, 128), NEG, np.float32)
    bs = np.full((128, 2, 8, 128), NEG, np.float32)
    qi, qs = q // 16, q % 16
    ki, ks = k // 16, k % 16
    for h in range(8):
        sl = 8.0 * slopes[h]
        bp[:, 0, h, :] = np.where(q >= k, -sl * (q - k), NEG)
        bp[:, 1, h, :] = np.where(k > q, -sl * (q + 128 - k), NEG)
        bs[:, 0, h, :] = np.where((qs == ks) & (ki <= qi), -sl * (qi - ki), NEG)
        bs[:, 1, h, :] = np.where(k > qi, -sl * (128 + qi - k), NEG)
    return (bp.reshape(128, 2048).astype(ml_dtypes.bfloat16), bs.reshape(128, 2048).astype(ml_dtypes.bfloat16))


_NC_CACHE = {}


def kernel(x_prompt, x_sample, state_conv_a, state_conv_b, state_pool_c, cache_win_k, cache_win_v,
           c_prompt, c_sample, w_mod_e, b_mod_e, g_pre_e, g_post_e, w_in_e, conv_a_w, conv_b_w, conv_b_b,
           ln_b_g, ln_b_b, w_out_e, w_mod_o, b_mod_o, g_pre_o, g_post_o, w_in_o, pool_w, pool_scale,
           sinks, w_out_o):
    f32 = np.float32
    A = lambda a: np.asarray(a, dtype=f32)
    x_prompt, x_sample = A(x_prompt), A(x_sample)
    hp = np.array([(g * 4 + r) * 64 + d for r in range(4) for g in range(2) for d in range(64)])
    cols = np.concatenate([np.arange(0, 1024), 1024 + hp, np.arange(1536, 1792), 1792 + hp])
    wino = A(w_in_o)[0][:, cols]
    rows = np.concatenate([np.arange(0, 512), 512 + hp])
    wouto = A(w_out_o)[0][rows, :]
    shared = {
        "wmod_e": _wl(A(w_mod_e)[0]), "wmod_o": _wl(A(w_mod_o)[0]),
        "win_e": _wl(A(w_in_e)[0]), "wout_e": _wl(A(w_out_e)[0]),
        "win_o": _wl(wino), "wout_o": _wl(wouto),
        "identf": np.eye(128, dtype=f32),
        "poolw": np.ascontiguousarray(A(pool_w)[0].transpose(1, 0, 2).reshape(128, 512)),
    }
    shared["biasp"], shared["biass"] = _bias_tables()
    sm_base = np.zeros((128, NSM), f32)

    def put(name, arr):
        o, w = SM[name]
        sm_base[:, o:o + w] = arr

    put("bmod_e", _vl(A(b_mod_e)[0], 24)); put("bmod_o", _vl(A(b_mod_o)[0], 24))
    put("gpre_e", _vl(A(g_pre_e)[0], 8)); put("gpost_e", _vl(A(g_post_e)[0], 8))
    put("gpre_o", _vl(A(g_pre_o)[0], 8)); put("gpost_o", _vl(A(g_post_o)[0], 8))
    put("caw", A(conv_a_w)[0].reshape(3, 4, 128).transpose(2, 1, 0).reshape(128, 12))
    put("cbw", A(conv_b_w)[0].reshape(31, 4, 128).transpose(2, 1, 0).reshape(128, 124))
    put("cbb", _vl(A(conv_b_b)[0], 4)); put("lng", _vl(A(ln_b_g)[0], 4)); put("lnb", _vl(A(ln_b_b)[0], 4))
    put("psc", _vl(A(pool_scale)[0], 4))
    put("snk", np.repeat(A(sinks)[0].reshape(2, 4), 64, axis=0))
    fac = np.zeros((4, 16), f32)
    for g, w in enumerate(POOL_W):
        for t in range(16):
            fac[g, t] = w / min(w, t + 1) - 1.0
    in_maps = []
    for c in range(NCORES):
        b, hf = c // 2, c % 2
        xp = np.zeros((HALO + 2048, 1024), f32)
        if hf == 1:
            xp[:] = x_prompt[b, 2048 - HALO:4096]
        else:
            xp[HALO:] = x_prompt[b, 0:2048]
        sl = slice(16 * c, 16 * c + 16)
        xs = np.ascontiguousarray(x_sample[sl].transpose(1, 0, 2).reshape(128, 1024))
        call = np.concatenate([A(c_prompt)[b:b + 1], A(c_sample)[sl]], axis=0)
        cT = np.ascontiguousarray(call.reshape(17, 8, 128).transpose(2, 1, 0).reshape(128, 136))
        smc = sm_base.copy()
        o, w = SM["hm"]; smc[:, o] = float(hf)
        o, w = SM["facm1"]; smc[:, o:o + w] = (fac.reshape(1, 64) if hf == 0 else 0.0)
        m = dict(shared)
        m.update({
            "xp": xp, "xs": xs, "cT": cT, "small": smc,
            "sa": np.ascontiguousarray(A(state_conv_a)[0, sl].transpose(1, 0, 2).reshape(32, 512)),
            "sb": np.ascontiguousarray(A(state_conv_b)[0, sl].transpose(1, 0, 2).reshape(480, 512)),
            "sc": np.ascontiguousarray(A(state_pool_c)[0, sl].transpose(1, 0, 2).reshape(240, 512)),
            "ck": np.ascontiguousarray(A(cache_win_k)[0, sl].reshape(16, 128, 128).transpose(1, 0, 2).reshape(128, 2048)),
            "cv": np.ascontiguousarray(A(cache_win_v)[0, sl].reshape(16, 128, 128).transpose(1, 0, 2).reshape(128, 2048)),
        })
        in_maps.append(m)
    if "nc" not in _NC_CACHE:
        _NC_CACHE["nc"] = build_program()
    res = run_bass_kernel_spmd(_NC_CACHE["nc"], in_maps, core_ids=list(range(NCORES)))
    R = res.results
    y_prompt = np.zeros((4, 4096, 1024), f32); y_sample = np.zeros((128, 8, 1024), f32)
    pa = np.zeros((1, 4, 2, 512), f32); sa = np.zeros((1, 128, 2, 512), f32)
    pb = np.zeros((1, 4, 30, 512), f32); sbo = np.zeros((1, 128, 30, 512), f32)
    pc = np.zeros((1, 4, 15, 512), f32); sco = np.zeros((1, 128, 15, 512), f32)
    pk = np.zeros((1, 4, 128, 2, 64), f32); sk = np.zeros((1, 128, 128, 2, 64), f32)
    pv = np.zeros((1, 4, 128, 2, 64), f32); sv = np.zeros((1, 128, 128, 2, 64), f32)
    for c in range(NCORES):
        b, hf = c // 2, c % 2
        r = R[c]
        sl = slice(16 * c, 16 * c + 16)
        y_prompt[b, hf * 2048:(hf + 1) * 2048] = r["yp"]
        y_sample[sl] = r["ys"].reshape(8, 16, 1024).transpose(1, 0, 2)
        sa[0, sl] = r["sa_o"].reshape(2, 16, 512).transpose(1, 0, 2)
        sbo[0, sl] = r["sb_o"].reshape(30, 16, 512).transpose(1, 0, 2)
        sco[0, sl] = r["sc_o"].reshape(15, 16, 512).transpose(1, 0, 2)
        sk[0, sl] = r["sk_o"].reshape(128, 16, 2, 64).transpose(1, 0, 2, 3)
        sv[0, sl] = r["sv_o"].reshape(128, 16, 2, 64).transpose(1, 0, 2, 3)
        if hf == 1:
            pa[0, b] = r["pa"]; pb[0, b] = r["pb"]; pc[0, b] = r["pc"]
            pk[0, b] = r["pk"].reshape(128, 2, 64); pv[0, b] = r["pv"].reshape(128, 2, 64)
    return (y_prompt, y_sample, pa, sa, pb, sbo, pc, sco, pk, sk, pv, sv)
```
